# Optimizing a Trainium2 kernel written in Bass

```python
import math
import jax
import jax.numpy as jnp
from jax import lax
import numpy as np

D_MODEL = 1024
BATCH = 4
SEQ = 4096
DEPTH = 2

CTX_LEN = 256
GRID_W = 64
D_MIX = 2 * D_MODEL
W_BR = D_MIX // 4
NORM_EPS = 1e-6

HY_ORDER = 2
HY_EMB = 33
HY_BANDS = (HY_EMB - 1) // 2
HY_HID = 64
HY_SHORT = 3
HY_TARGET = 1e-2
HY_MIN_DECAY = math.log(HY_TARGET) / 1.5
HY_MAX_DECAY = math.log(HY_TARGET) / 0.3
HY_EPS = 1e-6

RG_HEADS = 8
RG_HD = W_BR // RG_HEADS
RG_C = 8.0
RG_CONV = 4

HG_DK = 128
HG_HEADS = W_BR // HG_DK
HG_CHUNK = 64

M2_HD = 64
M2_HEADS = W_BR // M2_HD
M2_GROUPS = 2
M2_STATE = 128
M2_CONV = 4
M2_CHUNK = 64
M2_XBC = W_BR + 2 * M2_GROUPS * M2_STATE

PIECES = (3 * W_BR, W_BR, W_BR, W_BR, W_BR, W_BR, W_BR, W_BR, W_BR, M2_XBC, M2_HEADS, W_BR)
IN_COLS = sum(PIECES)
SPLIT_IDX = tuple(int(v) for v in np.cumsum(PIECES)[:-1])

kernel_name = 'hybrid_hyena_rglru_hgrn2_ssd_prefix'


def rmsnorm(x, w):
    xf = x.astype(jnp.float32)
    y = xf * lax.rsqrt(jnp.mean(xf * xf, axis=-1, keepdims=True) + NORM_EPS)
    return (y * w.astype(jnp.float32)).astype(x.dtype)


def dwconv(x, w, b, pad_lo, pad_hi):
    y = lax.conv_general_dilated(x, w.astype(x.dtype)[:, None, :], window_strides=(1,),
                                 padding=[(pad_lo, pad_hi)],
                                 dimension_numbers=('NWC', 'WIO', 'NWC'),
                                 feature_group_count=x.shape[-1])
    return y + b.astype(x.dtype)


def to_colmajor(a, rows):
    b, n, ch = a.shape
    return a.reshape(b, rows, GRID_W, ch).transpose(0, 2, 1, 3).reshape(b, n, ch)


def from_colmajor(a, rows):
    b, n, ch = a.shape
    return a.reshape(b, GRID_W, rows, ch).transpose(0, 2, 1, 3).reshape(b, n, ch)


def bidir(dir_fn, ctx_fwd, lat_fwd, ctx_bwd, lat_bwd, state0, p_fwd, p_bwd):
    rev = lambda t: tuple(jnp.flip(a, 1) for a in t)
    yc_f, s_f = dir_fn(*ctx_fwd, state0, *p_fwd)
    yl_f, _ = dir_fn(*lat_fwd, s_f, *p_fwd)
    yc_b, s_b = dir_fn(*rev(ctx_bwd), state0, *p_bwd)
    yl_b, _ = dir_fn(*rev(lat_bwd), s_b, *p_bwd)
    return yc_f + jnp.flip(yc_b, 1), yl_f + jnp.flip(yl_b, 1)


def hyena_filters(n, w1, b1, w2, b2, w3, freq):
    f32 = jnp.float32
    t = jnp.linspace(0.0, 1.0, n, dtype=f32)[:, None]
    bands = jnp.linspace(1e-4, HY_BANDS - 1, HY_BANDS, dtype=f32)[None]
    ang = (2.0 * math.pi / n) * jnp.arange(n, dtype=f32)[:, None] * bands
    z = jnp.concatenate([t, jnp.cos(ang), -jnp.sin(ang)], axis=-1)
    fr = freq.astype(f32)
    h = jnp.sin(fr * (z @ w1.astype(f32) + b1.astype(f32)))
    h = jnp.sin(fr * (h @ w2.astype(f32) + b2.astype(f32)))
    h = (h @ w3.astype(f32)).reshape(n, HY_ORDER, 2, W_BR)
    deltas = jnp.abs(jnp.linspace(HY_MIN_DECAY, HY_MAX_DECAY, W_BR, dtype=f32))
    return h * jnp.exp(-t[:, :, None, None] * deltas)


def two_sided_kernel(h_fwd, h_bwd):
    k = jnp.concatenate([h_fwd, jnp.zeros_like(h_fwd[:1]), jnp.flip(h_bwd[1:], 0)], axis=0)
    return k / (jnp.sum(jnp.abs(k), axis=0, keepdims=True) + HY_EPS)


def fft_conv(u, k):
    n = u.shape[1]
    y = jnp.fft.irfft(jnp.fft.rfft(u, n=2 * n, axis=1) * jnp.fft.rfft(k, axis=0), n=2 * n, axis=1)
    return y[:, :n]


def hyena_mix(u, conv_w, conv_b, w1, b1, w2, b2, w3, freq, skip):
    n = u.shape[1]
    u = dwconv(u, conv_w, conv_b, HY_SHORT // 2, HY_SHORT // 2).astype(jnp.float32)
    v, x1, x2 = jnp.split(u, 3, axis=-1)
    h = hyena_filters(n, w1, b1, w2, b2, w3, freq)
    z = v
    for o, xg in enumerate((x1, x2)):
        k = two_sided_kernel(h[:, o, 0], h[:, o, 1])
        z = xg * (fft_conv(z, k) + skip[o].astype(jnp.float32) * z)
    return z


def _lin_combine(e1, e2):
    a1, b1 = e1
    a2, b2 = e2
    return a1 * a2, a2 * b1 + b2


def rglru_dir(x, h0, conv_w, conv_b, wa, ba, wx, bx, lam):
    f32 = jnp.float32
    bn, n, _ = x.shape
    xc = dwconv(x, conv_w, conv_b, RG_CONV - 1, 0).astype(f32)
    xh = xc.reshape(bn, n, RG_HEADS, RG_HD)
    r = jax.nn.sigmoid(jnp.einsum('blhi,hij->blhj', xh, wa.astype(f32)).reshape(bn, n, W_BR) + ba.astype(f32))
    gi = jax.nn.sigmoid(jnp.einsum('blhi,hij->blhj', xh, wx.astype(f32)).reshape(bn, n, W_BR) + bx.astype(f32))
    log_a = -RG_C * r * jax.nn.softplus(-lam.astype(f32))
    a = jnp.exp(log_a)
    b = jnp.sqrt(-jnp.expm1(2.0 * log_a)) * (gi * xc)
    a_cum, h = lax.associative_scan(_lin_combine, (a, b), axis=1)
    h = h + a_cum * h0[:, None]
    return h, h[:, -1]


def gla_chunked(q, k, v, log_f, s0):
    bn, n, nh, _ = q.shape
    dv = v.shape[-1]
    nc = n // HG_CHUNK
    to_chunks = lambda a: a.reshape(bn, nc, HG_CHUNK, nh, a.shape[-1]).transpose(1, 0, 3, 2, 4)
    mask = jnp.tril(jnp.ones((HG_CHUNK, HG_CHUNK), dtype=bool))[:, :, None]

    def step(s, inp):
        qc, kc, vc, lf = inp
        g = jnp.cumsum(lf, axis=2)
        o_inter = jnp.einsum('bhtd,bhde->bhte', qc * jnp.exp(g), s)
        diff = g[:, :, :, None, :] - g[:, :, None, :, :]
        decay = jnp.exp(jnp.where(mask, diff, -jnp.inf))
        att = jnp.einsum('bhtd,bhsd,bhtsd->bhts', qc, kc, decay)
        o = o_inter + jnp.einsum('bhts,bhse->bhte', att, vc)
        g_last = g[:, :, -1]
        s = jnp.exp(g_last)[..., None] * s + jnp.einsum('bhsd,bhse->bhde', kc * jnp.exp(g_last[:, :, None] - g), vc)
        return s, o

    s_last, o = lax.scan(step, s0, (to_chunks(q), to_chunks(k), to_chunks(v), to_chunks(log_f)))
    return o.transpose(1, 0, 3, 2, 4).reshape(bn, n, nh, dv), s_last


def hgrn2_dir(q, f_logit, v, s0, lb):
    f32 = jnp.float32
    bn, n, _ = q.shape
    lb = lb.astype(f32)
    z = f_logit.astype(f32)
    f = lb + (1.0 - lb) * jax.nn.sigmoid(z)
    k = (1.0 - lb) * jax.nn.sigmoid(-z)
    heads = lambda a: a.reshape(bn, n, HG_HEADS, HG_DK)
    return gla_chunked(heads(q.astype(f32) * HG_DK ** -0.5), heads(k), heads(v.astype(f32)),
                       heads(jnp.log(f)), s0)


def segsum(a):
    cs = jnp.cumsum(a, axis=-1)
    t = a.shape[-1]
    mask = jnp.tril(jnp.ones((t, t), dtype=bool))
    return jnp.where(mask, cs[..., :, None] - cs[..., None, :], -jnp.inf)


def ssd_chunked(xdt, adt, bm, cm, h0):
    bn, n, nh, hp = xdt.shape
    nc = n // M2_CHUNK
    X = xdt.reshape(bn, nc, M2_CHUNK, nh, hp)
    Bc = bm.reshape(bn, nc, M2_CHUNK, nh, M2_STATE)
    Cc = cm.reshape(bn, nc, M2_CHUNK, nh, M2_STATE)
    A = adt.reshape(bn, nc, M2_CHUNK, nh).transpose(0, 3, 1, 2)
    A_cum = jnp.cumsum(A, axis=-1)
    y_diag = jnp.einsum('bclhn,bcshn,bhcls,bcshp->bclhp', Cc, Bc, jnp.exp(segsum(A)), X)
    decay_states = jnp.exp(A_cum[..., -1:] - A_cum)
    states = jnp.einsum('bclhn,bhcl,bclhp->bchpn', Bc, decay_states, X)
    states = jnp.concatenate([h0[:, None], states], axis=1)
    chunk_decay = jnp.exp(segsum(jnp.pad(A_cum[..., -1], ((0, 0), (0, 0), (1, 0)))))
    states = jnp.einsum('bhzc,bchpn->bzhpn', chunk_decay, states)
    y_off = jnp.einsum('bclhn,bchpn,bhcl->bclhp', Cc, states[:, :-1], jnp.exp(A_cum))
    return (y_diag + y_off).reshape(bn, n, nh, hp), states[:, -1]


def ssd_dir(xbc, dt_raw, h0, conv_w, conv_b, dt_bias, a_log, d_skip):
    f32 = jnp.float32
    bn, n, _ = xbc.shape
    xbc = jax.nn.silu(dwconv(xbc, conv_w, conv_b, M2_CONV - 1, 0).astype(f32))
    xs, bm, cm = jnp.split(xbc, [W_BR, W_BR + M2_GROUPS * M2_STATE], axis=-1)
    xs = xs.reshape(bn, n, M2_HEADS, M2_HD)
    rep = M2_HEADS // M2_GROUPS
    bm = jnp.repeat(bm.reshape(bn, n, M2_GROUPS, M2_STATE), rep, axis=2)
    cm = jnp.repeat(cm.reshape(bn, n, M2_GROUPS, M2_STATE), rep, axis=2)
    dt = jax.nn.softplus(dt_raw.astype(f32) + dt_bias.astype(f32))
    a = -jnp.exp(a_log.astype(f32))
    y, h_last = ssd_chunked(xs * dt[..., None], dt * a, bm, cm, h0)
    return y + d_skip.astype(f32)[:, None] * xs, h_last


def head_rmsnorm(o, w):
    o = o * lax.rsqrt(jnp.mean(o * o, axis=-1, keepdims=True) + NORM_EPS)
    return o.reshape(o.shape[0], o.shape[1], -1) * w.astype(jnp.float32)


def merge_branches(u, y_hy, y_rg, y_hg, y_m2, hg_norm_w, m2_norm_w, w_out):
    f32 = jnp.float32
    bn, n = u[0].shape[:2]
    gate = lambda g: jax.nn.silu(g.astype(f32))
    b_hy = y_hy * gate(u[1])
    b_rg = y_rg * gate(u[3])
    b_hg = head_rmsnorm(y_hg, hg_norm_w) * gate(u[8])
    m2 = (y_m2.reshape(bn, n, W_BR) * gate(u[11])).reshape(bn, n, M2_GROUPS, W_BR // M2_GROUPS)
    b_m2 = (m2 * lax.rsqrt(jnp.mean(m2 * m2, axis=-1, keepdims=True) + NORM_EPS)).reshape(bn, n, W_BR) * m2_norm_w.astype(f32)
    y = jnp.concatenate([b_hy, b_rg, b_hg, b_m2], axis=-1).astype(u[0].dtype)
    return y @ w_out


def setup_inputs(seed: int = 0) -> dict:
    key = jax.random.key(seed)
    ks = iter(jax.random.split(key, 48))
    f32 = jnp.float32
    nrm = lambda shape, scale: scale * jax.random.normal(next(ks), shape, f32)
    D, W, X, L = D_MODEL, W_BR, M2_XBC, DEPTH
    ac = jax.random.uniform(next(ks), (L, 2, W), f32, 0.9, 0.999)
    a = ac ** (1.0 / RG_C)
    rg_lam = jnp.log(a) - jnp.log1p(-a)
    dt = jnp.exp(jax.random.uniform(next(ks), (L, 2, M2_HEADS), f32, math.log(1e-3), math.log(1e-1)))
    m2_dt_bias = dt + jnp.log(-jnp.expm1(-dt))
    m2_a_log = jnp.log(jax.random.uniform(next(ks), (L, 2, M2_HEADS), f32, 1.0, 16.0))
    return {
        'x': nrm((BATCH, SEQ, D), 1.0),
        'c': nrm((BATCH, D), 1.0),
        'ctx': nrm((BATCH, CTX_LEN, D), 1.0),
        'c_ctx': nrm((D,), 1.0),
        'w_mod': nrm((L, D, 3 * D), 0.5 * D ** -0.5),
        'b_mod': nrm((L, 3 * D), 0.02),
        'norm_w': 1.0 + nrm((L, D), 0.02),
        'w_in': nrm((L, D, IN_COLS), D ** -0.5),
        'w_out': nrm((L, D_MIX, D), D_MIX ** -0.5),
        'hy_conv_w': nrm((L, HY_SHORT, 3 * W), HY_SHORT ** -0.5),
        'hy_conv_b': nrm((L, 3 * W), 0.02),
        'hy_w1': nrm((L, HY_EMB, HY_HID), HY_EMB ** -0.5),
        'hy_b1': nrm((L, HY_HID), 0.1),
        'hy_w2': nrm((L, HY_HID, HY_HID), HY_HID ** -0.5),
        'hy_b2': nrm((L, HY_HID), 0.1),
        'hy_w3': nrm((L, HY_HID, HY_ORDER * 2 * W), HY_HID ** -0.5),
        'hy_freq': 1.0 + nrm((L, HY_HID), 0.1),
        'hy_skip': nrm((L, HY_ORDER, W), 1.0),
        'rg_conv_w': nrm((L, 2, RG_CONV, W), RG_CONV ** -0.5),
        'rg_conv_b': nrm((L, 2, W), 0.02),
        'rg_wa': nrm((L, 2, RG_HEADS, RG_HD, RG_HD), RG_HD ** -0.5),
        'rg_ba': nrm((L, 2, W), 0.1),
        'rg_wx': nrm((L, 2, RG_HEADS, RG_HD, RG_HD), RG_HD ** -0.5),
        'rg_bx': nrm((L, 2, W), 0.1),
        'rg_lam': rg_lam,
        'hg_lb': nrm((L, 2, W), 1.0),
        'hg_norm_w': 1.0 + nrm((L, W), 0.02),
        'm2_conv_w': nrm((L, 2, M2_CONV, X), M2_CONV ** -0.5),
        'm2_conv_b': nrm((L, 2, X), 0.02),
        'm2_dt_bias': m2_dt_bias,
        'm2_a_log': m2_a_log,
        'm2_d': 1.0 + nrm((L, 2, M2_HEADS), 0.1),
        'm2_norm_w': 1.0 + nrm((L, W), 0.02),
        'final_norm_w': 1.0 + nrm((D,), 0.02),
    }


def reference(x, c, ctx, c_ctx, w_mod, b_mod, norm_w, w_in, w_out,
              hy_conv_w, hy_conv_b, hy_w1, hy_b1, hy_w2, hy_b2, hy_w3, hy_freq, hy_skip,
              rg_conv_w, rg_conv_b, rg_wa, rg_ba, rg_wx, rg_bx, rg_lam,
              hg_lb, hg_norm_w,
              m2_conv_w, m2_conv_b, m2_dt_bias, m2_a_log, m2_d, m2_norm_w,
              final_norm_w):
    f32 = jnp.float32
    bsz, n_lat, _ = x.shape
    rows = n_lat // GRID_W
    lb_all = jnp.cumsum(jax.nn.softmax(hg_lb.astype(f32), axis=0), axis=0)
    lb_all = lb_all - lb_all[:1]
    cond_l = jax.nn.silu(c)
    cond_c = jax.nn.silu(c_ctx)
    rg0 = jnp.zeros((bsz, W_BR), f32)
    hg0 = jnp.zeros((bsz, HG_HEADS, HG_DK, HG_DK), f32)
    m20 = jnp.zeros((bsz, M2_HEADS, M2_HD, M2_STATE), f32)
    xl, xc = x, ctx
    for i in range(DEPTH):
        need_ctx = i < DEPTH - 1
        sh_l, sc_l, g_l = jnp.split(cond_l @ w_mod[i] + b_mod[i], 3, axis=-1)
        sh_c, sc_c, g_c = jnp.split(cond_c @ w_mod[i] + b_mod[i], 3, axis=-1)
        hl = rmsnorm(xl, norm_w[i]) * (1.0 + sc_l[:, None]) + sh_l[:, None]
        hc = rmsnorm(xc, norm_w[i]) * (1.0 + sc_c) + sh_c
        ul = jnp.split(hl @ w_in[i], SPLIT_IDX, axis=-1)
        uc = jnp.split(hc @ w_in[i], SPLIT_IDX, axis=-1)

        hy_p = (hy_conv_w[i], hy_conv_b[i], hy_w1[i], hy_b1[i], hy_w2[i], hy_b2[i], hy_w3[i], hy_freq[i], hy_skip[i])
        lat_hy = hyena_mix(ul[0], *hy_p)

        rg_f = (rg_conv_w[i, 0], rg_conv_b[i, 0], rg_wa[i, 0], rg_ba[i, 0], rg_wx[i, 0], rg_bx[i, 0], rg_lam[i, 0])
        rg_b = (rg_conv_w[i, 1], rg_conv_b[i, 1], rg_wa[i, 1], rg_ba[i, 1], rg_wx[i, 1], rg_bx[i, 1], rg_lam[i, 1])
        ctx_rg, lat_rg = bidir(rglru_dir, (uc[2],), (ul[2],), (uc[2],), (ul[2],), rg0, rg_f, rg_b)

        ctx_hg, lat_hg = bidir(hgrn2_dir, (uc[4], uc[5], uc[7]), (ul[4], ul[5], ul[7]),
                               (uc[4], uc[6], uc[7]), (ul[4], ul[6], ul[7]), hg0,
                               (lb_all[i, 0],), (lb_all[i, 1],))

        m2_f = (m2_conv_w[i, 0], m2_conv_b[i, 0], m2_dt_bias[i, 0], m2_a_log[i, 0], m2_d[i, 0])
        m2_b = (m2_conv_w[i, 1], m2_conv_b[i, 1], m2_dt_bias[i, 1], m2_a_log[i, 1], m2_d[i, 1])
        m2_ctx_in = (uc[9], uc[10])
        m2_lat_in = (to_colmajor(ul[9], rows), to_colmajor(ul[10], rows))
        ctx_m2, lat_m2 = bidir(ssd_dir, m2_ctx_in, m2_lat_in, m2_ctx_in, m2_lat_in, m20, m2_f, m2_b)
        lat_m2 = from_colmajor(lat_m2.reshape(bsz, n_lat, W_BR), rows)

        xl = xl + g_l[:, None] * merge_branches(ul, lat_hy, lat_rg, lat_hg, lat_m2,
                                                hg_norm_w[i], m2_norm_w[i], w_out[i])
        if need_ctx:
            ctx_hy = hyena_mix(uc[0], *hy_p)
            xc = xc + g_c * merge_branches(uc, ctx_hy, ctx_rg, ctx_hg, ctx_m2,
                                           hg_norm_w[i], m2_norm_w[i], w_out[i])
    return rmsnorm(xl, final_norm_w)
```

```python
import contextlib
import math
import numpy as np
import ml_dtypes
import concourse.bass as bass
import concourse.mybir as mybir
from concourse.bass_utils import run_bass_kernel_spmd

F32 = mybir.dt.float32
BF16 = mybir.dt.bfloat16
I32 = mybir.dt.int32
AF = mybir.ActivationFunctionType
ALU = mybir.AluOpType
AX = mybir.AxisListType

D = 1024
SEQ = 4096
CTX = 256
NT = SEQ + CTX
NTILE = NT // 128
DEPTH = 2
W_BR = 512
CH = 256
IN_COLS = 7176
NCOL = 3588
EPS = 1e-6
HY_HID = 64
HY_EMB = 33
PI = math.pi

ENGS = ("pe", "dve", "act", "pool", "sp")
NSLOT = 12
STQ = "pool"


class Prog:
    def __init__(self, nc, self_sync=True):
        self.nc = nc
        self.self_sync = self_sync
        self.stack = contextlib.ExitStack()
        self.sem = {e: self.stack.enter_context(nc.semaphore("s_" + e)) for e in ENGS}
        self.cnt = {e: 0 for e in ENGS}
        self.dq = ("sp", "pool", "act")
        self.slot_sem = {q: [self.stack.enter_context(nc.semaphore(f"d_{q}{i}")) for i in range(NSLOT)]
                         for q in self.dq}
        self.slot_val = {q: [0] * NSLOT for q in self.dq}
        self.slot_next = {q: 0 for q in self.dq}
        self.known = {e: {} for e in ENGS}
        self.semobj = {}
        for e in ENGS:
            self.semobj["s_" + e] = self.sem[e]
        for q in self.dq:
            for i in range(NSLOT):
                self.semobj[f"d_{q}{i}"] = self.slot_sem[q][i]
        self.last_w = {}
        self.readers = {}
        self.rec = {e: [] for e in ENGS}
        self.n_ops = 0
        self.cc_sem = self.stack.enter_context(nc.semaphore("d_cc"))
        self.semobj["d_cc"] = self.cc_sem
        self.cc_val = 0

    def collective(self, emit, r=(), w=()):
        r = tuple(r); w = tuple(w)
        waits = self._deps("pool", r, w)
        self.cc_val += 1
        ev = ("d_cc", self.cc_val, "pool")
        self._commit(ev, r, w)
        self.rec["pool"].append((waits, emit, self.cc_sem, 1))
        self.n_ops += 1

    def _commit(self, ev, r, w):
        for k in r:
            lst = self.readers.setdefault(k, {})
            lst[ev[0]] = max(lst.get(ev[0], (0,))[0], ev[1]), ev[2]
        for k in w:
            self.last_w[k] = ev
            self.readers[k] = {}

    def op(self, eng, emit, r=(), w=()):
        r = tuple(r); w = tuple(w)
        waits = self._deps(eng, r, w)
        self.cnt[eng] += 1
        ev = ("s_" + eng, self.cnt[eng], eng)
        self._commit(ev, r, w)
        self.rec[eng].append((waits, emit, self.sem[eng], 1))
        self.n_ops += 1

    def _deps(self, eng, r, w):
        deps = {}

        def add(sn, v, src):
            if src == eng and (eng == "pe" or not self.self_sync) and not sn.startswith("d_"):
                return
            if v > deps.get(sn, 0):
                deps[sn] = v
        for k in r:
            ev = self.last_w.get(k)
            if ev is not None:
                add(*ev)
        for k in w:
            ev = self.last_w.get(k)
            if ev is not None:
                add(*ev)
            for sn, (v, src) in self.readers.get(k, {}).items():
                add(sn, v, src)
        out = []
        kn = self.known[eng]
        for sn, v in deps.items():
            if kn.get(sn, 0) < v:
                kn[sn] = v
                out.append((sn, v))
        return out

    def dma(self, q, out, in_, r=(), w=(), **kw):
        r = tuple(r); w = tuple(w)
        s = self.slot_next[q]
        self.slot_next[q] = (s + 1) % NSLOT
        sn = f"d_{q}{s}"
        waits = self._deps(q, r, w)
        prev = self.slot_val[q][s]
        if prev > 0 and self.known[q].get(sn, 0) < prev:
            self.known[q][sn] = prev
            waits.append((sn, prev))
        self.slot_val[q][s] = prev + 16
        ev = (sn, prev + 16, q)
        self._commit(ev, r, w)
        self.rec[q].append((waits, (lambda e, o=out, i=in_, k=kw: e.dma_start(out=o, in_=i, **k)),
                            self.slot_sem[q][s], 16))
        self.n_ops += 1

    def flush(self):
        final = []
        for e in ENGS:
            if self.cnt[e] > 0:
                final.append(("s_" + e, self.cnt[e]))
        for q in self.dq:
            for i in range(NSLOT):
                if self.slot_val[q][i] > 0:
                    final.append((f"d_{q}{i}", self.slot_val[q][i]))
        if self.cc_val > 0:
            final.append(("d_cc", self.cc_val))
        recs = {}
        for e in ENGS:
            ws = []
            for sn, v in final:
                if sn == "s_" + e:
                    continue
                if self.known[e].get(sn, 0) < v:
                    self.known[e][sn] = v
                    ws.append((sn, v))
            recs[e] = (self.rec[e], ws)
            self.rec[e] = []
        semobj = self.semobj

        def body(lst, ws):
            def f(engine):
                for waits, emit, sem, inc in lst:
                    for sn, v in waits:
                        engine.wait_ge(semobj[sn], v)
                    emit(engine).then_inc(sem, inc)
                for sn, v in ws:
                    engine.wait_ge(semobj[sn], v)
            return f
        with self.nc.Block() as block:
            block.tensor(body(*recs["pe"]))
            block.vector(body(*recs["dve"]))
            block.scalar(body(*recs["act"]))
            block.gpsimd(body(*recs["pool"]))
            block.sync(body(*recs["sp"]))
        self.last_w = {}
        self.readers = {}

    def close(self):
        self.stack.close()


class Pack:
    def __init__(self):
        self.cols = []
        self.off = {}
        self.n = 0

    def add(self, name, arr):
        arr = np.asarray(arr, np.float32)
        assert arr.shape[0] == 128, (name, arr.shape)
        arr = arr.reshape(128, -1)
        self.off[name] = self.n
        self.cols.append(arr)
        self.n += arr.shape[1]

    def build(self):
        return np.ascontiguousarray(np.concatenate(self.cols, axis=1))


def fm(vec):
    vec = np.asarray(vec, np.float32)
    return np.ascontiguousarray(vec.reshape(-1, 128).T)


def fm64(vec):
    out = np.zeros((128, 1), np.float32)
    out[:64, 0] = vec
    return out


def rowb(vec):
    vec = np.asarray(vec, np.float32).reshape(1, -1)
    return np.ascontiguousarray(np.broadcast_to(vec, (128, vec.shape[1])))


P_HY, P_HYG, P_RGX, P_RGG, P_HGQ, P_HGFF, P_HGFB, P_HGI, P_HGG, P_M2X, P_M2DT, P_M2G = (
    0, 1536, 2048, 2560, 3072, 3584, 4096, 4608, 5120, 5632, 6656, 6664)
C_HYV, C_HYX1, C_HYX2, C_HYG, C_RGX, C_RGG, C_HGQ, C_HGFF, C_HGFB, C_HGI, C_HGG, C_M2XS, C_M2B, C_M2C, C_M2G, C_M2DT = (
    0, 256, 512, 768, 1024, 1280, 1536, 1792, 2048, 2304, 2560, 2816, 3072, 3200, 3328, 3584)


def core_cols(h):
    r = lambda a, n: list(range(a, a + n))
    idx = []
    idx += r(P_HY + h * 256, 256) + r(P_HY + 512 + h * 256, 256) + r(P_HY + 1024 + h * 256, 256)
    idx += r(P_HYG + h * 256, 256)
    idx += r(P_RGX + h * 256, 256) + r(P_RGG + h * 256, 256)
    idx += r(P_HGQ + h * 256, 256) + r(P_HGFF + h * 256, 256) + r(P_HGFB + h * 256, 256)
    idx += r(P_HGI + h * 256, 256) + r(P_HGG + h * 256, 256)
    idx += r(P_M2X + h * 256, 256) + r(P_M2X + 512 + h * 128, 128) + r(P_M2X + 768 + h * 128, 128)
    idx += r(P_M2G + h * 256, 256) + r(P_M2DT + h * 4, 4)
    assert len(idx) == NCOL
    return np.array(idx)


_CONST_CACHE = {}


def dft_tables(n):
    key = ("dft", n)
    if key not in _CONST_CACHE:
        a = (np.arange(n, dtype=np.float64) + 0.5)
        ang = (2.0 * np.pi / (2 * n)) * np.outer(a, a)
        nb = n // 128
        tbg, cq = min(4, nb), min(512, n)

        def tile(t):
            t = t.astype(ml_dtypes.bfloat16).reshape(nb // tbg, tbg, 128, n // cq, cq)
            return np.ascontiguousarray(t.transpose(0, 3, 2, 1, 4))
        _CONST_CACHE[key] = (tile(np.cos(ang)), tile(np.sin(ang)))
    return _CONST_CACHE[key]


def hy_zfeat(n):
    f32 = np.float32
    t = np.linspace(0.0, 1.0, n, dtype=f32)[:, None]
    bands = np.linspace(1e-4, 16 - 1, 16, dtype=f32)[None]
    ang = (f32(2.0 * math.pi / n) * np.arange(n, dtype=f32)[:, None]) * bands
    z = np.concatenate([t, np.cos(ang), -np.sin(ang)], axis=-1).astype(f32)
    zt = np.zeros((HY_EMB, n + 1), f32)
    zt[:, :n] = z.T
    return zt


def const_pack():
    pk = Pack()
    for n, tag in ((SEQ, "L"), (CTX, "C")):
        nb = n // 128
        l = np.arange(n, dtype=np.float64)
        t = np.linspace(0.0, 1.0, n, dtype=np.float32).astype(np.float64)
        tf = np.zeros(n + 1); tf[:n] = t; tf[n] = 0.0
        pk.add("negt_f" + tag, fm(-tf[0:n]))
        pk.add("negt_b" + tag, fm(-tf[1:n + 1]))
        th2 = np.pi * (l + 0.5) / (2 * n)
        pk.add("chalf" + tag, fm(np.cos(th2)))
        pk.add("shalf" + tag, fm(np.sin(th2)))
    return pk


def const_mats():
    f32 = np.float32
    m = {}
    m["ident"] = np.eye(128, dtype=f32)
    m["ones"] = np.ones((128, 128), f32)
    k = np.arange(128)
    m["tri_incl"] = (k[:, None] <= k[None, :]).astype(f32)
    m["tri_excl"] = (k[:, None] < k[None, :]).astype(f32)
    s32 = np.arange(128) % 32
    t32 = np.arange(32)
    m["hgmask_f"] = (t32[None, :] >= s32[:, None]).astype(np.int32)
    m["hgmask_b"] = (t32[None, :] <= s32[:, None]).astype(np.int32)
    m["m2mask_f"] = (k[None, :] >= k[:, None]).astype(f32)
    m["m2mask_b"] = (k[None, :] <= k[:, None]).astype(f32)
    sel = np.zeros((2, 2, 128), f32)
    sel[0, 0, :] = 1.0
    sel[1, 1, :] = 1.0
    m["sel"] = sel
    return m


def host_prep(inp, b, h):
    g = lambda k: np.asarray(inp[k], np.float32)
    d = {}
    d["x"] = np.ascontiguousarray(g("x")[b])
    d["ctx"] = np.ascontiguousarray(g("ctx")[b])
    d["w_mod"] = g("w_mod")
    d["w_out"] = g("w_out")
    cols = core_cols(h)
    d["w_in"] = np.ascontiguousarray(g("w_in")[:, :, cols])
    sl = slice(h * 256, (h + 1) * 256)
    pk = Pack()
    cv = np.stack([fm(g("c")[b]), fm(g("c_ctx"))], axis=-1)
    pk.add("cvec", cv)
    for l in range(DEPTH):
        L = str(l)
        pk.add("b_mod" + L, fm(g("b_mod")[l]))
        pk.add("norm_w" + L, fm(g("norm_w")[l]))
        cw = g("hy_conv_w")[l]
        cb = g("hy_conv_b")[l]
        hyc = np.concatenate([np.arange(o + h * 256, o + h * 256 + 256) for o in (0, 512, 1024)])
        pk.add("hy_cw" + L, np.stack([fm(cw[j, hyc]) for j in range(3)], axis=-1))
        pk.add("hy_cb" + L, fm(cb[hyc]))
        pk.add("hy_b1" + L, fm64(g("hy_b1")[l]))
        pk.add("hy_b2" + L, fm64(g("hy_b2")[l]))
        pk.add("hy_fr" + L, fm64(g("hy_freq")[l]))
        pk.add("hy_skip" + L, np.stack([fm(g("hy_skip")[l, o, sl]) for o in range(2)], axis=1))
        for dr in range(2):
            DD = L + str(dr)
            pk.add("rg_cw" + DD, np.stack([fm(g("rg_conv_w")[l, dr, j, sl]) for j in range(4)], axis=-1))
            pk.add("rg_cb" + DD, fm(g("rg_conv_b")[l, dr, sl]))
            pk.add("rg_ba" + DD, fm(g("rg_ba")[l, dr, sl]))
            pk.add("rg_bx" + DD, fm(g("rg_bx")[l, dr, sl]))
            pk.add("rg_lam" + DD, fm(g("rg_lam")[l, dr, sl]))
            m2c = np.concatenate([np.arange(h * 256, h * 256 + 256), np.arange(512 + h * 128, 512 + h * 128 + 128),
                                  np.arange(768 + h * 128, 768 + h * 128 + 128)])
            pk.add("m2_cw" + DD, np.stack([fm(g("m2_conv_w")[l, dr, j, m2c]) for j in range(4)], axis=-1))
            pk.add("m2_cb" + DD, fm(g("m2_conv_b")[l, dr, m2c]))
            for ll in range(DEPTH):
                pk.add("hg_lb%d_%s" % (ll, DD), fm(g("hg_lb")[ll, dr, sl]))
        pk.add("hg_nw" + L, fm(g("hg_norm_w")[l, sl]))
        pk.add("m2_nw" + L, fm(g("m2_norm_w")[l, sl]))
    cp = const_pack()
    for name in cp.off:
        pass
    base = pk.n
    pk.cols += cp.cols
    for name, o in cp.off.items():
        pk.off[name] = base + o
    pk.n += cp.n
    d["pcol"] = pk.build()
    pr = Pack()
    for l in range(DEPTH):
        L = str(l)
        pr.add("bmod_g" + L, rowb(g("b_mod")[l, 2048:3072]))
        for dr in range(2):
            DD = L + str(dr)
            hs = slice(h * 4, h * 4 + 4)
            pr.add("m2_dtb" + DD, rowb(g("m2_dt_bias")[l, dr, hs]))
            pr.add("m2_alog" + DD, rowb(g("m2_a_log")[l, dr, hs]))
            pr.add("m2_d" + DD, rowb(np.repeat(g("m2_d")[l, dr, hs], 64)))
    pr.add("final_w", rowb(g("final_norm_w")))
    deltas = np.abs(np.linspace(math.log(1e-2) / 1.5, math.log(1e-2) / 0.3, W_BR, dtype=np.float32))
    pr.add("hy_delta", rowb(deltas[sl]))
    d["prow"] = pr.build()
    bd = np.zeros((DEPTH, 2, 2, 2, 128, 128), np.float32)
    for l in range(DEPTH):
        for dr in range(2):
            for ai, nm in enumerate(("rg_wa", "rg_wx")):
                wgt = g(nm)[l, dr]
                for cc in range(2):
                    for hh in range(2):
                        head = h * 4 + cc * 2 + hh
                        bd[l, dr, ai, cc, hh * 64:(hh + 1) * 64, hh * 64:(hh + 1) * 64] = wgt[head]
    d["rg_bd"] = bd
    d["hy_w1"] = g("hy_w1")
    d["hy_w2"] = g("hy_w2")
    w3 = g("hy_w3").reshape(DEPTH, HY_HID, 2, 2, W_BR)[:, :, :, :, sl]
    d["hy_w3"] = np.ascontiguousarray(w3.transpose(0, 1, 3, 2, 4)).reshape(DEPTH, HY_HID, 2, 512)
    return d, pk.off, pr.off


PADW = 3
RB_CTX = PADW
RB_LAT = PADW + CTX + PADW
RB_W = RB_LAT + SEQ + PADW
TOKCH = [(0, CTX)] + [(CTX + 512 * i, 512) for i in range(8)]


def rb_off(t0):
    return RB_CTX + t0 if t0 < CTX else RB_LAT + (t0 - CTX)


class Kern:
    def __init__(self, mode, poff, roff, dbg=()):
        self.mode = mode
        self.poff = poff
        self.roff = roff
        self.dbg = set(dbg)
        nc = self.nc = bass.Bass("TRN2", target_bir_lowering=False)
        self.P = Prog(nc)
        self.din = {}
        self.scr = {}
        self.gst = contextlib.ExitStack()
        self.ps_i = 0
        self.uid = 0

    def inp(self, name, shape, dt=F32):
        ap = self.nc.dram_tensor(name, list(shape), dt, kind="ExternalInput").ap()
        self.din[name] = ap
        return ap

    def out(self, name, shape, dt=F32):
        ap = self.nc.dram_tensor(name, list(shape), dt, kind="ExternalOutput").ap()
        self.din[name] = ap
        return ap

    def scratch(self, name, shape, dt=F32):
        kind = "ExternalOutput" if name in self.dbg else "Internal"
        ap = self.nc.dram_tensor(name, list(shape), dt, kind=kind).ap()
        self.scr[name] = ap
        return ap

    def sb(self, st, name, shape, dt=F32):
        self.uid += 1
        return st.enter_context(self.nc.sbuf_tensor(f"{name}_{self.uid}", list(shape), dt))

    def psum(self):
        rot = getattr(self, "ps_rot", None) or list(range(8))
        self.ps_i = (self.ps_i + 1) % len(rot)
        i = rot[self.ps_i]
        return self.ps[i], f"ps{i}"

    def pc(self, name, j=0, n=1):
        o = self.poff[name] + j
        return self.pcol[:, o:o + n]

    def mm(self, out, lhsT, rhs, start, stop, r, w):
        self.P.op("pe", lambda e: e.matmul(out, lhsT=lhsT, rhs=rhs, start=start, stop=stop), r, w)

    def tr(self, out, in_, r, w):
        idn = self.ident
        self.P.op("pe", lambda e: e.transpose(out, in_, idn[:]), list(r) + ["const"], w)

    def act(self, out, in_, func, r, w, **kw):
        self.P.op("act", lambda e: e.activation(out=out, in_=in_, func=func, **kw), r, w)

    def ts(self, eng, out, in0, s1, op0, r, w, s2=None, op1=None):
        if op1 is None:
            self.P.op(eng, lambda e: e.tensor_scalar(out=out, in0=in0, scalar1=s1, scalar2=None, op0=op0), r, w)
        else:
            self.P.op(eng, lambda e: e.tensor_scalar(out=out, in0=in0, scalar1=s1, scalar2=s2, op0=op0, op1=op1), r, w)

    def tt(self, eng, out, in0, in1, op, r, w):
        self.P.op(eng, lambda e: e.tensor_tensor(out=out, in0=in0, in1=in1, op=op), r, w)

    def stt(self, out, in0, scalar, in1, op0, op1, r, w):
        self.P.op("dve", lambda e: e.scalar_tensor_tensor(out=out, in0=in0, scalar=scalar, in1=in1, op0=op0, op1=op1), r, w)

    def cp(self, eng, out, in_, r, w):
        if eng == "act":
            self.act(out, in_, AF.Copy, r, w)
        else:
            self.P.op(eng, lambda e: e.tensor_copy(out=out, in_=in_), r, w)

    def memset(self, eng, ap, val, w):
        self.P.op(eng, lambda e: e.memset(ap, val), (), w)

    def setup(self):
        nc, P = self.nc, self.P
        g = self.gst
        self.ps = [g.enter_context(nc.psum_tensor(f"psb{i}", [128, 512], F32)) for i in range(8)]
        npc = max(self.poff.values()) + 64
        self.npc = self.din["pcol"].shape[1]
        self.pcol = self.sb(g, "pcol", [128, self.npc])
        self.ident = self.sb(g, "ident", [128, 128])
        self.ones_f = self.sb(g, "ones_f", [128, 128])
        self.ones_b = self.sb(g, "ones_b", [128, 128], BF16)
        self.ident_b = self.sb(g, "ident_b", [128, 128], BF16)
        self.modT = self.sb(g, "modT", [128, DEPTH, 24, 2])
        self.Amod = self.sb(g, "Amod", [128, DEPTH, 8, 2])
        P.dma("sp", self.pcol[:], self.din["pcol"], w=["pcol"])
        P.dma("sp", self.ident[:], self.din["c_ident"], w=["const"])
        P.dma("sp", self.ones_f[:], self.din["c_ones"], w=["const"])
        self.epsc = self.sb(g, "epsc", [128, 2])
        self.memset("dve", self.epsc[:, 0:1], EPS, ["epsc"])
        self.memset("dve", self.epsc[:, 1:2], 1.0, ["epsc"])
        self.cp("dve", self.ones_b[:], self.ones_f[:], ["const"], ["const"])
        self.cp("dve", self.ident_b[:], self.ident[:], ["const"], ["const"])

    def phase_mod(self):
        P = self.P
        w_mod = self.din["w_mod"]
        with contextlib.ExitStack() as st:
            wst = [self.sb(st, "wm", [128, 8, 512]) for _ in range(2)]
            cond = self.sb(st, "cond", [128, 8, 2])
            grow = self.sb(st, "grow", [2, 1024])
            brow = self.sb(st, "brow", [2, 1024])
            sel = self.sb(st, "sel", [2, 2, 128])
            gb = [self.sb(st, "gb", [128, 512]) for _ in range(2)]
            P.dma("sp", sel[:], self.din["c_sel"].rearrange("s k m -> k s m"), w=["sel"])
            self.act(cond[:].rearrange("p k s -> p (k s)"), self.pc("cvec", 0, 16), AF.Silu, ["pcol"], ["cond"])
            it = 0
            for l in range(DEPTH):
                o = self.roff["bmod_g%d" % l]
                P.dma("sp", brow[:], self.din["prow"][0:2, o:o + 1024], w=["brow"])
                for jb in range(6):
                    buf = wst[it % 2]; bk = "wm%d" % (it % 2); it += 1
                    P.dma("sp", buf[:], w_mod[l, :, jb * 512:(jb + 1) * 512].rearrange("(k p) c -> p k c", p=128), w=[bk])
                    ps, pk = self.psum()
                    for jj in range(4):
                        for k in range(8):
                            self.mm(ps[:, jj * 2:jj * 2 + 2], buf[:, k, jj * 128:(jj + 1) * 128], cond[:, k, :],
                                    k == 0, k == 7, [bk, "cond"], [pk])
                    bm = self.pc("b_mod%d" % l, jb * 4, 4)
                    self.tt("dve", self.modT[:, l, jb * 4:(jb + 1) * 4, :], ps[:, 0:8].rearrange("p (j s) -> p j s", s=2),
                            bm.unsqueeze(2).to_broadcast([128, 4, 2]), ALU.add, [pk, "pcol"], ["modT"])
                    if jb >= 4:
                        ps2, pk2 = self.psum()
                        for k in range(8):
                            self.mm(ps2[0:2, :], cond[:, k, :], buf[:, k, :], k == 0, k == 7, [bk, "cond"], [pk2])
                        c0 = (jb - 4) * 512
                        self.tt("dve", grow[:, c0:c0 + 512], ps2[0:2, :], brow[:, c0:c0 + 512], ALU.add,
                                [pk2, "brow"], ["grow"])
                nw = self.pc("norm_w%d" % l, 0, 8)
                self.ts("dve", self.Amod[:, l, :, :], self.modT[:, l, 8:16, :], 1.0, ALU.add, ["modT"], ["Amod"])
                self.tt("dve", self.Amod[:, l, :, :], self.Amod[:, l, :, :], nw.unsqueeze(2).to_broadcast([128, 8, 2]),
                        ALU.mult, ["Amod", "pcol"], ["Amod"])
                for s in range(2):
                    for hf in range(2):
                        ps3, pk3 = self.psum()
                        self.mm(ps3[:, :], sel[:, s, :], grow[:, hf * 512:(hf + 1) * 512], True, True, ["sel", "grow"], [pk3])
                        gt = gb[(s * 2 + hf) % 2]; gk = "gb%d" % ((s * 2 + hf) % 2)
                        self.cp("act", gt[:], ps3[:, :], [pk3], [gk])
                        P.dma(STQ, self.scr["gbc"][l, s, :, hf * 512:(hf + 1) * 512], gt[:], r=[gk], w=["gbc"])
            P.flush()

    def phase_norm(self, l, st, xlat, xctx):
        P = self.P
        hlT = self.hlT = self.sb(st, "hlT", [128, 8, NT], BF16)
        with contextlib.ExitStack() as s2:
            xt = [self.sb(s2, "xt", [128, 1024]) for _ in range(6)]
            junks = [self.sb(s2, "junk", [128, 1024]) for _ in range(2)]
            ssq = self.sb(s2, "ssq", [128, 4])
            rstd = self.sb(s2, "rstd", [128, 4])
            groups = [[0, 1]] + [list(range(2 + 4 * i, 6 + 4 * i)) for i in range(8)]
            xi = 0
            for gi, grp in enumerate(groups):
                s = 1 if gi == 0 else 0
                ng = len(grp)
                tl = []
                for i, ti in enumerate(grp):
                    t = xt[xi % 6]; tk = "xt%d" % (xi % 6); xi += 1
                    src = xctx[ti * 128:(ti + 1) * 128, :] if ti < 2 else xlat[(ti - 2) * 128:(ti - 1) * 128, :]
                    P.dma("sp", t[:], src, w=[tk])
                    jn = junks[xi % 2]
                    self.P.op("act", lambda e, t=t, i=i, jn=jn: e.activation(out=jn[:], in_=t[:], func=AF.Square,
                                                                           accum_out=ssq[:, i:i + 1]), [tk], ["junk%d" % (xi % 2), ("ssq", i)])
                    tl.append((t, tk))
                self.act(rstd[:, :ng], ssq[:, :ng], AF.Ln, [("ssq", i) for i in range(ng)], ["rstd"], scale=1.0 / D, bias=self.epsc[:, 0:1])
                self.act(rstd[:, :ng], rstd[:, :ng], AF.Exp, ["rstd"], ["rstd"], scale=-0.5)
                for i, (t, tk) in enumerate(tl):
                    if i % 2 == 0:
                        self.ts("dve", t[:], t[:], rstd[:, i:i + 1], ALU.mult, [tk, "rstd"], [tk])
                    else:
                        self.act(t[:], t[:], AF.Copy, [tk, "rstd"], [tk], scale=rstd[:, i:i + 1])
                tok0 = grp[0] * 128
                for k in range(8):
                    ps, pk = self.psum()
                    for i, (t, tk) in enumerate(tl):
                        self.tr(ps[:, i * 128:(i + 1) * 128], t[:, k * 128:(k + 1) * 128], [tk], [pk])
                    hk = ("hlT", tok0 // 512 if tok0 >= CTX else -1)
                    a_col = self.Amod[:, l, k, s:s + 1]
                    sh_col = self.modT[:, l, k, s:s + 1]
                    if k % 2 == 0:
                        self.ts("dve", hlT[:, k, tok0:tok0 + ng * 128], ps[:, :ng * 128], a_col, ALU.mult,
                                [pk, "Amod", "modT"], [hk], s2=sh_col, op1=ALU.add)
                    else:
                        self.act(hlT[:, k, tok0:tok0 + ng * 128], ps[:, :ng * 128], AF.Identity,
                                 [pk, "Amod", "modT"], [hk], scale=a_col, bias=sh_col)
            P.flush()

    def hl_key(self, t0):
        return ("hlT", (t0 - CTX) // 512 if t0 >= CTX else -1)

    def conv(self, obuf, ok, rbuf, rk, wname, wj, ntap, bname, bj, offs):
        lo, hi = PADW, RB_W - PADW
        o = obuf[:, lo:hi]
        self.ts("dve", o, rbuf[:, lo + offs[0]:hi + offs[0]], self.pc(wname, wj, 1), ALU.mult,
                [rk, "pcol"], [ok], s2=self.pc(bname, bj, 1), op1=ALU.add)
        for j in range(1, ntap):
            self.stt(o, rbuf[:, lo + offs[j]:hi + offs[j]], self.pc(wname, wj + j, 1), o, ALU.mult, ALU.add,
                     [rk, ok, "pcol"], [ok])

    def rb_store(self, q, dst_row_ap, buf, bk, wkey):
        self.P.dma(q, dst_row_ap[:, 0:CTX], buf[:, RB_CTX:RB_CTX + CTX], r=[bk], w=[wkey])
        self.P.dma(q, dst_row_ap[:, CTX:NT], buf[:, RB_LAT:RB_LAT + SEQ], r=[bk], w=[wkey])

    def phase_inproj(self, l):
        P = self.P
        hlT = self.hlT
        w_in = self.din["w_in"]
        S = self.scr
        L = str(l)
        with contextlib.ExitStack() as st:
            wst = [self.sb(st, "wst", [128, 8, 256]) for _ in range(2)]
            wbf = [self.sb(st, "wbf", [128, 8, 256], BF16) for _ in range(2)]
            rbs = [self.sb(st, "rb", [128, RB_W]) for _ in range(3)]
            obs = [self.sb(st, "ob", [128, RB_W]) for _ in range(3)]
            tmv = [self.sb(st, "tmv", [128, 256]) for _ in range(2)]
            dtt = self.sb(st, "dtt", [128, NTILE * 4])
            for i in range(3):
                self.memset("pool", rbs[i][:], 0.0, ["rb%d" % i])
            cnt = {"w": 0, "rb": 0, "ob": 0, "ev": 0, "tm": 0}

            def load_w(c0, n):
                i = cnt["w"] % 2; cnt["w"] += 1
                P.dma("sp", wst[i][:, :, :n], w_in[l, :, c0:c0 + n].rearrange("(k p) c -> p k c", p=128), w=["wst%d" % i])
                self.cp("act", wbf[i][:, :, :n], wst[i][:, :, :n], ["wst%d" % i], ["wbf%d" % i])
                return wbf[i], "wbf%d" % i

            def rhs_nat(k, t0, n):
                return hlT[:, k, t0:t0 + n], [self.hl_key(t0)]

            def rhs_cm(k, t0, n):
                if t0 < CTX:
                    return rhs_nat(k, t0, n)
                i = (t0 - CTX) // 512
                v = hlT[:, k, CTX:NT].rearrange("p (r w) -> p w r", w=64)[:, 8 * i:8 * i + 8, :]
                return v, [("hlT", j) for j in range(8)]

            def fm_block(wb, wk, cb, rhsf, evac):
                for (t0, n) in TOKCH:
                    ps, pk = self.psum()
                    if rhsf is rhs_cm and t0 >= CTX:
                        i = (t0 - CTX) // 512
                        hk = [("hlT", j) for j in range(8)]
                        for wl in range(8):
                            w = 8 * i + wl
                            for k in range(8):
                                self.mm(ps[:, wl * 64:(wl + 1) * 64], wb[:, k, cb * 128:(cb + 1) * 128],
                                        hlT[:, k, CTX + w:NT:64], k == 0, k == 7, [wk] + hk, [pk])
                    else:
                        for k in range(8):
                            rhs, rk = rhs_nat(k, t0, n)
                            self.mm(ps[:, :n], wb[:, k, cb * 128:(cb + 1) * 128], rhs, k == 0, k == 7, [wk] + rk, [pk])
                    evac(ps, pk, t0, n)

            def evac_to(buf, bk, func, scale=None):
                def f(ps, pk, t0, n):
                    o = buf[:, rb_off(t0):rb_off(t0) + n]
                    cnt["ev"] += 1
                    if func is None and scale is None and cnt["ev"] % 2 == 0:
                        self.cp("dve", o, ps[:, :n], [pk], [bk])
                    else:
                        kw = {} if scale is None else {"scale": scale}
                        self.act(o, ps[:, :n], AF.Copy if func is None else func, [pk], [bk], **kw)
                return f

            def next_rb():
                i = cnt["rb"] % 3; cnt["rb"] += 1
                return rbs[i], "rb%d" % i

            def next_ob():
                i = cnt["ob"] % 3; cnt["ob"] += 1
                return obs[i], "ob%d" % i

            def do_hy():
                for j in range(3):
                    wb, wk = load_w(C_HYV + j * 256, 256)
                    for cb in range(2):
                        rb, rk = next_rb()
                        fm_block(wb, wk, cb, rhs_nat, evac_to(rb, rk, None))
                        ob, ok = next_ob()
                        self.conv(ob, ok, rb, rk, "hy_cw" + L, (j * 2 + cb) * 3, 3, "hy_cb" + L, j * 2 + cb, (-1, 0, 1))
                        self.rb_store(STQ, S["hy_u"][j, cb * 128:(cb + 1) * 128, :], ob, ok, "hy_u")

            def do_gate(c0, nm):
                wb, wk = load_w(c0, 256)
                for cb in range(2):
                    rb, rk = next_rb()
                    fm_block(wb, wk, cb, rhs_nat, evac_to(rb, rk, AF.Silu))
                    self.rb_store(STQ, S[nm][cb * 128:(cb + 1) * 128, :], rb, rk, nm)

            def do_rg():
                wb, wk = load_w(C_RGX, 256)
                for cb in range(2):
                    rb, rk = next_rb()
                    fm_block(wb, wk, cb, rhs_nat, evac_to(rb, rk, None))
                    for dr in range(2):
                        ob, ok = next_ob()
                        offs = (-3, -2, -1, 0) if dr == 0 else (3, 2, 1, 0)
                        self.conv(ob, ok, rb, rk, "rg_cw" + L + str(dr), cb * 4, 4, "rg_cb" + L + str(dr), cb, offs)
                        self.rb_store(STQ, S["rg_x"][dr, cb * 128:(cb + 1) * 128, :], ob, ok, "rg_x")

            def do_plain(c0, dst, scale):
                wb, wk = load_w(c0, 256)
                for cb in range(2):
                    rb, rk = next_rb()
                    fm_block(wb, wk, cb, rhs_nat, evac_to(rb, rk, None, scale))
                    self.rb_store(STQ, dst[cb * 128:(cb + 1) * 128, :], rb, rk, "hg_qf")

            def do_m2(c0, n, ch0):
                wb, wk = load_w(c0, n)
                for cb in range(2):
                    rb, rk = next_rb()
                    fm_block(wb, wk, cb, rhs_cm, evac_to(rb, rk, None))
                    for dr in range(2):
                        ob, ok = next_ob()
                        offs = (-3, -2, -1, 0) if dr == 0 else (3, 2, 1, 0)
                        ch = ch0 + cb
                        self.conv(ob, ok, rb, rk, "m2_cw" + L + str(dr), ch * 4, 4, "m2_cb" + L + str(dr), ch, offs)
                        self.act(ob[:, PADW:RB_W - PADW], ob[:, PADW:RB_W - PADW], AF.Silu, [ok], [ok])
                        self.rb_store(STQ, S["m2_x"][dr, ch * 128:(ch + 1) * 128, :], ob, ok, "m2_x")

            do_m2(C_M2XS, 256, 0)
            do_gate(C_HYG, "hy_g")
            do_gate(C_RGG, "rg_g")
            do_rg()
            do_gate(C_HGG, "hg_g")
            do_gate(C_M2G, "m2_g")
            do_m2(C_M2B, 256, 2)
            do_plain(C_HGQ, S["hg_q"], 128.0 ** -0.5)
            do_plain(C_HGFF, S["hg_f"][0], None)
            do_hy()
            do_plain(C_HGFB, S["hg_f"][1], None)
            wb, wk = load_w(C_HGI, 256)
            for ti in range(NTILE):
                ps, pk = self.psum()
                for k in range(8):
                    self.mm(ps[:, 0:256], hlT[:, k, ti * 128:(ti + 1) * 128], wb[:, k, :], k == 0, k == 7,
                            [wk, self.hl_key(ti * 128)], [pk])
                i = cnt["tm"] % 2; cnt["tm"] += 1
                self.cp("dve" if ti % 2 else "act", tmv[i][:], ps[:, 0:256], [pk], ["tmv%d" % i])
                P.dma(STQ, S["hg_v"][ti * 128:(ti + 1) * 128, :], tmv[i][:], r=["tmv%d" % i], w=["hg_v"])
            wb, wk = load_w(C_M2DT, 4)
            ps, pk = self.psum()
            for ti in range(NTILE):
                if ti < 2:
                    for k in range(8):
                        self.mm(ps[:, ti * 4:ti * 4 + 4], hlT[:, k, ti * 128:(ti + 1) * 128], wb[:, k, 0:4], k == 0, k == 7,
                                [wk, self.hl_key(ti * 128)], [pk])
                else:
                    hk = [("hlT", q) for q in range(8)]
                    for wl in range(2):
                        w = 2 * (ti - 2) + wl
                        for k in range(8):
                            self.mm(ps[wl * 64:(wl + 1) * 64, ti * 4:ti * 4 + 4], hlT[:, k, CTX + w:NT:64], wb[:, k, 0:4],
                                    k == 0, k == 7, [wk] + hk, [pk])
            self.cp("dve", dtt[:], ps[:, 0:NTILE * 4], [pk], ["dtt"])
            P.dma(STQ, S["m2_dt"], dtt[:], r=["dtt"], w=["m2_dt"])
            P.flush()

    def declare(self, npc, npr):
        mode = self.mode
        mix = mode in ("A", "B", "test", "ALL")
        self.inp("w_mod", [DEPTH, D, 3 * D]); self.inp("w_out", [DEPTH, 2 * D, D])
        self.inp("pcol", [128, npc]); self.inp("prow", [128, npr])
        self.inp("c_ident", [128, 128]); self.inp("c_ones", [128, 128]); self.inp("c_sel", [2, 2, 128])
        if mode != "C":
            self.inp("x", [SEQ, D]); self.inp("ctx", [CTX, D])
        if mix:
            self.inp("w_in", [DEPTH, D, NCOL])
            self.inp("rg_bd", [DEPTH, 2, 2, 2, 128, 128])
            self.inp("hy_w1", [DEPTH, HY_EMB, HY_HID]); self.inp("hy_w2", [DEPTH, HY_HID, HY_HID])
            self.inp("hy_w3", [DEPTH, HY_HID, 2, 512])
            self.inp("c_tri_incl", [128, 128]); self.inp("c_tri_excl", [128, 128])
            self.inp("c_hgmask_f", [128, 32], I32); self.inp("c_hgmask_b", [128, 32], I32)
            self.inp("c_m2mask_f", [128, 128]); self.inp("c_m2mask_b", [128, 128])
            self.inp("tabC_L", [8, 8, 128, 4, 512], BF16); self.inp("tabS_L", [8, 8, 128, 4, 512], BF16)
            self.inp("tabC_C", [1, 1, 128, 2, 256], BF16); self.inp("tabS_C", [1, 1, 128, 2, 256], BF16)
            self.inp("z_L", [HY_EMB, SEQ + 1]); self.inp("z_C", [HY_EMB, CTX + 1])
        sc = self.scratch
        sc("gbc", [DEPTH, 2, 128, D])
        if mix:
            sc("hy_u", [3, CH, NT]); sc("hy_g", [CH, NT]); sc("rg_g", [CH, NT]); sc("hg_g", [CH, NT]); sc("m2_g", [CH, NT])
            sc("rg_x", [2, CH, NT]); sc("hg_q", [CH, NT]); sc("hg_f", [2, CH, NT]); sc("hg_v", [NT, CH])
            sc("m2_x", [2, 512, NT]); sc("m2_dt", [128, NTILE * 4])
            sc("khat_L", [SEQ // 128, 2, 128, 512]); sc("khat_C", [CTX // 128, 2, 128, 512])
            if mode in ("A", "B"):
                self.scr["ybuf"] = self.out("y_out", [4 * CH, NT], BF16)
            else:
                sc("ybuf", [4 * CH, NT], BF16)
        if mode == "ALL":
            sc("yf", [2 * D, NT], BF16)
            sc("xres", [NT, D])
            self.out("out", [SEQ, D])
        if mode == "B":
            self.inp("yf", [2 * D, NT], BF16)
            self.scr["xres"] = self.out("xres_out", [NT, D])
        if mode == "C":
            self.inp("yf", [2 * D, SEQ // 2], BF16)
            self.inp("xres", [SEQ // 2, D])
            self.out("out", [SEQ // 2, D])

    def mixers(self, l, xlat, xctx, need_ctx, early=None):
        with contextlib.ExitStack() as st:
            self.phase_norm(l, st, xlat, xctx)
            self.phase_inproj(l)
        self.phase_rg(l)
        self.phase_hg(l)
        self.phase_m2(l)
        if early is not None:
            early()
        self.phase_hy(l, need_ctx)

    def finish(self):
        self.gst.close()
        self.P.close()
        return self.nc


def build_program(mode, poff, roff, npc, npr):
    kn = Kern(mode, poff, roff)
    kn.declare(npc, npr)
    kn.setup()
    kn.phase_mod()
    din = kn.din
    if mode == "A":
        kn.mixers(0, din["x"], din["ctx"], True)
    elif mode == "B":
        xres = kn.scr["xres"]
        kn.phase_out(0, din["yf"], lambda ti: (din["ctx"][ti * 128:(ti + 1) * 128, :] if ti < 2 else
                                                din["x"][(ti - 2) * 128:(ti - 1) * 128, :]),
                     list(range(NTILE)), xdst=lambda ti: [xres[ti * 128:(ti + 1) * 128, :]])
        kn.mixers(1, xres[CTX:NT, :], xres[0:CTX, :], False)
    elif mode == "ALL":
        xres = kn.scr["xres"]
        yf = kn.scr["yf"]
        ybuf = kn.scr["ybuf"]
        groups = [[0, 1], [2, 3], [4, 5], [6, 7]]

        def gather(js):
            for j in js:
                kn.P.collective(lambda e, j=j: e.collective_compute(
                    "AllGather", ALU.bypass, replica_groups=groups,
                    ins=[ybuf[j * 128:(j + 1) * 128, :].opt()], outs=[yf[j * 256:(j + 1) * 256, :].opt()]),
                    r=["ybuf"], w=[("yf", j)])
            if 0 in js:
                kn.P.flush()
        kn.mixers(0, din["x"], din["ctx"], True)
        gather(range(0, 8))
        kn.phase_out(0, yf, lambda ti: (din["ctx"][ti * 128:(ti + 1) * 128, :] if ti < 2 else
                                        din["x"][(ti - 2) * 128:(ti - 1) * 128, :]),
                     list(range(NTILE)), xdst=lambda ti: [xres[ti * 128:(ti + 1) * 128, :]])
        kn.mixers(1, xres[CTX:NT, :], xres[0:CTX, :], False)
        gather(range(0, 8))
        kn.phase_out(1, yf, lambda ti: xres[ti * 128:(ti + 1) * 128, :], list(range(2, NTILE)),
                     final_dst=lambda ti: din["out"][(ti - 2) * 128:(ti - 1) * 128, :])
    elif mode == "C":
        kn.phase_out(1, din["yf"], lambda ti: din["xres"][ti * 128:(ti + 1) * 128, :], list(range(SEQ // 256)),
                     final_dst=lambda ti: din["out"][ti * 128:(ti + 1) * 128, :], lat_only=True)
    return kn.finish()


def const_inputs():
    key = "cin"
    if key not in _CONST_CACHE:
        m = const_mats()
        d = {"c_" + k: v for k, v in m.items()}
        d["tabC_L"], d["tabS_L"] = dft_tables(SEQ)
        d["tabC_C"], d["tabS_C"] = dft_tables(CTX)
        d["z_L"] = hy_zfeat(SEQ)
        d["z_C"] = hy_zfeat(CTX)
        _CONST_CACHE[key] = d
    return _CONST_CACHE[key]


def _phase_rg(self, l):
    P = self.P
    S = self.scr
    L = str(l)
    CHK = [(i * 512, min(512, NT - i * 512)) for i in range(9)]
    with contextlib.ExitStack() as st:
        wtmps = [self.sb(st, "rgw", [128, 128]) for _ in range(2)]
        wbds = [self.sb(st, "rgwb", [128, 4, 128], BF16) for _ in range(2)]
        xcs = [self.sb(st, "rgxc", [128, NT]) for _ in range(2)]
        xcbs = [self.sb(st, "rgxcb", [128, NT], BF16) for _ in range(2)]
        avs = [self.sb(st, "rga", [128, NT]) for _ in range(2)]
        bvs = [self.sb(st, "rgb", [128, NT]) for _ in range(2)]
        gis = [self.sb(st, "rggi", [128, NT]) for _ in range(2)]
        prms = [self.sb(st, "rgprm", [128, 2]) for _ in range(2)]
        hs = self.sb(st, "rghs", [128, NT])
        hb = self.sb(st, "rghb", [128, NT])
        yb = self.sb(st, "rgy", [128, NT], BF16)
        for cc in range(2):
            for dr in range(2):
                DD = L + str(dr)
                bi = (cc * 2 + dr) % 2
                wtmp, wbd, xc, xcb, av, bv, gi, prm = wtmps[bi], wbds[bi], xcs[bi], xcbs[bi], avs[bi], bvs[bi], gis[bi], prms[bi]
                sfx = str(bi)
                self.act(prm[:, 0:1], self.pc("rg_lam" + DD, cc, 1), AF.Exp, ["pcol"], ["rgprm" + sfx], scale=-1.0)
                self.act(prm[:, 0:1], prm[:, 0:1], AF.Ln, ["rgprm" + sfx], ["rgprm" + sfx], bias=self.epsc[:, 1:2])
                self.ts("dve", prm[:, 1:2], prm[:, 0:1], -8.0, ALU.mult, ["rgprm" + sfx], ["rgprm" + sfx])
                for ai in range(2):
                    P.dma("sp", wtmp[:], self.din["rg_bd"][l, dr, ai, cc], w=["rgw" + sfx])
                    self.cp("dve", wbd[:, ai, :], wtmp[:], ["rgw" + sfx], ["rgwb" + sfx])
                P.dma("sp", xc[:], S["rg_x"][dr, cc * 128:(cc + 1) * 128, :], w=["rgxc" + sfx])
                self.cp("dve", xcb[:], xc[:], ["rgxc" + sfx], ["rgxcb" + sfx])
                for (t0, n) in CHK:
                    pa, pak = self.psum()
                    px, pxk = self.psum()
                    self.mm(pa[:, :n], wbd[:, 0, :], xcb[:, t0:t0 + n], True, True, ["rgwb" + sfx, "rgxcb" + sfx], [pak])
                    self.mm(px[:, :n], wbd[:, 1, :], xcb[:, t0:t0 + n], True, True, ["rgwb" + sfx, "rgxcb" + sfx], [pxk])
                    self.act(av[:, t0:t0 + n], pa[:, :n], AF.Sigmoid, [pak, "pcol"], ["rga" + sfx], bias=self.pc("rg_ba" + DD, cc, 1))
                    self.act(gi[:, t0:t0 + n], px[:, :n], AF.Sigmoid, [pxk, "pcol"], ["rggi" + sfx], bias=self.pc("rg_bx" + DD, cc, 1))
                self.act(av[:], av[:], AF.Exp, ["rga" + sfx, "rgprm" + sfx], ["rga" + sfx], scale=prm[:, 1:2])
                self.tt("pool", bv[:], av[:], av[:], ALU.mult, ["rga" + sfx], ["rgb" + sfx])
                self.act(bv[:], bv[:], AF.Sqrt, ["rgb" + sfx], ["rgb" + sfx], scale=-1.0, bias=self.epsc[:, 1:2])
                self.tt("dve", gi[:], gi[:], xc[:], ALU.mult, ["rggi" + sfx, "rgxc" + sfx], ["rggi" + sfx])
                self.tt("dve", bv[:], bv[:], gi[:], ALU.mult, ["rgb" + sfx, "rggi" + sfx], ["rgb" + sfx])
                dst = hs if dr == 0 else hb
                dk = "rghs" if dr == 0 else "rghb"
                if dr == 0:
                    segs = [(slice(0, CTX), None), (slice(CTX, NT), (CTX - 1, CTX))]
                    rev = False
                else:
                    segs = [(slice(0, CTX), None), (slice(CTX, NT), (0, 1))]
                    rev = True
                for (sg, init) in segs:
                    a0, a1 = sg.start, sg.stop
                    if not rev:
                        o_ap, d0, d1 = dst[:, a0:a1], av[:, a0:a1], bv[:, a0:a1]
                    else:
                        def rv(t, a0=a0, a1=a1):
                            return t[:, a1 - 1:a0 - 1:-1] if a0 > 0 else t[:, a1 - 1::-1]
                        o_ap, d0, d1 = rv(dst), rv(av), rv(bv)
                    ini = 0.0 if init is None else dst[:, init[0]:init[1]]
                    self.P.op("dve", lambda e, o_ap=o_ap, d0=d0, d1=d1, ini=ini: e.tensor_tensor_scan(
                        out=o_ap, data0=d0, data1=d1, initial=ini, op0=ALU.mult, op1=ALU.add), ["rga" + sfx, "rgb" + sfx, dk], [dk])
            self.tt("pool", hs[:], hs[:], hb[:], ALU.add, ["rghs", "rghb"], ["rghs"])
            P.dma("sp", hb[:], S["rg_g"][cc * 128:(cc + 1) * 128, :], r=[], w=["rghb"])
            self.tt("dve", yb[:], hs[:], hb[:], ALU.mult, ["rghs", "rghb"], ["rgy"])
            P.dma(STQ, S["ybuf"][CH + cc * 128:CH + (cc + 1) * 128, :], yb[:], r=["rgy"], w=["ybuf"])
        P.flush()


Kern.phase_rg = _phase_rg


def _phase_hg(self, l):
    P = self.P
    S = self.scr
    L = str(l)
    NCH = NT // 64
    self.ps_rot = [0, 1, 2, 3, 4, 5]
    with contextlib.ExitStack() as st:
        vtok = self.sb(st, "hgvtok", [64, NCH, 256], BF16)
        q = self.sb(st, "hgq", [128, NT])
        zs = self.sb(st, "hgzs", [128, NT])
        Pb = self.sb(st, "hgP", [128, NT + 1])
        Db = self.sb(st, "hgD", [128, NT])
        kGf = self.sb(st, "hgkG", [128, NT])
        qg = self.sb(st, "hgqg", [128, NT], BF16)
        kg = self.sb(st, "hgkg", [128, NT], BF16)
        qG = self.sb(st, "hgqG", [128, NT], BF16)
        osum = self.sb(st, "hgos", [128, NT])
        dec = self.sb(st, "hgdec", [128, NCH])
        lbc = self.sb(st, "hglb", [128, 4])
        Sf = self.sb(st, "hgSf", [128, 128])
        Sbs = [self.sb(st, "hgSb", [128, 128], BF16) for _ in range(2)]
        MT = [self.sb(st, "hgMT", [128, 64], BF16) for _ in range(3)]
        kGt = [self.sb(st, "hgkGt", [128, 128], BF16) for _ in range(3)]
        msk = [self.sb(st, "hgmsk", [128, 32], I32) for _ in range(2)]
        XB = self.sb(st, "hgXB", [128, NT], BF16)
        P.dma("sp", msk[0][:], self.din["c_hgmask_f"], w=["hgmsk"])
        P.dma("sp", msk[1][:], self.din["c_hgmask_b"], w=["hgmsk"])
        for i in range(3):
            self.memset("pool", MT[i][:], 0.0, ["hgMT%d" % i])
        vsrc = S["hg_v"].rearrange("(c p) d -> p c d", p=64)
        with contextlib.ExitStack() as st1:
            vst = self.sb(st1, "hgvst", [64, 17, 256])
            for qd in range(4):
                P.dma("sp", vst[:], vsrc[:, qd * 17:(qd + 1) * 17, :], w=["hgvst"])
                self.cp("pool", vtok[:, qd * 17:(qd + 1) * 17, :], vst[:], ["hgvst"], ["hgvtok"])
            P.flush()
        Pv3 = Pb[:, 0:NT].rearrange("p (c t) -> p c t", t=64)
        Pn3 = Pb[:, 1:NT + 1].rearrange("p (c t) -> p c t", t=64)
        D3 = Db[:].rearrange("p (c t) -> p c t", t=64)
        bc = lambda a: a.to_broadcast([128, NCH, 64])
        ref_b = bc(Pv3[:, :, 32:33])
        p0_b = bc(Pv3[:, :, 0:1])
        p1_b = bc(Pn3[:, :, 63:64])
        for cc in range(2):
            rows = slice(cc * 128, (cc + 1) * 128)
            P.dma("sp", q[:], S["hg_q"][rows, :], w=["hgq"])
            for dr in range(2):
                DD = L + str(dr)
                if l == 0:
                    self.memset("dve", lbc[:, 0:1], 0.0, ["hglb"])
                else:
                    self.tt("dve", lbc[:, 0:1], self.pc("hg_lb1_" + DD, cc, 1), self.pc("hg_lb0_" + DD, cc, 1), ALU.subtract,
                            ["pcol"], ["hglb"])
                    self.act(lbc[:, 0:1], lbc[:, 0:1], AF.Sigmoid, ["hglb"], ["hglb"])
                self.ts("dve", lbc[:, 1:2], lbc[:, 0:1], -1.0, ALU.mult, ["hglb"], ["hglb"], s2=1.0, op1=ALU.add)
                self.ts("dve", lbc[:, 2:3], lbc[:, 1:2], -1.0, ALU.mult, ["hglb"], ["hglb"])
                P.dma("sp", zs[:], S["hg_f"][dr, rows, :], w=["hgzs"])
                self.act(zs[:], zs[:], AF.Sigmoid, ["hgzs"], ["hgzs"])
                self.ts("dve", Db[:], zs[:], lbc[:, 1:2], ALU.mult, ["hgzs", "hglb"], ["hgD"], s2=lbc[:, 0:1], op1=ALU.add)
                self.act(Db[:], Db[:], AF.Ln, ["hgD"], ["hgD"])
                self.ts("pool", zs[:], zs[:], lbc[:, 2:3], ALU.mult, ["hgzs", "hglb"], ["hgzs"], s2=lbc[:, 1:2], op1=ALU.add)
                self.memset("dve", Pb[:, 0:1], 0.0, ["hgP"])
                ones_bc = self.ones_f[:, 0:1].to_broadcast([128, NT])
                self.P.op("dve", lambda e, ones_bc=ones_bc: e.tensor_tensor_scan(
                    out=Pb[:, 1:NT + 1], data0=ones_bc, data1=Db[:], initial=0.0, op0=ALU.mult, op1=ALU.add),
                    ["hgD", "const"], ["hgP"])
                arr3 = Pn3 if dr == 0 else Pv3
                sg = 1.0 if dr == 0 else -1.0
                self.tt("dve", D3, arr3, ref_b, ALU.subtract, ["hgP"], ["hgD"])
                XB4 = XB[:].rearrange("p (c h t) -> p c h t", h=2, t=32)
                q4 = q[:].rearrange("p (c h t) -> p c h t", h=2, t=32)
                k4 = zs[:].rearrange("p (c h t) -> p c h t", h=2, t=32)
                e4 = kGf[:].rearrange("p (c h t) -> p c h t", h=2, t=32)
                hq, hk = (1, 0) if dr == 0 else (0, 1)
                self.act(kGf[:], Db[:], AF.Exp, ["hgD"], ["hgkG"], scale=sg)
                self.tt("pool", XB4[:, :, hq, :], q4[:, :, hq, :], e4[:, :, hq, :], ALU.mult, ["hgq", "hgkG"], ["hgXB"])
                self.act(kGf[:], Db[:], AF.Exp, ["hgD", "hgXB"], ["hgkG"], scale=-sg)
                self.tt("dve", XB4[:, :, hk, :], k4[:, :, hk, :], e4[:, :, hk, :], ALU.mult, ["hgzs", "hgkG"], ["hgXB"])
                Ph3 = Pb[:, 0:NT].rearrange("p (c t) -> p c t", t=32)
                a32 = (Pb[:, 1:NT + 1] if dr == 0 else Pb[:, 0:NT]).rearrange("p (c t) -> p c t", t=32)
                self.tt("dve", Db[:].rearrange("p (c t) -> p c t", t=32), a32,
                        Ph3[:, :, 16:17].to_broadcast([128, NT // 32, 32]), ALU.subtract, ["hgP", "hgXB"], ["hgD"])
                self.act(kGf[:], Db[:], AF.Exp, ["hgD", "hgXB"], ["hgkG"], scale=sg)
                self.tt("pool", qg[:], q[:], kGf[:], ALU.mult, ["hgq", "hgkG"], ["hgqg"])
                self.act(kGf[:], Db[:], AF.Exp, ["hgD", "hgqg"], ["hgkG"], scale=-sg)
                self.tt("dve", kg[:], zs[:], kGf[:], ALU.mult, ["hgzs", "hgkG"], ["hgkg"])
                self.tt("pool", D3, arr3, p0_b if dr == 0 else p1_b, ALU.subtract, ["hgP", "hgkg"], ["hgD"])
                self.act(kGf[:], Db[:], AF.Exp, ["hgD", "hgkg"], ["hgkG"], scale=sg)
                self.tt("dve", qG[:], q[:], kGf[:], ALU.mult, ["hgq", "hgkG"], ["hgqG"])
                self.tt("pool", D3, arr3, p1_b if dr == 0 else p0_b, ALU.subtract, ["hgP", "hgqG"], ["hgD"])
                self.act(kGf[:], Db[:], AF.Exp, ["hgD", "hgqG"], ["hgkG"], scale=-sg)
                self.tt("dve", kGf[:], kGf[:], zs[:], ALU.mult, ["hgzs", "hgkG"], ["hgkG"])
                self.tt("dve", dec[:], Pb[:, 64:NT + 1:64], Pb[:, 0:NT:64], ALU.subtract, ["hgP"], ["hgdec"])
                self.act(dec[:], dec[:], AF.Exp, ["hgdec"], ["hgdec"])
                self.memset("dve", Sf[:], 0.0, ["hgSf"])
                for i in range(2):
                    self.memset("pool", Sbs[i][:], 0.0, ["hgSb%d" % i])
                for i in range(3):
                    self.memset("pool", MT[i][:], 0.0, ["hgMT%d" % i])
                order = list(range(NCH)) if dr == 0 else [3, 2, 1, 0] + list(range(NCH - 1, 3, -1))
                grp_of = lambda c: -1 if c < 4 else (c - 4) // 8
                bank_of = {}
                gi = 0
                for c in order:
                    g = grp_of(c)
                    if g not in bank_of:
                        bank_of[g] = 6 + gi % 2
                        gi += 1

                def front(ci, c):
                    t0 = c * 64
                    mt, mk = MT[ci % 3], "hgMT%d" % (ci % 3)
                    kt, kk = kGt[ci % 3], "hgkGt%d" % (ci % 3)
                    psT, ptk = self.psum()
                    self.tr(psT[0:64, 0:128], kGf[:, t0:t0 + 64], ["hgkG"], [ptk])
                    self.cp("act", kt[0:64, :], psT[0:64, 0:128], [ptk], [kk])
                    psA, pak = self.psum()
                    for hh in range(2):
                        a0 = t0 + 32 * hh
                        self.mm(psA[32 * hh:32 * hh + 32, 32 * hh:32 * hh + 32], kg[:, a0:a0 + 32], qg[:, a0:a0 + 32],
                                True, True, ["hgkg", "hgqg"], [pak])
                    if dr == 0:
                        ob = (0, 32); kB = XB[:, t0:t0 + 32]; qB = XB[:, t0 + 32:t0 + 64]
                    else:
                        ob = (32, 0); kB = XB[:, t0 + 32:t0 + 64]; qB = XB[:, t0:t0 + 32]
                    self.mm(psA[ob[0]:ob[0] + 32, ob[1]:ob[1] + 32], kB, qB, True, True, ["hgXB"], [pak])
                    for hh in range(2):
                        pp = 32 * hh
                        self.P.op("dve", lambda e, mt=mt, pp=pp, hh=hh, psA=psA, m=msk[dr]: e.copy_predicated(
                            out=mt[pp:pp + 32, 32 * hh:32 * hh + 32], mask=m[pp:pp + 32, :],
                            data=psA[pp:pp + 32, 32 * hh:32 * hh + 32]), [pak, "hgmsk"], [mk])
                    self.cp("act", mt[ob[0]:ob[0] + 32, ob[1]:ob[1] + 32], psA[ob[0]:ob[0] + 32, ob[1]:ob[1] + 32], [pak], [mk])
                    return mt, mk, kt, kk

                def back(ci, c, fr):
                    mt, mk, kt, kk = fr
                    t0 = c * 64
                    g = grp_of(c)
                    bk = bank_of[g]
                    psO, pok = self.ps[bk], "ps%d" % bk
                    gstart = 0 if g < 0 else 4 + 8 * g
                    col = (c - gstart) * 64
                    vch = vtok[0:64, c, cc * 128:(cc + 1) * 128]
                    sb_prev, sbk_prev = Sbs[(ci + 1) % 2], "hgSb%d" % ((ci + 1) % 2)
                    sb_new, sbk_new = Sbs[ci % 2], "hgSb%d" % (ci % 2)
                    psS, psk = self.psum()
                    self.mm(psS[:, 0:128], kt[0:64, :], vch, True, True, [kk, "hgvtok"], [psk])
                    self.mm(psO[:, col:col + 64], sb_prev[:], qG[:, t0:t0 + 64], True, False, [sbk_prev, "hgqG"], [pok])
                    self.mm(psO[:, col:col + 64], vch, mt[0:64, :], False, True, ["hgvtok", mk], [pok])
                    self.stt(Sf[:], Sf[:], dec[:, c:c + 1], psS[:, 0:128], ALU.mult, ALU.add, ["hgSf", "hgdec", psk], ["hgSf"])
                    self.cp("act", sb_new[:], Sf[:], ["hgSf"], [sbk_new])
                    last_of_group = (ci + 1 == len(order)) or (grp_of(order[ci + 1]) != g)
                    if last_of_group:
                        n = 256 if g < 0 else 512
                        tg = gstart * 64
                        if dr == 0:
                            self.cp("act", osum[:, tg:tg + n], psO[:, 0:n], [pok], ["hgos"])
                        else:
                            self.tt("dve", osum[:, tg:tg + n], psO[:, 0:n], osum[:, tg:tg + n], ALU.add, [pok, "hgos"], ["hgos"])

                prev = None
                for ci, c in enumerate(order):
                    fr = front(ci, c)
                    if prev is not None:
                        back(*prev)
                    prev = (ci, c, fr)
                back(*prev)
            self.act(qg[:], osum[:], AF.Square, ["hgos"], ["hgqg"])
            for i in range(9):
                t0 = i * 512
                n = min(512, NT - t0)
                ps, pk = self.psum()
                self.mm(ps[:, :n], self.ones_b[:], qg[:, t0:t0 + n], True, True, ["const", "hgqg"], [pk])
                self.act(Db[:, t0:t0 + n], ps[:, :n], AF.Ln, [pk], ["hgD"], scale=1.0 / 128, bias=self.epsc[:, 0:1])
            self.act(Db[:], Db[:], AF.Exp, ["hgD"], ["hgD"], scale=-0.5)
            self.stt(osum[:], osum[:], self.pc("hg_nw" + L, cc, 1), Db[:], ALU.mult, ALU.mult, ["hgos", "hgD", "pcol"], ["hgos"])
            P.dma("sp", zs[:], S["hg_g"][rows, :], w=["hgzs"])
            self.tt("dve", kg[:], osum[:], zs[:], ALU.mult, ["hgos", "hgzs"], ["hgkg"])
            P.dma(STQ, S["ybuf"][2 * CH + cc * 128:2 * CH + (cc + 1) * 128, :], kg[:], r=["hgkg"], w=["ybuf"])
        P.flush()
    self.ps_rot = list(range(8))


Kern.phase_hg = _phase_hg


def _phase_m2(self, l):
    P = self.P
    S = self.scr
    L = str(l)
    NH = 4
    self.ps_rot = list(range(8))
    with contextlib.ExitStack() as st0:
        ytok = self.sb(st0, "m2ytok", [128, NTILE, 256])
        with contextlib.ExitStack() as st:
            stage = self.sb(st, "m2stage", [128, NT])
            BTb = self.sb(st, "m2BT", [128, NT], BF16)
            CTb = self.sb(st, "m2CT", [128, NT], BF16)
            Btok = self.sb(st, "m2Btok", [128, NTILE, 128], BF16)
            xstok = self.sb(st, "m2xstok", [128, NTILE, 256])
            Xtok = self.sb(st, "m2Xtok", [128, NTILE, 256], BF16)
            Xdtok = self.sb(st, "m2Xdtok", [128, NTILE, 256], BF16)
            dtraw = self.sb(st, "m2dtraw", [128, NTILE, NH])
            dt = self.sb(st, "m2dt", [128, NTILE, NH])
            dtA = self.sb(st, "m2dtA", [128, NTILE, NH])
            Qt = self.sb(st, "m2Q", [128, NTILE, NH])
            Qtot = self.sb(st, "m2Qtot", [128, NTILE, NH])
            Eoff = self.sb(st, "m2Eoff", [128, NTILE, NH])
            decs = self.sb(st, "m2decs", [128, NTILE, NH])
            cdec = self.sb(st, "m2cdec", [128, NTILE, NH])
            hp = self.sb(st, "m2hp", [128, 3, NH])
            Dt = self.sb(st, "m2Dt", [128, 256])
            tri = [self.sb(st, "m2tri", [128, 128]) for _ in range(2)]
            mask = [self.sb(st, "m2mask", [128, 128]) for _ in range(2)]
            GMs = [self.sb(st, "m2GM", [128, 128]) for _ in range(3)]
            Lhs = [self.sb(st, "m2Lh", [128, NH, 128]) for _ in range(3)]
            Exs = [self.sb(st, "m2Ex", [128, NH, 128]) for _ in range(3)]
            MT = [self.sb(st, "m2MT", [128, NH, 128], BF16) for _ in range(3)]
            tmps = [self.sb(st, "m2tmp", [128, 256]) for _ in range(3)]
            tmp = tmps[0]
            stf = self.sb(st, "m2stf", [128, 256])
            stbs = [self.sb(st, "m2stb", [128, 256], BF16) for _ in range(2)]
            P.dma("sp", tri[0][:], self.din["c_tri_incl"], w=["m2tri"])
            P.dma("sp", tri[1][:], self.din["c_tri_excl"], w=["m2tri"])
            P.dma("sp", mask[0][:], self.din["c_m2mask_f"], w=["m2mask"])
            P.dma("sp", mask[1][:], self.din["c_m2mask_b"], w=["m2mask"])
            negm = [self.sb(st, "m2neg", [128, NH, 128]) for _ in range(2)]
            negb = [self.sb(st, "m2negb", [128, NH, 128], BF16) for _ in range(2)]
            trib = [self.sb(st, "m2trib", [128, 128], BF16) for _ in range(2)]
            Rrs = [self.sb(st, "m2Rr", [128, 2 * NH, 128], BF16) for _ in range(3)]
            dtAhl = self.sb(st, "m2dtAhl", [128, NTILE, 2, NH], BF16)
            dtmp = self.sb(st, "m2dtmp", [128, NTILE, NH])
            nQ = self.sb(st, "m2nQ", [128, NTILE, NH])
            for d_ in range(2):
                self.ts("dve", negm[d_][:], mask[d_][:].unsqueeze(1).to_broadcast([128, NH, 128]), -1.0, ALU.add, ["m2mask"], ["m2neg"],
                        s2=(30000.0 if d_ == 0 else -30000.0), op1=ALU.mult)
                self.cp("dve", negb[d_][:], negm[d_][:], ["m2neg"], ["m2neg"])
                self.cp("dve", trib[d_][:], tri[d_][:], ["m2tri"], ["m2tri"])
            P.dma("sp", dtraw[:].rearrange("p t h -> p (t h)"), S["m2_dt"], w=["m2dtraw"])
            b34 = lambda a: a.unsqueeze(1).to_broadcast([128, NTILE, NH])
            for dr in range(2):
                DD = L + str(dr)
                for i, nm in enumerate(("m2_dtb", "m2_alog")):
                    o = self.roff[nm + DD]
                    P.dma("sp", hp[:, i, :], self.din["prow"][:, o:o + NH], w=["m2hp"])
                o = self.roff["m2_d" + DD]
                P.dma("sp", Dt[:], self.din["prow"][:, o:o + 256], w=["m2Dt"])
                self.act(hp[:, 2, :], hp[:, 1, :], AF.Exp, ["m2hp"], ["m2hp"])
                self.ts("dve", hp[:, 2, :], hp[:, 2, :], -1.0, ALU.mult, ["m2hp"], ["m2hp"])
                self.tt("dve", dt[:], dtraw[:], b34(hp[:, 0, :]), ALU.add, ["m2dtraw", "m2hp"], ["m2dt"])
                self.act(dt[:], dt[:], AF.Exp, ["m2dt"], ["m2dt"])
                self.act(dt[:], dt[:], AF.Ln, ["m2dt"], ["m2dt"], bias=self.epsc[:, 1:2])
                self.tt("dve", dtA[:], dt[:], b34(hp[:, 2, :]), ALU.mult, ["m2dt", "m2hp"], ["m2dtA"])
                self.cp("dve", dtAhl[:, :, 0, :], dtA[:], ["m2dtA"], ["m2dtAhl"])
                self.tt("dve", dtmp[:], dtA[:], dtAhl[:, :, 0, :], ALU.subtract, ["m2dtA", "m2dtAhl"], ["m2dtmp"])
                self.cp("dve", dtAhl[:, :, 1, :], dtmp[:], ["m2dtmp"], ["m2dtAhl"])
                dtA2 = dtA[:].rearrange("p t h -> p (t h)")
                psq, pqk = self.psum()
                self.mm(psq[:, 0:NTILE * NH], tri[0][:], dtA2, True, True, ["m2tri", "m2dtA"], [pqk])
                pst, ptk = self.psum()
                self.mm(pst[:, 0:NTILE * NH], self.ones_f[:], dtA2, True, True, ["const", "m2dtA"], [ptk])
                Q2 = Qt[:].rearrange("p t h -> p (t h)")
                T2 = Qtot[:].rearrange("p t h -> p (t h)")
                self.cp("dve", T2, pst[:, 0:NTILE * NH], [ptk], ["m2Qtot"])
                if dr == 0:
                    self.cp("dve", Q2, psq[:, 0:NTILE * NH], [pqk], ["m2Q"])
                    self.act(Eoff[:], Qt[:], AF.Exp, ["m2Q"], ["m2Eoff"])
                    self.tt("dve", decs[:], Qtot[:], Qt[:], ALU.subtract, ["m2Q", "m2Qtot"], ["m2decs"])
                    self.act(decs[:], decs[:], AF.Exp, ["m2decs"], ["m2decs"])
                else:
                    self.tt("dve", Q2, psq[:, 0:NTILE * NH], dtA2, ALU.subtract, [pqk, "m2dtA"], ["m2Q"])
                    self.tt("dve", Eoff[:], Qtot[:], Qt[:], ALU.subtract, ["m2Q", "m2Qtot"], ["m2Eoff"])
                    self.act(Eoff[:], Eoff[:], AF.Exp, ["m2Eoff"], ["m2Eoff"])
                    self.act(decs[:], Qt[:], AF.Exp, ["m2Q"], ["m2decs"])
                self.act(cdec[:], Qtot[:], AF.Exp, ["m2Qtot"], ["m2cdec"])
                self.ts("dve", nQ[:], Qt[:], (-1.0 if dr == 0 else 1.0), ALU.mult, ["m2Q"], ["m2Q"])
                for ai, (r0, dstt, ck) in enumerate(((0, xstok, 0), (128, xstok, 1), (256, Btok, None), (384, None, None))):
                    P.dma("sp", stage[:], S["m2_x"][dr, r0:r0 + 128, :], w=["m2stage"])
                    if r0 == 256:
                        self.cp("pool", BTb[:], stage[:], ["m2stage"], ["m2BT"])
                    if r0 == 384:
                        self.cp("pool", CTb[:], stage[:], ["m2stage"], ["m2CT"])
                        continue
                    for g0 in range(0, NTILE, 4):
                        ng = min(4, NTILE - g0)
                        ps, pk = self.psum()
                        for i in range(ng):
                            self.tr(ps[:, i * 128:(i + 1) * 128], stage[:, (g0 + i) * 128:(g0 + i + 1) * 128], ["m2stage"], [pk])
                        src = ps[:, 0:ng * 128].rearrange("p (t c) -> p t c", c=128)
                        if dstt is xstok:
                            self.cp("act" if (g0 // 4) % 2 else "dve", xstok[:, g0:g0 + ng, ck * 128:(ck + 1) * 128], src, [pk], ["m2xstok"])
                        else:
                            self.cp("act" if (g0 // 4) % 2 else "dve", Btok[:, g0:g0 + ng, :], src, [pk], ["m2Btok"])
                xs4 = xstok[:].rearrange("p t (h q) -> p t h q", q=64)
                b64 = lambda a: a.unsqueeze(3).to_broadcast([128, NTILE, NH, 64])
                self.tt("dve", Xtok[:].rearrange("p t (h q) -> p t h q", q=64), xs4, b64(dt[:]), ALU.mult,
                        ["m2xstok", "m2dt"], ["m2Xtok"])
                self.tt("dve", decs[:], decs[:], dt[:], ALU.mult, ["m2decs", "m2dt"], ["m2decs"])
                self.tt("pool", Xdtok[:].rearrange("p t (h q) -> p t h q", q=64), xs4, b64(decs[:]), ALU.mult,
                        ["m2xstok", "m2decs"], ["m2Xdtok"])
                Db = Dt[:].unsqueeze(1).to_broadcast([128, NTILE, 256])
                if dr == 0:
                    self.tt("dve", ytok[:], xstok[:], Db, ALU.mult, ["m2xstok", "m2Dt"], ["m2ytok"])
                else:
                    self.tt("dve", xstok[:], xstok[:], Db, ALU.mult, ["m2xstok", "m2Dt"], ["m2xstok"])
                    self.tt("pool", ytok[:], ytok[:], xstok[:], ALU.add, ["m2xstok", "m2ytok"], ["m2ytok"])
                self.memset("dve", stf[:], 0.0, ["m2stf"])
                for i in range(2):
                    self.memset("pool", stbs[i][:], 0.0, ["m2stb%d" % i])
                order = list(range(NTILE)) if dr == 0 else [1, 0] + list(range(NTILE - 1, 1, -1))
                def front_a(ci, j):
                    tk = slice(j * 128, (j + 1) * 128)
                    Rr, rrk = Rrs[ci % 3], "m2Rr%d" % (ci % 3)
                    self.tt("dve", Rr[:].rearrange("p (a h) t -> p a h t", a=2),
                            trib[dr][:].unsqueeze(1).unsqueeze(1).to_broadcast([128, 2, NH, 128]),
                            dtAhl[:, j, :, :].unsqueeze(3).to_broadcast([128, 2, NH, 128]), ALU.mult, ["m2tri", "m2dtAhl"], [rrk])
                    psG, pgk = self.psum()
                    self.mm(psG[:, 0:128], BTb[:, tk], CTb[:, tk], True, True, ["m2BT", "m2CT"], [pgk])
                    psL, plk = self.psum()
                    R2 = Rr[:].rearrange("p g t -> p (g t)")
                    self.mm(psL[:, :], self.ones_b[:], R2[:, 0:512], True, False, ["const", rrk], [plk])
                    self.mm(psL[:, :], self.ones_b[:], R2[:, 512:1024], False, False, ["const", rrk], [plk])
                    self.mm(psL[:, :], self.ident_b[:], negb[dr][:].rearrange("p h t -> p (h t)"), False, True, ["const", "m2neg"], [plk])
                    return psG, pgk, psL, plk

                def front_b(ci, j, fa):
                    psG, pgk, psL, plk = fa
                    mt, mk = MT[ci % 3], "m2MT%d" % (ci % 3)
                    Ex, exk = Exs[ci % 3], "m2Ex%d" % (ci % 3)
                    for h in range(NH):
                        self.act(Ex[:, h, :], psL[:, h * 128:(h + 1) * 128], AF.Exp, [plk, "m2Q"], [exk],
                                 scale=(1.0 if dr == 0 else -1.0), bias=nQ[:, j, h:h + 1])
                    self.tt("dve", mt[:], Ex[:], psG[:, 0:128].unsqueeze(1).to_broadcast([128, NH, 128]), ALU.mult,
                            [exk, pgk], [mk])
                    return mt, mk

                def back(ci, j, fr):
                    mt, mk = fr
                    tk = slice(j * 128, (j + 1) * 128)
                    tmp, tmk = tmps[ci % 3], "m2tmp%d" % (ci % 3)
                    st_prev, sk_prev = stbs[(ci + 1) % 2], "m2stb%d" % ((ci + 1) % 2)
                    st_new, sk_new = stbs[ci % 2], "m2stb%d" % (ci % 2)
                    psS, psk = self.psum()
                    self.mm(psS[:, 0:256], Btok[:, j, :], Xdtok[:, j, :], True, True, ["m2Btok", "m2Xdtok"], [psk])
                    psO, pok = self.psum()
                    self.mm(psO[:, 0:256], CTb[:, tk], st_prev[:], True, True, ["m2CT", sk_prev], [pok])
                    for h in range(NH):
                        self.mm(psO[:, 256 + h * 64:256 + (h + 1) * 64], mt[:, h, :], Xtok[:, j, h * 64:(h + 1) * 64], True, True,
                                [mk, "m2Xtok"], [pok])
                    self.tt("dve", stf[:].rearrange("p (h q) -> p h q", q=64), stf[:].rearrange("p (h q) -> p h q", q=64),
                            cdec[:, j, :].unsqueeze(2).to_broadcast([128, NH, 64]), ALU.mult, ["m2stf", "m2cdec"], ["m2stf"])
                    self.tt("dve", stf[:], psS[:, 0:256], stf[:], ALU.add, [psk, "m2stf"], ["m2stf"])
                    self.cp("act", st_new[:], stf[:], ["m2stf"], [sk_new])
                    yk = ("m2ytok", j)
                    self.tt("dve", tmp[:].rearrange("p (h q) -> p h q", q=64), psO[:, 0:256].rearrange("p (h q) -> p h q", q=64),
                            Eoff[:, j, :].unsqueeze(2).to_broadcast([128, NH, 64]), ALU.mult, [pok, "m2Eoff"], [tmk])
                    self.tt("dve", ytok[:, j, :], psO[:, 256:512], ytok[:, j, :], ALU.add, [pok, "m2ytok", yk], [yk])
                    self.tt("pool", ytok[:, j, :], ytok[:, j, :], tmp[:], ALU.add, [tmk, yk], [yk])

                nO = len(order)
                fas = {}
                frs = {}
                for it in range(nO + 2):
                    if it < nO:
                        fas[it] = front_a(it, order[it])
                    if 0 <= it - 1 < nO:
                        frs[it - 1] = front_b(it - 1, order[it - 1], fas.pop(it - 1))
                    if 0 <= it - 2 < nO:
                        back(it - 2, order[it - 2], frs.pop(it - 2))
                self.P.op("pool", lambda e: e.memset(tmp[:, 0:1], 0.0), [("m2ytok", j) for j in range(NTILE)] + ["m2tmp0"], ["m2ytok", "m2tmp0"])
            P.flush()
        with contextlib.ExitStack() as st:
            yT = [self.sb(st, "m2yT", [128, NT]) for _ in range(2)]
            gt = self.sb(st, "m2gt", [128, NT])
            sq = [self.sb(st, "m2sq", [128, NT], BF16) for _ in range(2)]
            rs = self.sb(st, "m2rs", [128, NT])
            yb = self.sb(st, "m2yb", [128, NT], BF16)
            for cc in range(2):
                yk = "m2yT%d" % cc
                for g0 in [0] + list(range(2, NTILE, 4)):
                    ng = 2 if g0 == 0 else 4
                    ps, pk = self.psum()
                    for i in range(ng):
                        self.tr(ps[:, i * 128:(i + 1) * 128], ytok[:, g0 + i, cc * 128:(cc + 1) * 128], ["m2ytok"], [pk])
                    if g0 == 0:
                        self.cp("act", yT[cc][:, 0:CTX], ps[:, 0:256], [pk], [yk])
                    else:
                        gi = (g0 - 2) // 4
                        dst = yT[cc][:, CTX:NT].rearrange("p (r w) -> p w r", w=64)[:, 8 * gi:8 * gi + 8, :]
                        self.cp("act" if gi % 2 else "dve", dst, ps[:, 0:512].rearrange("p (w r) -> p w r", r=64), [pk], [yk])
                P.dma("sp", gt[:], S["m2_g"][cc * 128:(cc + 1) * 128, :], w=["m2gt"])
                self.tt("dve", yT[cc][:], yT[cc][:], gt[:], ALU.mult, [yk, "m2gt"], [yk])
                self.act(sq[cc][:], yT[cc][:], AF.Square, [yk], ["m2sq%d" % cc])
            for i in range(9):
                t0 = i * 512
                n = min(512, NT - t0)
                ps, pk = self.psum()
                for cc in range(2):
                    self.mm(ps[:, :n], self.ones_b[:], sq[cc][:, t0:t0 + n], cc == 0, cc == 1, ["const", "m2sq%d" % cc], [pk])
                self.act(rs[:, t0:t0 + n], ps[:, :n], AF.Ln, [pk], ["m2rs"], scale=1.0 / 256, bias=self.epsc[:, 0:1])
            self.act(rs[:], rs[:], AF.Exp, ["m2rs"], ["m2rs"], scale=-0.5)
            for cc in range(2):
                self.stt(yT[cc][:], yT[cc][:], self.pc("m2_nw" + L, cc, 1), rs[:], ALU.mult, ALU.mult,
                         ["m2yT%d" % cc, "m2rs", "pcol"], ["m2yT%d" % cc])
                self.cp("pool", yb[:], yT[cc][:], ["m2yT%d" % cc], ["m2yb"])
                P.dma(STQ, S["ybuf"][3 * CH + cc * 128:3 * CH + (cc + 1) * 128, :], yb[:], r=["m2yb"], w=["ybuf"])
            P.flush()


Kern.phase_m2 = _phase_m2


MAGIC = 12582912.0


def _phase_hy(self, l, n, tag, tok0):
    P = self.P
    S = self.scr
    L = str(l)
    NB = n // 128
    tabC = self.din["tabC_" + tag]
    tabS = self.din["tabS_" + tag]
    khat = S["khat_" + tag]
    FG = min(8, NB)
    FGK = min(4, NB)
    TBG = min(4, NB)
    TW = min(2048, n)
    CW = min(512, n)
    CQ = CW
    NQ = TW // CW
    with contextlib.ExitStack() as st0:
        HS = self.sb(st0, "hyHS", [128, NB, 512], BF16)
        HD = self.sb(st0, "hyHD", [128, NB, 512], BF16)
        invn = self.sb(st0, "hyinvn", [128, 512])
        NTB = 16 if False else 8
        tabb = [self.sb(st0, "hytab", [128, TBG * CQ], BF16) for _ in range(NTB)]
        tcnt = [0]

        def tab_load(tab, lg, fq):
            i = tcnt[0] % NTB; tcnt[0] += 1
            tk = "hytab%d" % i
            dst = tabb[i][:].rearrange("p (j c) -> p j c", c=CQ)
            P.dma("sp", dst, tab[lg, fq], w=[tk])
            return dst, tk

        with contextlib.ExitStack() as st:
            zT = self.sb(st, "hyzT", [HY_EMB, n + 1])
            h1 = self.sb(st, "hyh1", [64, n + 1])
            h2 = self.sb(st, "hyh2", [64, n + 1])
            w1 = self.sb(st, "hyw1", [HY_EMB, 64])
            w2 = self.sb(st, "hyw2", [64, 64])
            w3 = self.sb(st, "hyw3", [64, 2, 512])
            frb = self.sb(st, "hyfrb", [64, 2])
            delta = self.sb(st, "hydelta", [128, 256])
            arg = self.sb(st, "hyarg", [64, 512])
            tq = self.sb(st, "hytq", [64, 512])
            NR = 3
            decs = [[self.sb(st, "hydec", [128, 256]) for _ in range(2)] for _ in range(NR)]
            hfbs = [[self.sb(st, "hyhfb", [128, 512]) for _ in range(2)] for _ in range(NR)]
            abs_ = [[self.sb(st, "hyab", [128, 512]) for _ in range(2)] for _ in range(NR)]
            abbs = [self.sb(st, "hyabb", [128, 512], BF16) for _ in range(NR)]
            P.dma("sp", zT[:], self.din["z_" + tag], w=["hyzT"])
            P.dma("sp", w1[:], self.din["hy_w1"][l], w=["hyw"])
            P.dma("sp", w2[:], self.din["hy_w2"][l], w=["hyw"])
            P.dma("sp", w3[:], self.din["hy_w3"][l], w=["hyw"])
            o = self.roff["hy_delta"]
            P.dma("sp", delta[:], self.din["prow"][:, o:o + 256], w=["hydelta"])
            fr = self.pc("hy_fr" + L)[0:64, :]
            self.tt("dve", frb[:, 0:1], self.pc("hy_b1" + L)[0:64, :], fr, ALU.mult, ["pcol"], ["hyfrb"])
            self.tt("dve", frb[:, 1:2], self.pc("hy_b2" + L)[0:64, :], fr, ALU.mult, ["pcol"], ["hyfrb"])
            for li, (wt, kdim, src, dst, dk) in enumerate(((w1, HY_EMB, zT, h1, "hyh1"), (w2, 64, h1, h2, "hyh2"))):
                sk = "hyzT" if li == 0 else "hyh1"
                for c0 in range(0, n, 512):
                    m = min(512, n - c0)
                    ps, pk = self.psum()
                    self.mm(ps[0:64, :m], wt[0:kdim, :], src[0:kdim, c0:c0 + m], True, True, ["hyw", sk], [pk])
                    self.ts("dve", arg[:, :m], ps[0:64, :m], fr, ALU.mult, [pk, "pcol", "hyfrb"], ["hyarg"],
                            s2=frb[:, li:li + 1], op1=ALU.add)
                    self.ts("dve", tq[:, :m], arg[:, :m], 1.0 / (2 * PI), ALU.mult, ["hyarg"], ["hytq"], s2=MAGIC, op1=ALU.add)
                    self.ts("dve", tq[:, :m], tq[:, :m], -MAGIC, ALU.add, ["hytq"], ["hytq"], s2=-2 * PI, op1=ALU.mult)
                    self.tt("dve", arg[:, :m], arg[:, :m], tq[:, :m], ALU.add, ["hyarg", "hytq"], ["hyarg"])
                    self.ts("dve", arg[:, :m], arg[:, :m], 3.141592, ALU.min, ["hyarg"], ["hyarg"], s2=-3.141592, op1=ALU.max)
                    self.act(dst[:, c0:c0 + m], arg[:, :m], AF.Sin, ["hyarg"], [dk])
            self.memset("dve", h2[:, n:n + 1], 0.0, ["hyh2"])
            psN, pnk = self.ps[7], "ps7"
            self.ps_rot = [0, 1, 2, 3, 4, 5, 6]
            for lb in range(NB):
                r3 = lb % NR
                dec, hfb, ab, abb = decs[r3], hfbs[r3], abs_[r3], abbs[r3]
                sf = str(r3)
                pF, pfk = self.psum()
                pB, pbk = self.psum()
                self.mm(pF[:, :], h2[:, lb * 128:lb * 128 + 128], w3[:, 0, :], True, True, ["hyh2", "hyw"], [pfk])
                self.mm(pB[:, :], h2[:, lb * 128 + 1:lb * 128 + 129], w3[:, 1, :], True, True, ["hyh2", "hyw"], [pbk])
                self.act(dec[0][:], delta[:], AF.Exp, ["hydelta", "pcol"], ["hydec0" + sf], scale=self.pc("negt_f" + tag, lb, 1))
                self.act(dec[1][:], delta[:], AF.Exp, ["hydelta", "pcol"], ["hydec1" + sf], scale=self.pc("negt_b" + tag, lb, 1))
                v3 = lambda a: a.rearrange("p (o c) -> p o c", c=256)
                bo = lambda a: a.unsqueeze(1).to_broadcast([128, 2, 256])
                self.tt("dve", v3(hfb[0][:]), v3(pF[:, :]), bo(dec[0][:]), ALU.mult, [pfk, "hydec0" + sf], ["hyhf" + sf])
                self.tt("dve", v3(hfb[1][:]), v3(pB[:, :]), bo(dec[1][:]), ALU.mult, [pbk, "hydec1" + sf], ["hyhb" + sf])
                self.tt("pool", HS[:, lb, :], hfb[0][:], hfb[1][:], ALU.add, ["hyhf" + sf, "hyhb" + sf], [("hyHS", lb)])
                self.tt("dve", HD[:, lb, :], hfb[0][:], hfb[1][:], ALU.subtract, ["hyhf" + sf, "hyhb" + sf], [("hyHS", lb)])
                self.act(ab[0][:], hfb[0][:], AF.Abs, ["hyhf" + sf], ["hyab0" + sf])
                self.act(ab[1][:], hfb[1][:], AF.Abs, ["hyhb" + sf], ["hyab1" + sf])
                self.tt("pool", abb[:], ab[0][:], ab[1][:], ALU.add, ["hyab0" + sf, "hyab1" + sf], ["hyabb" + sf])
                self.mm(psN[:, :], self.ones_b[:], abb[:], lb == 0, lb == NB - 1, ["const", "hyabb" + sf], [pnk])
            self.ts("dve", invn[:], psN[:, :], HY_EPS_, ALU.add, [pnk], ["hyinvn"])
            self.P.op("dve", lambda e: e.reciprocal(out=invn[:], in_=invn[:]), ["hyinvn"], ["hyinvn"])
            self.ts("dve", invn[:], invn[:], 1.0 / n, ALU.mult, ["hyinvn"], ["hyinvn"])
            P.flush()
        self.ps_rot = list(range(8))
        with contextlib.ExitStack() as st:
            ka = [self.sb(st, "hyka", [128, 512]) for _ in range(2)]
            kb = [self.sb(st, "hykb", [128, 512]) for _ in range(2)]
            kr = [self.sb(st, "hykr", [128, 512]) for _ in range(2)]
            ki = [self.sb(st, "hyki", [128, 512]) for _ in range(2)]
            it = 0
            for fg in range(NB // FGK):
                for lg in range(NB // TBG):
                    Ct, ck = tab_load(tabC, lg, fg)
                    St, sk = tab_load(tabS, lg, fg)
                    for j in range(TBG):
                        lb = lg * TBG + j
                        for fb in range(FGK):
                            self.mm(self.ps[fb][:, :], Ct[:, j, fb * 128:(fb + 1) * 128], HS[:, lb, :], lb == 0, lb == NB - 1,
                                    [ck, ("hyHS", lb)], ["ps%d" % fb])
                            self.mm(self.ps[4 + fb][:, :], St[:, j, fb * 128:(fb + 1) * 128], HD[:, lb, :], lb == 0, lb == NB - 1,
                                    [sk, ("hyHS", lb)], ["ps%d" % (4 + fb)])
                for fb in range(FGK):
                    F = fg * FGK + fb
                    i2 = it % 2; it += 1
                    ch = self.pc("chalf" + tag, F, 1)
                    sh = self.pc("shalf" + tag, F, 1)
                    self.tt("dve", ka[i2][:], self.ps[fb][:, :], invn[:], ALU.mult, ["ps%d" % fb, "hyinvn"], ["hyka%d" % i2])
                    self.tt("dve", kb[i2][:], self.ps[4 + fb][:, :], invn[:], ALU.mult, ["ps%d" % (4 + fb), "hyinvn"], ["hykb%d" % i2])
                    self.act(kr[i2][:], ka[i2][:], AF.Copy, ["hyka%d" % i2, "pcol"], ["hykr%d" % i2], scale=ch)
                    self.stt(kr[i2][:], kb[i2][:], sh, kr[i2][:], ALU.mult, ALU.add, ["hykb%d" % i2, "hykr%d" % i2, "pcol"], ["hykr%d" % i2])
                    self.act(ki[i2][:], kb[i2][:], AF.Copy, ["hykb%d" % i2, "pcol"], ["hyki%d" % i2], scale=ch)
                    self.stt(ki[i2][:], ka[i2][:], sh, ki[i2][:], ALU.mult, ALU.subtract, ["hyka%d" % i2, "hyki%d" % i2, "pcol"], ["hyki%d" % i2])
                    P.dma(STQ, khat[F, 0], kr[i2][:], r=["hykr%d" % i2], w=["khat"])
                    P.dma(STQ, khat[F, 1], ki[i2][:], r=["hyki%d" % i2], w=["khat"])
            P.flush()
    with contextlib.ExitStack() as st:
        NTB = 16 if True else 8
        tabb = [self.sb(st, "hytab", [128, TBG * CQ], BF16) for _ in range(NTB)]
        tcnt = [0]

        def tab_load(tab, lg, fq):
            i = tcnt[0] % NTB; tcnt[0] += 1
            tk = "hytab%d" % i
            dst = tabb[i][:].rearrange("p (j c) -> p j c", c=CQ)
            P.dma("sp", dst, tab[lg, fq], w=[tk])
            return dst, tk
        zT = self.sb(st, "hyz", [128, 2, n])
        ztok = self.sb(st, "hyztok", [128, NB, 256], BF16)
        Yh = self.sb(st, "hyYh", [128, NB, 2, 256], BF16)
        Kt = [self.sb(st, "hyKt", [128, 2, 256]) for _ in range(3)]
        ta = [self.sb(st, "hyta", [128, 256]) for _ in range(2)]
        tb_ = [self.sb(st, "hytb", [128, 256]) for _ in range(2)]
        xs = [self.sb(st, "hyxs", [128, 512]) for _ in range(4)]
        te = [self.sb(st, "hyte", [128, 512]) for _ in range(2)]
        yo = self.sb(st, "hyyo", [128, n], BF16)
        for cc in range(2):
            P.dma("sp", zT[:, cc, :], S["hy_u"][0, cc * 128:(cc + 1) * 128, tok0:tok0 + n], w=[("hyz", cc)])
        kcnt = 0
        xcnt = 0
        for o in range(2):
            for tbp in range(0, NB, 2):
                ps, pk = self.psum()
                for i in range(2):
                    for cc in range(2):
                        self.tr(ps[:, (i * 2 + cc) * 128:(i * 2 + cc + 1) * 128], zT[:, cc, (tbp + i) * 128:(tbp + i + 1) * 128],
                                [("hyz", cc)], [pk])
                self.cp("act" if (tbp // 2) % 2 else "dve", ztok[:, tbp:tbp + 2, :], ps[:, :].rearrange("p (t c) -> p t c", c=256),
                        [pk], [("hyztok", tbp // 2)])
            TPF = (FG * 128) // CQ
            for fg in range(NB // FG):
                for lg in range(NB // TBG):
                    tl = []
                    for q in range(TPF):
                        tl.append((tab_load(tabC, lg, fg * TPF + q), tab_load(tabS, lg, fg * TPF + q)))
                    for j in range(TBG):
                        tb = lg * TBG + j
                        zk = ("hyztok", tb // 2)
                        for fb in range(FG):
                            (Ct, ck), (St, sk) = tl[(fb * 128) // CQ]
                            cc0 = (fb * 128) % CQ
                            self.mm(self.ps[fb][:, 0:256], Ct[:, j, cc0:cc0 + 128], ztok[:, tb, :], tb == 0, False,
                                    [ck, zk], ["ps%d" % fb])
                            self.mm(self.ps[fb][:, 256:512], St[:, j, cc0:cc0 + 128], ztok[:, tb, :], False, tb == NB - 1,
                                    [sk, zk], ["ps%d" % fb])
                for fb in range(FG):
                    F = fg * FG + fb
                    kt = Kt[kcnt % 3]; kk = "hyKt%d" % (kcnt % 3); kcnt += 1
                    i2 = F % 2
                    P.dma("sp", kt[:], khat[F, :, :, o * 256:(o + 1) * 256].rearrange("r p c -> p r c"), w=[kk])
                    M1 = self.ps[fb][:, 0:256]
                    M2 = self.ps[fb][:, 256:512]
                    pk = "ps%d" % fb
                    self.tt("dve", ta[i2][:], M1, kt[:, 0, :], ALU.mult, [pk, kk], ["hyta%d" % i2])
                    self.tt("dve", tb_[i2][:], M2, kt[:, 1, :], ALU.mult, [pk, kk], ["hytb%d" % i2])
                    self.tt("pool", Yh[:, F, 0, :], ta[i2][:], tb_[i2][:], ALU.add, ["hyta%d" % i2, "hytb%d" % i2], [("hyYh", F)])
                    self.tt("dve", ta[i2][:], M2, kt[:, 0, :], ALU.mult, [pk, kk, ("hyYh", F)], ["hyta%d" % i2])
                    self.tt("dve", tb_[i2][:], M1, kt[:, 1, :], ALU.mult, [pk, kk, ("hyYh", F)], ["hytb%d" % i2])
                    self.tt("pool", Yh[:, F, 1, :], ta[i2][:], tb_[i2][:], ALU.subtract, ["hyta%d" % i2, "hytb%d" % i2], [("hyYh", F)])
            for th in range(n // TW):
                w0 = th * TW
                for lg in range(NB // TBG):
                    tl = [(tab_load(tabC, lg, th * NQ + q), tab_load(tabS, lg, th * NQ + q)) for q in range(NQ)]
                    for j in range(TBG):
                        fb = lg * TBG + j
                        for cc in range(2):
                            for q in range(NQ):
                                bank = cc * NQ + q
                                (Ct, ck), (St, sk) = tl[q]
                                self.mm(self.ps[bank][:, 0:CW], Yh[:, fb, 0, cc * 128:(cc + 1) * 128], Ct[:, j, :],
                                        fb == 0, False, [ck, ("hyYh", fb)], ["ps%d" % bank])
                                self.mm(self.ps[bank][:, 0:CW], Yh[:, fb, 1, cc * 128:(cc + 1) * 128], St[:, j, :],
                                        False, fb == NB - 1, [sk, ("hyYh", fb)], ["ps%d" % bank])
                for cc in range(2):
                    for q in range(NQ):
                        bank = cc * NQ + q
                        t0 = w0 + q * CW
                        x = xs[xcnt % 4]; xk = "hyxs%d" % (xcnt % 4)
                        tt_ = te[xcnt % 2]; tk = "hyte%d" % (xcnt % 2); xcnt += 1
                        P.dma("sp", x[:, :CW], S["hy_u"][1 + o, cc * 128:(cc + 1) * 128, tok0 + t0:tok0 + t0 + CW], w=[xk])
                        self.stt(tt_[:, :CW], zT[:, cc, t0:t0 + CW], self.pc("hy_skip" + L, o * 2 + cc, 1), self.ps[bank][:, 0:CW],
                                 ALU.mult, ALU.add, [("hyz", cc), "ps%d" % bank, "pcol"], [tk])
                        self.tt("pool", zT[:, cc, t0:t0 + CW], tt_[:, :CW], x[:, :CW], ALU.mult, [tk, xk], [("hyz", cc)])
        for cc in range(2):
            for q in range(n // CW):
                t0 = q * CW
                x = xs[xcnt % 4]; xk = "hyxs%d" % (xcnt % 4); xcnt += 1
                P.dma("sp", x[:, :CW], S["hy_g"][cc * 128:(cc + 1) * 128, tok0 + t0:tok0 + t0 + CW], w=[xk])
                self.tt("dve", yo[:, t0:t0 + CW], zT[:, cc, t0:t0 + CW], x[:, :CW], ALU.mult, [("hyz", cc), xk], ["hyyo"])
            P.dma(STQ, S["ybuf"][cc * 128:(cc + 1) * 128, tok0:tok0 + n], yo[:], r=["hyyo"], w=["ybuf"])
        P.flush()


HY_EPS_ = 1e-6
Kern.phase_hy1 = _phase_hy


def _phase_hy_all(self, l, need_ctx=True):
    self.phase_hy1(l, SEQ, "L", CTX)
    if need_ctx:
        self.phase_hy1(l, CTX, "C", 0)


Kern.phase_hy = _phase_hy_all


def _phase_out(self, l, yf, xsrc, tiles, xdst=None, final_dst=None, lat_only=False):
    P = self.P
    S = self.scr
    w_out = self.din["w_out"]
    with contextlib.ExitStack() as st:
        wst = [self.sb(st, "wost", [128, 1024]) for _ in range(2)]
        wob = self.sb(st, "wob", [128, 16, 1024], BF16)
        gb = self.sb(st, "ogb", [128, 2, 1024])
        fw = self.sb(st, "ofw", [128, 1024])
        yt = [self.sb(st, "oyt", [128, 16, 512], BF16) for _ in range(2)]
        xo = [self.sb(st, "oxo", [128, 1024]) for _ in range(5)]
        tm = [self.sb(st, "otm", [128, 1024]) for _ in range(4)]
        junks = [self.sb(st, "ojunk", [128, 1024]) for _ in range(2)]
        ssqs = self.sb(st, "ossq", [128, 8])
        for kc in range(16):
            if self.mode == "ALL":
                j, hh = kc // 2, kc % 2
                br, cc = j // 2, j % 2
            else:
                hh, br, cc = kc // 8, (kc % 8) // 2, kc % 2
            r0 = br * 512 + hh * 256 + cc * 128
            P.dma("sp", wst[kc % 2][:], w_out[l, r0:r0 + 128, :], w=["wost%d" % (kc % 2)])
            self.cp("act" if kc % 2 else "dve", wob[:, kc, :], wst[kc % 2][:], ["wost%d" % (kc % 2)], ["wob"])
        for s in range(2):
            P.dma("sp", gb[:, s, :], S["gbc"][l, s], w=["ogb"])
        o = self.roff["final_w"]
        P.dma("sp", fw[:], self.din["prow"][:, o:o + 1024], w=["ofw"])
        yfv = yf.rearrange("(kc p) t -> p kc t", p=128)
        chunks = {}
        for ti in tiles:
            t0 = ti * 128
            if lat_only:
                ck = t0 // 512
            else:
                ck = -1 if t0 < CTX else (t0 - CTX) // 512
            chunks.setdefault(ck, []).append(ti)
        it = 0
        xi = 0
        for ck, tl in chunks.items():
            if lat_only:
                c0, cw = ck * 512, 512
            else:
                c0 = 0 if ck < 0 else CTX + ck * 512
                cw = CTX if ck < 0 else 512
            ybuf = yt[it % 2]; yk = "oyt%d" % (it % 2); it += 1
            P.dma("sp", ybuf[:, :, :cw], yfv[:, :, c0:c0 + cw], w=[yk])
            for ti in tl:
                off = ti * 128 - c0
                s = 1 if (ti < 2 and not lat_only) else 0
                pa, pak = self.psum()
                pb, pbk = self.psum()
                for kc in range(16):
                    self.mm(pa[:, :], ybuf[:, kc, off:off + 128], wob[:, kc, 0:512], kc == 0, kc == 15, [yk, "wob"], [pak])
                    self.mm(pb[:, :], ybuf[:, kc, off:off + 128], wob[:, kc, 512:1024], kc == 0, kc == 15, [yk, "wob"], [pbk])
                x = xo[xi % 5]; xk = "oxo%d" % (xi % 5)
                t = tm[xi % 4]; tk = "otm%d" % (xi % 4)
                junk = junks[xi % 2]; jk = "ojunk%d" % (xi % 2)
                ssq = ssqs[:, 2 * (xi % 4):2 * (xi % 4) + 2]; sqk = "ossq%d" % (xi % 4); xi += 1
                P.dma("sp", x[:], xsrc(ti), w=[xk])
                self.tt("dve", t[:, 0:512], pa[:, :], gb[:, s, 0:512], ALU.mult, [pak, "ogb"], [tk])
                self.tt("dve", t[:, 512:1024], pb[:, :], gb[:, s, 512:1024], ALU.mult, [pbk, "ogb"], [tk])
                self.tt("dve", x[:], x[:], t[:], ALU.add, [xk, tk], [xk])
                if xdst is not None:
                    for dst in xdst(ti):
                        P.dma(STQ, dst, x[:], r=[xk], w=["xres"])
                if final_dst is not None:
                    self.P.op("act", lambda e, x=x, junk=junk, ssq=ssq: e.activation(out=junk[:], in_=x[:], func=AF.Square,
                                                                                   accum_out=ssq[:, 0:1]), [xk], [jk, sqk])
                    self.act(ssq[:, 1:2], ssq[:, 0:1], AF.Ln, [sqk], [sqk], scale=1.0 / D, bias=self.epsc[:, 0:1])
                    self.act(ssq[:, 1:2], ssq[:, 1:2], AF.Exp, [sqk], [sqk], scale=-0.5)
                    self.stt(t[:], x[:], ssq[:, 1:2], fw[:], ALU.mult, ALU.mult, [xk, sqk, "ofw", tk], [tk])
                    P.dma(STQ, final_dst(ti), t[:], r=[tk], w=["final"])
        P.flush()


Kern.phase_out = _phase_out


_PROG_CACHE = {}


def _launch(mode, in_maps, poff, roff, npc, npr):
    nc = build_program(mode, poff, roff, npc, npr)
    res = run_bass_kernel_spmd(nc, in_maps, core_ids=list(range(8)))
    return res.results


MIX_KEYS = ("w_in", "rg_bd", "hy_w1", "hy_w2", "hy_w3", "c_tri_incl", "c_tri_excl", "c_hgmask_f", "c_hgmask_b",
            "c_m2mask_f", "c_m2mask_b", "tabC_L", "tabS_L", "tabC_C", "tabS_C", "z_L", "z_C")
BASE_KEYS = ("w_mod", "w_out", "pcol", "prow", "c_ident", "c_ones", "c_sel")


def kernel(**inputs):
    consts = const_inputs()
    preps = []
    for b in range(4):
        for h in range(2):
            d, poff, roff = host_prep(inputs, b, h)
            d.update(consts)
            preps.append(d)
    npc = preps[0]["pcol"].shape[1]
    npr = preps[0]["prow"].shape[1]
    maps = [{k: d[k] for k in BASE_KEYS + MIX_KEYS + ("x", "ctx")} for d in preps]
    res = _launch("ALL", maps, poff, roff, npc, npr)
    out = np.empty((4, SEQ, D), np.float32)
    for b in range(4):
        out[b] = res[2 * b]["out"]
    return out
```

```python
import contextlib
import math
import numpy as np
import ml_dtypes
import concourse.bass as bass
import concourse.mybir as mybir
from concourse.bass_utils import run_bass_kernel_spmd

F32 = mybir.dt.float32
BF16 = mybir.dt.bfloat16
I32 = mybir.dt.int32
AF = mybir.ActivationFunctionType
ALU = mybir.AluOpType
AX = mybir.AxisListType

D = 1024
SEQ = 4096
CTX = 256
NT = SEQ + CTX
NTILE = NT // 128
DEPTH = 2
W_BR = 512
CH = 256
IN_COLS = 7176
NCOL = 3588
EPS = 1e-6
HY_HID = 64
HY_EMB = 33
PI = math.pi

ENGS = ("pe", "dve", "act", "pool", "sp")
NSLOT = 12
STQ = "pool"


class Prog:
    def __init__(self, nc, self_sync=True):
        self.nc = nc
        self.self_sync = self_sync
        self.stack = contextlib.ExitStack()
        self.sem = {e: self.stack.enter_context(nc.semaphore("s_" + e)) for e in ENGS}
        self.cnt = {e: 0 for e in ENGS}
        self.dq = ("sp", "pool", "act")
        self.slot_sem = {q: [self.stack.enter_context(nc.semaphore(f"d_{q}{i}")) for i in range(NSLOT)]
                         for q in self.dq}
        self.slot_val = {q: [0] * NSLOT for q in self.dq}
        self.slot_next = {q: 0 for q in self.dq}
        self.known = {e: {} for e in ENGS}
        self.semobj = {}
        for e in ENGS:
            self.semobj["s_" + e] = self.sem[e]
        for q in self.dq:
            for i in range(NSLOT):
                self.semobj[f"d_{q}{i}"] = self.slot_sem[q][i]
        self.last_w = {}
        self.readers = {}
        self.rec = {e: [] for e in ENGS}
        self.n_ops = 0
        self.cc_sem = self.stack.enter_context(nc.semaphore("d_cc"))
        self.semobj["d_cc"] = self.cc_sem
        self.cc_val = 0

    def collective(self, emit, r=(), w=()):
        r = tuple(r); w = tuple(w)
        waits = self._deps("pool", r, w)
        self.cc_val += 1
        ev = ("d_cc", self.cc_val, "pool")
        self._commit(ev, r, w)
        self.rec["pool"].append((waits, emit, self.cc_sem, 1))
        self.n_ops += 1

    def _commit(self, ev, r, w):
        for k in r:
            lst = self.readers.setdefault(k, {})
            lst[ev[0]] = max(lst.get(ev[0], (0,))[0], ev[1]), ev[2]
        for k in w:
            self.last_w[k] = ev
            self.readers[k] = {}

    def op(self, eng, emit, r=(), w=()):
        r = tuple(r); w = tuple(w)
        waits = self._deps(eng, r, w)
        self.cnt[eng] += 1
        ev = ("s_" + eng, self.cnt[eng], eng)
        self._commit(ev, r, w)
        self.rec[eng].append((waits, emit, self.sem[eng], 1))
        self.n_ops += 1

    def _deps(self, eng, r, w):
        deps = {}

        def add(sn, v, src):
            if src == eng and (eng == "pe" or not self.self_sync) and not sn.startswith("d_"):
                return
            if v > deps.get(sn, 0):
                deps[sn] = v
        for k in r:
            ev = self.last_w.get(k)
            if ev is not None:
                add(*ev)
        for k in w:
            ev = self.last_w.get(k)
            if ev is not None:
                add(*ev)
            for sn, (v, src) in self.readers.get(k, {}).items():
                add(sn, v, src)
        out = []
        kn = self.known[eng]
        for sn, v in deps.items():
            if kn.get(sn, 0) < v:
                kn[sn] = v
                out.append((sn, v))
        return out

    def dma(self, q, out, in_, r=(), w=(), **kw):
        r = tuple(r); w = tuple(w)
        s = self.slot_next[q]
        self.slot_next[q] = (s + 1) % NSLOT
        sn = f"d_{q}{s}"
        waits = self._deps(q, r, w)
        prev = self.slot_val[q][s]
        if prev > 0 and self.known[q].get(sn, 0) < prev:
            self.known[q][sn] = prev
            waits.append((sn, prev))
        self.slot_val[q][s] = prev + 16
        ev = (sn, prev + 16, q)
        self._commit(ev, r, w)
        self.rec[q].append((waits, (lambda e, o=out, i=in_, k=kw: e.dma_start(out=o, in_=i, **k)),
                            self.slot_sem[q][s], 16))
        self.n_ops += 1

    def flush(self):
        final = []
        for e in ENGS:
            if self.cnt[e] > 0:
                final.append(("s_" + e, self.cnt[e]))
        for q in self.dq:
            for i in range(NSLOT):
                if self.slot_val[q][i] > 0:
                    final.append((f"d_{q}{i}", self.slot_val[q][i]))
        if self.cc_val > 0:
            final.append(("d_cc", self.cc_val))
        recs = {}
        for e in ENGS:
            ws = []
            for sn, v in final:
                if sn == "s_" + e:
                    continue
                if self.known[e].get(sn, 0) < v:
                    self.known[e][sn] = v
                    ws.append((sn, v))
            recs[e] = (self.rec[e], ws)
            self.rec[e] = []
        semobj = self.semobj

        def body(lst, ws):
            def f(engine):
                for waits, emit, sem, inc in lst:
                    for sn, v in waits:
                        engine.wait_ge(semobj[sn], v)
                    emit(engine).then_inc(sem, inc)
                for sn, v in ws:
                    engine.wait_ge(semobj[sn], v)
            return f
        with self.nc.Block() as block:
            block.tensor(body(*recs["pe"]))
            block.vector(body(*recs["dve"]))
            block.scalar(body(*recs["act"]))
            block.gpsimd(body(*recs["pool"]))
            block.sync(body(*recs["sp"]))
        self.last_w = {}
        self.readers = {}

    def close(self):
        self.stack.close()


class Pack:
    def __init__(self):
        self.cols = []
        self.off = {}
        self.n = 0

    def add(self, name, arr):
        arr = np.asarray(arr, np.float32)
        assert arr.shape[0] == 128, (name, arr.shape)
        arr = arr.reshape(128, -1)
        self.off[name] = self.n
        self.cols.append(arr)
        self.n += arr.shape[1]

    def build(self):
        return np.ascontiguousarray(np.concatenate(self.cols, axis=1))


def fm(vec):
    vec = np.asarray(vec, np.float32)
    return np.ascontiguousarray(vec.reshape(-1, 128).T)


def fm64(vec):
    out = np.zeros((128, 1), np.float32)
    out[:64, 0] = vec
    return out


def rowb(vec):
    vec = np.asarray(vec, np.float32).reshape(1, -1)
    return np.ascontiguousarray(np.broadcast_to(vec, (128, vec.shape[1])))


P_HY, P_HYG, P_RGX, P_RGG, P_HGQ, P_HGFF, P_HGFB, P_HGI, P_HGG, P_M2X, P_M2DT, P_M2G = (
    0, 1536, 2048, 2560, 3072, 3584, 4096, 4608, 5120, 5632, 6656, 6664)
C_HYV, C_HYX1, C_HYX2, C_HYG, C_RGX, C_RGG, C_HGQ, C_HGFF, C_HGFB, C_HGI, C_HGG, C_M2XS, C_M2B, C_M2C, C_M2G, C_M2DT = (
    0, 256, 512, 768, 1024, 1280, 1536, 1792, 2048, 2304, 2560, 2816, 3072, 3200, 3328, 3584)


def core_cols(h):
    r = lambda a, n: list(range(a, a + n))
    idx = []
    idx += r(P_HY + h * 256, 256) + r(P_HY + 512 + h * 256, 256) + r(P_HY + 1024 + h * 256, 256)
    idx += r(P_HYG + h * 256, 256)
    idx += r(P_RGX + h * 256, 256) + r(P_RGG + h * 256, 256)
    idx += r(P_HGQ + h * 256, 256) + r(P_HGFF + h * 256, 256) + r(P_HGFB + h * 256, 256)
    idx += r(P_HGI + h * 256, 256) + r(P_HGG + h * 256, 256)
    idx += r(P_M2X + h * 256, 256) + r(P_M2X + 512 + h * 128, 128) + r(P_M2X + 768 + h * 128, 128)
    idx += r(P_M2G + h * 256, 256) + r(P_M2DT + h * 4, 4)
    assert len(idx) == NCOL
    return np.array(idx)


_CONST_CACHE = {}


def dft_tables(n):
    key = ("dft", n)
    if key not in _CONST_CACHE:
        a = (np.arange(n, dtype=np.float64) + 0.5)
        ang = (2.0 * np.pi / (2 * n)) * np.outer(a, a)
        nb = n // 128
        tbg, cq = min(4, nb), min(512, n)

        def tile(t):
            t = t.astype(ml_dtypes.bfloat16).reshape(nb // tbg, tbg, 128, n // cq, cq)
            return np.ascontiguousarray(t.transpose(0, 3, 2, 1, 4))
        _CONST_CACHE[key] = (tile(np.cos(ang)), tile(np.sin(ang)))
    return _CONST_CACHE[key]


def hy_zfeat(n):
    f32 = np.float32
    t = np.linspace(0.0, 1.0, n, dtype=f32)[:, None]
    bands = np.linspace(1e-4, 16 - 1, 16, dtype=f32)[None]
    ang = (f32(2.0 * math.pi / n) * np.arange(n, dtype=f32)[:, None]) * bands
    z = np.concatenate([t, np.cos(ang), -np.sin(ang)], axis=-1).astype(f32)
    zt = np.zeros((HY_EMB, n + 1), f32)
    zt[:, :n] = z.T
    return zt


def const_pack():
    pk = Pack()
    for n, tag in ((SEQ, "L"), (CTX, "C")):
        nb = n // 128
        l = np.arange(n, dtype=np.float64)
        t = np.linspace(0.0, 1.0, n, dtype=np.float32).astype(np.float64)
        tf = np.zeros(n + 1); tf[:n] = t; tf[n] = 0.0
        pk.add("negt_f" + tag, fm(-tf[0:n]))
        pk.add("negt_b" + tag, fm(-tf[1:n + 1]))
        th2 = np.pi * (l + 0.5) / (2 * n)
        pk.add("chalf" + tag, fm(np.cos(th2)))
        pk.add("shalf" + tag, fm(np.sin(th2)))
    return pk


def const_mats():
    f32 = np.float32
    m = {}
    m["ident"] = np.eye(128, dtype=f32)
    m["ones"] = np.ones((128, 128), f32)
    k = np.arange(128)
    m["tri_incl"] = (k[:, None] <= k[None, :]).astype(f32)
    m["tri_excl"] = (k[:, None] < k[None, :]).astype(f32)
    s32 = np.arange(128) % 32
    t32 = np.arange(32)
    m["hgmask_f"] = (t32[None, :] >= s32[:, None]).astype(np.int32)
    m["hgmask_b"] = (t32[None, :] <= s32[:, None]).astype(np.int32)
    s64 = np.arange(64)
    m["hgmask64_f"] = (s64[None, :] >= s64[:, None]).astype(np.int32)
    m["hgmask64_b"] = (s64[None, :] <= s64[:, None]).astype(np.int32)
    m["m2mask_f"] = (k[None, :] >= k[:, None]).astype(f32)
    m["m2mask_b"] = (k[None, :] <= k[:, None]).astype(f32)
    sel = np.zeros((2, 2, 128), f32)
    sel[0, 0, :] = 1.0
    sel[1, 1, :] = 1.0
    m["sel"] = sel
    return m


def host_prep(inp, b, h):
    g = lambda k: np.asarray(inp[k], np.float32)
    d = {}
    d["x"] = np.ascontiguousarray(g("x")[b])
    d["ctx"] = np.ascontiguousarray(g("ctx")[b])
    d["w_mod"] = g("w_mod")
    d["w_out"] = g("w_out")
    cols = core_cols(h)
    d["w_in"] = np.ascontiguousarray(g("w_in")[:, :, cols])
    sl = slice(h * 256, (h + 1) * 256)
    pk = Pack()
    cv = np.stack([fm(g("c")[b]), fm(g("c_ctx"))], axis=-1)
    pk.add("cvec", cv)
    for l in range(DEPTH):
        L = str(l)
        pk.add("b_mod" + L, fm(g("b_mod")[l]))
        pk.add("norm_w" + L, fm(g("norm_w")[l]))
        cw = g("hy_conv_w")[l]
        cb = g("hy_conv_b")[l]
        hyc = np.concatenate([np.arange(o + h * 256, o + h * 256 + 256) for o in (0, 512, 1024)])
        pk.add("hy_cw" + L, np.stack([fm(cw[j, hyc]) for j in range(3)], axis=-1))
        pk.add("hy_cb" + L, fm(cb[hyc]))
        pk.add("hy_b1" + L, fm64(g("hy_b1")[l]))
        pk.add("hy_b2" + L, fm64(g("hy_b2")[l]))
        pk.add("hy_fr" + L, fm64(g("hy_freq")[l]))
        pk.add("hy_skip" + L, np.stack([fm(g("hy_skip")[l, o, sl]) for o in range(2)], axis=1))
        for dr in range(2):
            DD = L + str(dr)
            pk.add("rg_cw" + DD, np.stack([fm(g("rg_conv_w")[l, dr, j, sl]) for j in range(4)], axis=-1))
            pk.add("rg_cb" + DD, fm(g("rg_conv_b")[l, dr, sl]))
            pk.add("rg_ba" + DD, fm(g("rg_ba")[l, dr, sl]))
            pk.add("rg_bx" + DD, fm(g("rg_bx")[l, dr, sl]))
            pk.add("rg_lam" + DD, fm(g("rg_lam")[l, dr, sl]))
            m2c = np.concatenate([np.arange(h * 256, h * 256 + 256), np.arange(512 + h * 128, 512 + h * 128 + 128),
                                  np.arange(768 + h * 128, 768 + h * 128 + 128)])
            pk.add("m2_cw" + DD, np.stack([fm(g("m2_conv_w")[l, dr, j, m2c]) for j in range(4)], axis=-1))
            pk.add("m2_cb" + DD, fm(g("m2_conv_b")[l, dr, m2c]))
            for ll in range(DEPTH):
                pk.add("hg_lb%d_%s" % (ll, DD), fm(g("hg_lb")[ll, dr, sl]))
        pk.add("hg_nw" + L, fm(g("hg_norm_w")[l, sl]))
        pk.add("m2_nw" + L, fm(g("m2_norm_w")[l, sl]))
    cp = const_pack()
    for name in cp.off:
        pass
    base = pk.n
    pk.cols += cp.cols
    for name, o in cp.off.items():
        pk.off[name] = base + o
    pk.n += cp.n
    d["pcol"] = pk.build()
    pr = Pack()
    for l in range(DEPTH):
        L = str(l)
        pr.add("bmod_g" + L, rowb(g("b_mod")[l, 2048:3072]))
        for dr in range(2):
            DD = L + str(dr)
            hs = slice(h * 4, h * 4 + 4)
            pr.add("m2_dtb" + DD, rowb(g("m2_dt_bias")[l, dr, hs]))
            pr.add("m2_alog" + DD, rowb(g("m2_a_log")[l, dr, hs]))
            pr.add("m2_d" + DD, rowb(np.repeat(g("m2_d")[l, dr, hs], 64)))
    pr.add("final_w", rowb(g("final_norm_w")))
    deltas = np.abs(np.linspace(math.log(1e-2) / 1.5, math.log(1e-2) / 0.3, W_BR, dtype=np.float32))
    pr.add("hy_delta", rowb(deltas[sl]))
    d["prow"] = pr.build()
    bd = np.zeros((DEPTH, 2, 2, 2, 128, 128), np.float32)
    for l in range(DEPTH):
        for dr in range(2):
            for ai, nm in enumerate(("rg_wa", "rg_wx")):
                wgt = g(nm)[l, dr]
                for cc in range(2):
                    for hh in range(2):
                        head = h * 4 + cc * 2 + hh
                        bd[l, dr, ai, cc, hh * 64:(hh + 1) * 64, hh * 64:(hh + 1) * 64] = wgt[head]
    d["rg_bd"] = bd
    d["hy_w1"] = g("hy_w1")
    d["hy_w2"] = g("hy_w2")
    w3 = g("hy_w3").reshape(DEPTH, HY_HID, 2, 2, W_BR)[:, :, :, :, sl]
    d["hy_w3"] = np.ascontiguousarray(w3.transpose(0, 1, 3, 2, 4)).reshape(DEPTH, HY_HID, 2, 512)
    return d, pk.off, pr.off


PADW = 3
RB_CTX = PADW
RB_LAT = PADW + CTX + PADW
RB_W = RB_LAT + SEQ + PADW
TOKCH = [(0, CTX)] + [(CTX + 512 * i, 512) for i in range(8)]


def rb_off(t0):
    return RB_CTX + t0 if t0 < CTX else RB_LAT + (t0 - CTX)


class Kern:
    def __init__(self, mode, poff, roff, dbg=()):
        self.mode = mode
        self.poff = poff
        self.roff = roff
        self.dbg = set(dbg)
        nc = self.nc = bass.Bass("TRN2", target_bir_lowering=False)
        self.P = Prog(nc)
        self.din = {}
        self.scr = {}
        self.gst = contextlib.ExitStack()
        self.ps_i = 0
        self.uid = 0

    def inp(self, name, shape, dt=F32):
        ap = self.nc.dram_tensor(name, list(shape), dt, kind="ExternalInput").ap()
        self.din[name] = ap
        return ap

    def out(self, name, shape, dt=F32):
        ap = self.nc.dram_tensor(name, list(shape), dt, kind="ExternalOutput").ap()
        self.din[name] = ap
        return ap

    def scratch(self, name, shape, dt=F32):
        kind = "ExternalOutput" if name in self.dbg else "Internal"
        ap = self.nc.dram_tensor(name, list(shape), dt, kind=kind).ap()
        self.scr[name] = ap
        return ap

    def sb(self, st, name, shape, dt=F32):
        self.uid += 1
        return st.enter_context(self.nc.sbuf_tensor(f"{name}_{self.uid}", list(shape), dt))

    def psum(self):
        rot = getattr(self, "ps_rot", None) or list(range(8))
        self.ps_i = (self.ps_i + 1) % len(rot)
        i = rot[self.ps_i]
        return self.ps[i], f"ps{i}"

    def pc(self, name, j=0, n=1):
        o = self.poff[name] + j
        return self.pcol[:, o:o + n]

    def mm(self, out, lhsT, rhs, start, stop, r, w):
        self.P.op("pe", lambda e: e.matmul(out, lhsT=lhsT, rhs=rhs, start=start, stop=stop), r, w)

    def tr(self, out, in_, r, w):
        idn = self.ident
        self.P.op("pe", lambda e: e.transpose(out, in_, idn[:]), list(r) + ["const"], w)

    def act(self, out, in_, func, r, w, **kw):
        self.P.op("act", lambda e: e.activation(out=out, in_=in_, func=func, **kw), r, w)

    def ts(self, eng, out, in0, s1, op0, r, w, s2=None, op1=None):
        if op1 is None:
            self.P.op(eng, lambda e: e.tensor_scalar(out=out, in0=in0, scalar1=s1, scalar2=None, op0=op0), r, w)
        else:
            self.P.op(eng, lambda e: e.tensor_scalar(out=out, in0=in0, scalar1=s1, scalar2=s2, op0=op0, op1=op1), r, w)

    def tt(self, eng, out, in0, in1, op, r, w):
        self.P.op(eng, lambda e: e.tensor_tensor(out=out, in0=in0, in1=in1, op=op), r, w)

    def stt(self, out, in0, scalar, in1, op0, op1, r, w):
        self.P.op("dve", lambda e: e.scalar_tensor_tensor(out=out, in0=in0, scalar=scalar, in1=in1, op0=op0, op1=op1), r, w)

    def cp(self, eng, out, in_, r, w):
        if eng == "act":
            self.act(out, in_, AF.Copy, r, w)
        else:
            self.P.op(eng, lambda e: e.tensor_copy(out=out, in_=in_), r, w)

    def memset(self, eng, ap, val, w):
        self.P.op(eng, lambda e: e.memset(ap, val), (), w)

    def setup(self):
        nc, P = self.nc, self.P
        g = self.gst
        self.ps = [g.enter_context(nc.psum_tensor(f"psb{i}", [128, 512], F32)) for i in range(8)]
        npc = max(self.poff.values()) + 64
        self.npc = self.din["pcol"].shape[1]
        self.pcol = self.sb(g, "pcol", [128, self.npc])
        self.ident = self.sb(g, "ident", [128, 128])
        self.ones_f = self.sb(g, "ones_f", [128, 128])
        self.ones_b = self.sb(g, "ones_b", [128, 128], BF16)
        self.ident_b = self.sb(g, "ident_b", [128, 128], BF16)
        self.modT = self.sb(g, "modT", [128, DEPTH, 24, 2])
        self.Amod = self.sb(g, "Amod", [128, DEPTH, 8, 2])
        P.dma("sp", self.pcol[:], self.din["pcol"], w=["pcol"])
        P.dma("sp", self.ident[:], self.din["c_ident"], w=["const"])
        P.dma("sp", self.ones_f[:], self.din["c_ones"], w=["const"])
        self.epsc = self.sb(g, "epsc", [128, 2])
        self.memset("dve", self.epsc[:, 0:1], EPS, ["epsc"])
        self.memset("dve", self.epsc[:, 1:2], 1.0, ["epsc"])
        self.cp("dve", self.ones_b[:], self.ones_f[:], ["const"], ["const"])
        self.cp("dve", self.ident_b[:], self.ident[:], ["const"], ["const"])

    def phase_mod(self):
        P = self.P
        w_mod = self.din["w_mod"]
        with contextlib.ExitStack() as st:
            wst = [self.sb(st, "wm", [128, 8, 512]) for _ in range(2)]
            cond = self.sb(st, "cond", [128, 8, 2])
            grow = self.sb(st, "grow", [2, 1024])
            brow = self.sb(st, "brow", [2, 1024])
            sel = self.sb(st, "sel", [2, 2, 128])
            gb = [self.sb(st, "gb", [128, 512]) for _ in range(2)]
            P.dma("sp", sel[:], self.din["c_sel"].rearrange("s k m -> k s m"), w=["sel"])
            self.act(cond[:].rearrange("p k s -> p (k s)"), self.pc("cvec", 0, 16), AF.Silu, ["pcol"], ["cond"])
            it = 0
            for l in range(DEPTH):
                o = self.roff["bmod_g%d" % l]
                P.dma("sp", brow[:], self.din["prow"][0:2, o:o + 1024], w=["brow"])
                for jb in range(6):
                    buf = wst[it % 2]; bk = "wm%d" % (it % 2); it += 1
                    P.dma("sp", buf[:], w_mod[l, :, jb * 512:(jb + 1) * 512].rearrange("(k p) c -> p k c", p=128), w=[bk])
                    ps, pk = self.psum()
                    for jj in range(4):
                        for k in range(8):
                            self.mm(ps[:, jj * 2:jj * 2 + 2], buf[:, k, jj * 128:(jj + 1) * 128], cond[:, k, :],
                                    k == 0, k == 7, [bk, "cond"], [pk])
                    bm = self.pc("b_mod%d" % l, jb * 4, 4)
                    self.tt("dve", self.modT[:, l, jb * 4:(jb + 1) * 4, :], ps[:, 0:8].rearrange("p (j s) -> p j s", s=2),
                            bm.unsqueeze(2).to_broadcast([128, 4, 2]), ALU.add, [pk, "pcol"], ["modT"])
                    if jb >= 4:
                        ps2, pk2 = self.psum()
                        for k in range(8):
                            self.mm(ps2[0:2, :], cond[:, k, :], buf[:, k, :], k == 0, k == 7, [bk, "cond"], [pk2])
                        c0 = (jb - 4) * 512
                        self.tt("dve", grow[:, c0:c0 + 512], ps2[0:2, :], brow[:, c0:c0 + 512], ALU.add,
                                [pk2, "brow"], ["grow"])
                nw = self.pc("norm_w%d" % l, 0, 8)
                self.ts("dve", self.Amod[:, l, :, :], self.modT[:, l, 8:16, :], 1.0, ALU.add, ["modT"], ["Amod"])
                self.tt("dve", self.Amod[:, l, :, :], self.Amod[:, l, :, :], nw.unsqueeze(2).to_broadcast([128, 8, 2]),
                        ALU.mult, ["Amod", "pcol"], ["Amod"])
                for s in range(2):
                    for hf in range(2):
                        ps3, pk3 = self.psum()
                        self.mm(ps3[:, :], sel[:, s, :], grow[:, hf * 512:(hf + 1) * 512], True, True, ["sel", "grow"], [pk3])
                        gt = gb[(s * 2 + hf) % 2]; gk = "gb%d" % ((s * 2 + hf) % 2)
                        self.cp("act", gt[:], ps3[:, :], [pk3], [gk])
                        P.dma(STQ, self.scr["gbc"][l, s, :, hf * 512:(hf + 1) * 512], gt[:], r=[gk], w=["gbc"])
            P.flush()

    def phase_norm(self, l, st, xlat, xctx):
        P = self.P
        hlT = self.hlT = self.sb(st, "hlT", [128, 8, NT], BF16)
        with contextlib.ExitStack() as s2:
            xt = [self.sb(s2, "xt", [128, 1024]) for _ in range(6)]
            junks = [self.sb(s2, "junk", [128, 1024]) for _ in range(2)]
            ssq = self.sb(s2, "ssq", [128, 4])
            rstd = self.sb(s2, "rstd", [128, 4])
            groups = [[0, 1]] + [list(range(2 + 4 * i, 6 + 4 * i)) for i in range(8)]
            xi = 0
            for gi, grp in enumerate(groups):
                s = 1 if gi == 0 else 0
                ng = len(grp)
                tl = []
                for i, ti in enumerate(grp):
                    t = xt[xi % 6]; tk = "xt%d" % (xi % 6); xi += 1
                    src = xctx[ti * 128:(ti + 1) * 128, :] if ti < 2 else xlat[(ti - 2) * 128:(ti - 1) * 128, :]
                    P.dma("sp", t[:], src, w=[tk])
                    jn = junks[xi % 2]
                    self.P.op("act", lambda e, t=t, i=i, jn=jn: e.activation(out=jn[:], in_=t[:], func=AF.Square,
                                                                           accum_out=ssq[:, i:i + 1]), [tk], ["junk%d" % (xi % 2), ("ssq", i)])
                    tl.append((t, tk))
                self.act(rstd[:, :ng], ssq[:, :ng], AF.Ln, [("ssq", i) for i in range(ng)], ["rstd"], scale=1.0 / D, bias=self.epsc[:, 0:1])
                self.act(rstd[:, :ng], rstd[:, :ng], AF.Exp, ["rstd"], ["rstd"], scale=-0.5)
                for i, (t, tk) in enumerate(tl):
                    if i % 2 == 0:
                        self.ts("dve", t[:], t[:], rstd[:, i:i + 1], ALU.mult, [tk, "rstd"], [tk])
                    else:
                        self.act(t[:], t[:], AF.Copy, [tk, "rstd"], [tk], scale=rstd[:, i:i + 1])
                tok0 = grp[0] * 128
                for k in range(8):
                    ps, pk = self.psum()
                    for i, (t, tk) in enumerate(tl):
                        self.tr(ps[:, i * 128:(i + 1) * 128], t[:, k * 128:(k + 1) * 128], [tk], [pk])
                    hk = ("hlT", tok0 // 512 if tok0 >= CTX else -1)
                    a_col = self.Amod[:, l, k, s:s + 1]
                    sh_col = self.modT[:, l, k, s:s + 1]
                    if k % 2 == 0:
                        self.ts("dve", hlT[:, k, tok0:tok0 + ng * 128], ps[:, :ng * 128], a_col, ALU.mult,
                                [pk, "Amod", "modT"], [hk], s2=sh_col, op1=ALU.add)
                    else:
                        self.act(hlT[:, k, tok0:tok0 + ng * 128], ps[:, :ng * 128], AF.Identity,
                                 [pk, "Amod", "modT"], [hk], scale=a_col, bias=sh_col)
            P.flush()

    def hl_key(self, t0):
        return ("hlT", (t0 - CTX) // 512 if t0 >= CTX else -1)

    def conv(self, obuf, ok, rbuf, rk, wname, wj, ntap, bname, bj, offs):
        lo, hi = PADW, RB_W - PADW
        o = obuf[:, lo:hi]
        self.ts("dve", o, rbuf[:, lo + offs[0]:hi + offs[0]], self.pc(wname, wj, 1), ALU.mult,
                [rk, "pcol"], [ok], s2=self.pc(bname, bj, 1), op1=ALU.add)
        for j in range(1, ntap):
            self.stt(o, rbuf[:, lo + offs[j]:hi + offs[j]], self.pc(wname, wj + j, 1), o, ALU.mult, ALU.add,
                     [rk, ok, "pcol"], [ok])

    def rb_store(self, q, dst_row_ap, buf, bk, wkey):
        self.P.dma(q, dst_row_ap[:, 0:CTX], buf[:, RB_CTX:RB_CTX + CTX], r=[bk], w=[wkey])
        self.P.dma(q, dst_row_ap[:, CTX:NT], buf[:, RB_LAT:RB_LAT + SEQ], r=[bk], w=[wkey])

    def phase_inproj(self, l):
        P = self.P
        hlT = self.hlT
        w_in = self.din["w_in"]
        S = self.scr
        L = str(l)
        with contextlib.ExitStack() as st:
            wst = [self.sb(st, "wst", [128, 8, 256]) for _ in range(2)]
            wbf = [self.sb(st, "wbf", [128, 8, 256], BF16) for _ in range(2)]
            rbs = [self.sb(st, "rb", [128, RB_W]) for _ in range(3)]
            obs = [self.sb(st, "ob", [128, RB_W]) for _ in range(3)]
            tmv = [self.sb(st, "tmv", [128, 256]) for _ in range(2)]
            dtt = self.sb(st, "dtt", [128, NTILE * 4])
            for i in range(3):
                self.memset("pool", rbs[i][:], 0.0, ["rb%d" % i])
            cnt = {"w": 0, "rb": 0, "ob": 0, "ev": 0, "tm": 0}

            def load_w(c0, n):
                i = cnt["w"] % 2; cnt["w"] += 1
                P.dma("sp", wst[i][:, :, :n], w_in[l, :, c0:c0 + n].rearrange("(k p) c -> p k c", p=128), w=["wst%d" % i])
                self.cp("act", wbf[i][:, :, :n], wst[i][:, :, :n], ["wst%d" % i], ["wbf%d" % i])
                return wbf[i], "wbf%d" % i

            def rhs_nat(k, t0, n):
                return hlT[:, k, t0:t0 + n], [self.hl_key(t0)]

            def rhs_cm(k, t0, n):
                if t0 < CTX:
                    return rhs_nat(k, t0, n)
                i = (t0 - CTX) // 512
                v = hlT[:, k, CTX:NT].rearrange("p (r w) -> p w r", w=64)[:, 8 * i:8 * i + 8, :]
                return v, [("hlT", j) for j in range(8)]

            def fm_block(wb, wk, cb, rhsf, evac):
                for (t0, n) in TOKCH:
                    ps, pk = self.psum()
                    if rhsf is rhs_cm and t0 >= CTX:
                        i = (t0 - CTX) // 512
                        hk = [("hlT", j) for j in range(8)]
                        for wl in range(8):
                            w = 8 * i + wl
                            for k in range(8):
                                self.mm(ps[:, wl * 64:(wl + 1) * 64], wb[:, k, cb * 128:(cb + 1) * 128],
                                        hlT[:, k, CTX + w:NT:64], k == 0, k == 7, [wk] + hk, [pk])
                    else:
                        for k in range(8):
                            rhs, rk = rhs_nat(k, t0, n)
                            self.mm(ps[:, :n], wb[:, k, cb * 128:(cb + 1) * 128], rhs, k == 0, k == 7, [wk] + rk, [pk])
                    evac(ps, pk, t0, n)

            def evac_to(buf, bk, func, scale=None):
                def f(ps, pk, t0, n):
                    o = buf[:, rb_off(t0):rb_off(t0) + n]
                    cnt["ev"] += 1
                    if func is None and scale is None and cnt["ev"] % 2 == 0:
                        self.cp("dve", o, ps[:, :n], [pk], [bk])
                    else:
                        kw = {} if scale is None else {"scale": scale}
                        self.act(o, ps[:, :n], AF.Copy if func is None else func, [pk], [bk], **kw)
                return f

            def next_rb():
                i = cnt["rb"] % 3; cnt["rb"] += 1
                return rbs[i], "rb%d" % i

            def next_ob():
                i = cnt["ob"] % 3; cnt["ob"] += 1
                return obs[i], "ob%d" % i

            def do_hy():
                for j in range(3):
                    wb, wk = load_w(C_HYV + j * 256, 256)
                    for cb in range(2):
                        rb, rk = next_rb()
                        fm_block(wb, wk, cb, rhs_nat, evac_to(rb, rk, None))
                        ob, ok = next_ob()
                        self.conv(ob, ok, rb, rk, "hy_cw" + L, (j * 2 + cb) * 3, 3, "hy_cb" + L, j * 2 + cb, (-1, 0, 1))
                        self.rb_store(STQ, S["hy_u"][j, cb * 128:(cb + 1) * 128, :], ob, ok, "hy_u")

            def do_gate(c0, nm):
                wb, wk = load_w(c0, 256)
                for cb in range(2):
                    rb, rk = next_rb()
                    fm_block(wb, wk, cb, rhs_nat, evac_to(rb, rk, AF.Silu))
                    self.rb_store(STQ, S[nm][cb * 128:(cb + 1) * 128, :], rb, rk, nm)

            def do_rg():
                wb, wk = load_w(C_RGX, 256)
                for cb in range(2):
                    rb, rk = next_rb()
                    fm_block(wb, wk, cb, rhs_nat, evac_to(rb, rk, None))
                    for dr in range(2):
                        ob, ok = next_ob()
                        offs = (-3, -2, -1, 0) if dr == 0 else (3, 2, 1, 0)
                        self.conv(ob, ok, rb, rk, "rg_cw" + L + str(dr), cb * 4, 4, "rg_cb" + L + str(dr), cb, offs)
                        self.rb_store(STQ, S["rg_x"][dr, cb * 128:(cb + 1) * 128, :], ob, ok, "rg_x")

            def do_plain(c0, dst, scale):
                wb, wk = load_w(c0, 256)
                for cb in range(2):
                    rb, rk = next_rb()
                    fm_block(wb, wk, cb, rhs_nat, evac_to(rb, rk, None, scale))
                    self.rb_store(STQ, dst[cb * 128:(cb + 1) * 128, :], rb, rk, "hg_qf")

            def do_m2(c0, n, ch0):
                wb, wk = load_w(c0, n)
                for cb in range(2):
                    rb, rk = next_rb()
                    fm_block(wb, wk, cb, rhs_cm, evac_to(rb, rk, None))
                    for dr in range(2):
                        ob, ok = next_ob()
                        offs = (-3, -2, -1, 0) if dr == 0 else (3, 2, 1, 0)
                        ch = ch0 + cb
                        self.conv(ob, ok, rb, rk, "m2_cw" + L + str(dr), ch * 4, 4, "m2_cb" + L + str(dr), ch, offs)
                        self.act(ob[:, PADW:RB_W - PADW], ob[:, PADW:RB_W - PADW], AF.Silu, [ok], [ok])
                        self.rb_store(STQ, S["m2_x"][dr, ch * 128:(ch + 1) * 128, :], ob, ok, "m2_x")

            do_m2(C_M2XS, 256, 0)
            do_gate(C_HYG, "hy_g")
            do_gate(C_RGG, "rg_g")
            do_rg()
            do_gate(C_HGG, "hg_g")
            do_gate(C_M2G, "m2_g")
            do_m2(C_M2B, 256, 2)
            do_plain(C_HGQ, S["hg_q"], 128.0 ** -0.5)
            do_plain(C_HGFF, S["hg_f"][0], None)
            do_hy()
            do_plain(C_HGFB, S["hg_f"][1], None)
            wb, wk = load_w(C_HGI, 256)
            for ti in range(NTILE):
                ps, pk = self.psum()
                for k in range(8):
                    self.mm(ps[:, 0:256], hlT[:, k, ti * 128:(ti + 1) * 128], wb[:, k, :], k == 0, k == 7,
                            [wk, self.hl_key(ti * 128)], [pk])
                i = cnt["tm"] % 2; cnt["tm"] += 1
                self.cp("dve" if ti % 2 else "act", tmv[i][:], ps[:, 0:256], [pk], ["tmv%d" % i])
                P.dma(STQ, S["hg_v"][ti * 128:(ti + 1) * 128, :], tmv[i][:], r=["tmv%d" % i], w=["hg_v"])
            wb, wk = load_w(C_M2DT, 4)
            ps, pk = self.psum()
            for ti in range(NTILE):
                if ti < 2:
                    for k in range(8):
                        self.mm(ps[:, ti * 4:ti * 4 + 4], hlT[:, k, ti * 128:(ti + 1) * 128], wb[:, k, 0:4], k == 0, k == 7,
                                [wk, self.hl_key(ti * 128)], [pk])
                else:
                    hk = [("hlT", q) for q in range(8)]
                    for wl in range(2):
                        w = 2 * (ti - 2) + wl
                        for k in range(8):
                            self.mm(ps[wl * 64:(wl + 1) * 64, ti * 4:ti * 4 + 4], hlT[:, k, CTX + w:NT:64], wb[:, k, 0:4],
                                    k == 0, k == 7, [wk] + hk, [pk])
            self.cp("dve", dtt[:], ps[:, 0:NTILE * 4], [pk], ["dtt"])
            P.dma(STQ, S["m2_dt"], dtt[:], r=["dtt"], w=["m2_dt"])
            P.flush()

    def declare(self, npc, npr):
        mode = self.mode
        mix = mode in ("A", "B", "test", "ALL")
        self.inp("w_mod", [DEPTH, D, 3 * D]); self.inp("w_out", [DEPTH, 2 * D, D])
        self.inp("pcol", [128, npc]); self.inp("prow", [128, npr])
        self.inp("c_ident", [128, 128]); self.inp("c_ones", [128, 128]); self.inp("c_sel", [2, 2, 128])
        if mode != "C":
            self.inp("x", [SEQ, D]); self.inp("ctx", [CTX, D])
        if mix:
            self.inp("w_in", [DEPTH, D, NCOL])
            self.inp("rg_bd", [DEPTH, 2, 2, 2, 128, 128])
            self.inp("hy_w1", [DEPTH, HY_EMB, HY_HID]); self.inp("hy_w2", [DEPTH, HY_HID, HY_HID])
            self.inp("hy_w3", [DEPTH, HY_HID, 2, 512])
            self.inp("c_tri_incl", [128, 128]); self.inp("c_tri_excl", [128, 128])
            self.inp("c_hgmask_f", [128, 32], I32); self.inp("c_hgmask_b", [128, 32], I32)
            self.inp("c_hgmask64_f", [64, 64], I32); self.inp("c_hgmask64_b", [64, 64], I32)
            self.inp("c_m2mask_f", [128, 128]); self.inp("c_m2mask_b", [128, 128])
            self.inp("tabC_L", [8, 8, 128, 4, 512], BF16); self.inp("tabS_L", [8, 8, 128, 4, 512], BF16)
            self.inp("tabC_C", [1, 1, 128, 2, 256], BF16); self.inp("tabS_C", [1, 1, 128, 2, 256], BF16)
            self.inp("z_L", [HY_EMB, SEQ + 1]); self.inp("z_C", [HY_EMB, CTX + 1])
        sc = self.scratch
        sc("gbc", [DEPTH, 2, 128, D])
        if mix:
            sc("hy_u", [3, CH, NT]); sc("hy_g", [CH, NT]); sc("rg_g", [CH, NT]); sc("hg_g", [CH, NT]); sc("m2_g", [CH, NT])
            sc("rg_x", [2, CH, NT]); sc("hg_q", [CH, NT]); sc("hg_f", [2, CH, NT]); sc("hg_v", [NT, CH])
            sc("m2_x", [2, 512, NT]); sc("m2_dt", [128, NTILE * 4])
            sc("khat_L", [SEQ // 128, 2, 128, 512]); sc("khat_C", [CTX // 128, 2, 128, 512])
            if mode in ("A", "B"):
                self.scr["ybuf"] = self.out("y_out", [4 * CH, NT], BF16)
            else:
                sc("ybuf", [4 * CH, NT], BF16)
        if mode == "ALL":
            sc("yf", [2 * D, NT], BF16)
            sc("xres", [NT, D])
            self.out("out", [SEQ, D])
        if mode == "B":
            self.inp("yf", [2 * D, NT], BF16)
            self.scr["xres"] = self.out("xres_out", [NT, D])
        if mode == "C":
            self.inp("yf", [2 * D, SEQ // 2], BF16)
            self.inp("xres", [SEQ // 2, D])
            self.out("out", [SEQ // 2, D])

    def mixers(self, l, xlat, xctx, need_ctx, early=None):
        with contextlib.ExitStack() as st:
            self.phase_norm(l, st, xlat, xctx)
            self.phase_inproj(l)
        self.phase_rg(l)
        self.phase_hg(l)
        self.phase_m2(l)
        if early is not None:
            early()
        self.phase_hy(l, need_ctx)

    def finish(self):
        self.gst.close()
        self.P.close()
        return self.nc


def build_program(mode, poff, roff, npc, npr):
    kn = Kern(mode, poff, roff)
    kn.declare(npc, npr)
    kn.setup()
    kn.phase_mod()
    din = kn.din
    if mode == "A":
        kn.mixers(0, din["x"], din["ctx"], True)
    elif mode == "B":
        xres = kn.scr["xres"]
        kn.phase_out(0, din["yf"], lambda ti: (din["ctx"][ti * 128:(ti + 1) * 128, :] if ti < 2 else
                                                din["x"][(ti - 2) * 128:(ti - 1) * 128, :]),
                     list(range(NTILE)), xdst=lambda ti: [xres[ti * 128:(ti + 1) * 128, :]])
        kn.mixers(1, xres[CTX:NT, :], xres[0:CTX, :], False)
    elif mode == "ALL":
        xres = kn.scr["xres"]
        yf = kn.scr["yf"]
        ybuf = kn.scr["ybuf"]
        groups = [[0, 1], [2, 3], [4, 5], [6, 7]]

        def gather(js):
            for j in js:
                kn.P.collective(lambda e, j=j: e.collective_compute(
                    "AllGather", ALU.bypass, replica_groups=groups,
                    ins=[ybuf[j * 128:(j + 1) * 128, :].opt()], outs=[yf[j * 256:(j + 1) * 256, :].opt()]),
                    r=["ybuf"], w=[("yf", j)])
            if 0 in js:
                kn.P.flush()
        kn.mixers(0, din["x"], din["ctx"], True)
        gather(range(0, 8))
        kn.phase_out(0, yf, lambda ti: (din["ctx"][ti * 128:(ti + 1) * 128, :] if ti < 2 else
                                        din["x"][(ti - 2) * 128:(ti - 1) * 128, :]),
                     list(range(NTILE)), xdst=lambda ti: [xres[ti * 128:(ti + 1) * 128, :]])
        kn.mixers(1, xres[CTX:NT, :], xres[0:CTX, :], False)
        gather(range(0, 8))
        kn.phase_out(1, yf, lambda ti: xres[ti * 128:(ti + 1) * 128, :], list(range(2, NTILE)),
                     final_dst=lambda ti: din["out"][(ti - 2) * 128:(ti - 1) * 128, :])
    elif mode == "C":
        kn.phase_out(1, din["yf"], lambda ti: din["xres"][ti * 128:(ti + 1) * 128, :], list(range(SEQ // 256)),
                     final_dst=lambda ti: din["out"][ti * 128:(ti + 1) * 128, :], lat_only=True)
    return kn.finish()


def const_inputs():
    key = "cin"
    if key not in _CONST_CACHE:
        m = const_mats()
        d = {"c_" + k: v for k, v in m.items()}
        d["tabC_L"], d["tabS_L"] = dft_tables(SEQ)
        d["tabC_C"], d["tabS_C"] = dft_tables(CTX)
        d["z_L"] = hy_zfeat(SEQ)
        d["z_C"] = hy_zfeat(CTX)
        _CONST_CACHE[key] = d
    return _CONST_CACHE[key]


def _phase_rg(self, l):
    P = self.P
    S = self.scr
    L = str(l)
    CHK = [(i * 512, min(512, NT - i * 512)) for i in range(9)]
    with contextlib.ExitStack() as st:
        wtmps = [self.sb(st, "rgw", [128, 128]) for _ in range(2)]
        wbds = [self.sb(st, "rgwb", [128, 4, 128], BF16) for _ in range(2)]
        xcs = [self.sb(st, "rgxc", [128, NT]) for _ in range(2)]
        xcbs = [self.sb(st, "rgxcb", [128, NT], BF16) for _ in range(2)]
        avs = [self.sb(st, "rga", [128, NT]) for _ in range(2)]
        bvs = [self.sb(st, "rgb", [128, NT]) for _ in range(2)]
        gis = [self.sb(st, "rggi", [128, NT]) for _ in range(2)]
        prms = [self.sb(st, "rgprm", [128, 2]) for _ in range(2)]
        hs = self.sb(st, "rghs", [128, NT])
        hb = self.sb(st, "rghb", [128, NT])
        yb = self.sb(st, "rgy", [128, NT], BF16)
        for cc in range(2):
            for dr in range(2):
                DD = L + str(dr)
                bi = (cc * 2 + dr) % 2
                wtmp, wbd, xc, xcb, av, bv, gi, prm = wtmps[bi], wbds[bi], xcs[bi], xcbs[bi], avs[bi], bvs[bi], gis[bi], prms[bi]
                sfx = str(bi)
                self.act(prm[:, 0:1], self.pc("rg_lam" + DD, cc, 1), AF.Exp, ["pcol"], ["rgprm" + sfx], scale=-1.0)
                self.act(prm[:, 0:1], prm[:, 0:1], AF.Ln, ["rgprm" + sfx], ["rgprm" + sfx], bias=self.epsc[:, 1:2])
                self.ts("dve", prm[:, 1:2], prm[:, 0:1], -8.0, ALU.mult, ["rgprm" + sfx], ["rgprm" + sfx])
                for ai in range(2):
                    P.dma("sp", wtmp[:], self.din["rg_bd"][l, dr, ai, cc], w=["rgw" + sfx])
                    self.cp("dve", wbd[:, ai, :], wtmp[:], ["rgw" + sfx], ["rgwb" + sfx])
                P.dma("sp", xc[:], S["rg_x"][dr, cc * 128:(cc + 1) * 128, :], w=["rgxc" + sfx])
                self.cp("dve", xcb[:], xc[:], ["rgxc" + sfx], ["rgxcb" + sfx])
                for (t0, n) in CHK:
                    pa, pak = self.psum()
                    px, pxk = self.psum()
                    self.mm(pa[:, :n], wbd[:, 0, :], xcb[:, t0:t0 + n], True, True, ["rgwb" + sfx, "rgxcb" + sfx], [pak])
                    self.mm(px[:, :n], wbd[:, 1, :], xcb[:, t0:t0 + n], True, True, ["rgwb" + sfx, "rgxcb" + sfx], [pxk])
                    self.act(av[:, t0:t0 + n], pa[:, :n], AF.Sigmoid, [pak, "pcol"], ["rga" + sfx], bias=self.pc("rg_ba" + DD, cc, 1))
                    self.act(gi[:, t0:t0 + n], px[:, :n], AF.Sigmoid, [pxk, "pcol"], ["rggi" + sfx], bias=self.pc("rg_bx" + DD, cc, 1))
                self.act(av[:], av[:], AF.Exp, ["rga" + sfx, "rgprm" + sfx], ["rga" + sfx], scale=prm[:, 1:2])
                self.tt("pool", bv[:], av[:], av[:], ALU.mult, ["rga" + sfx], ["rgb" + sfx])
                self.act(bv[:], bv[:], AF.Sqrt, ["rgb" + sfx], ["rgb" + sfx], scale=-1.0, bias=self.epsc[:, 1:2])
                self.tt("dve", gi[:], gi[:], xc[:], ALU.mult, ["rggi" + sfx, "rgxc" + sfx], ["rggi" + sfx])
                self.tt("dve", bv[:], bv[:], gi[:], ALU.mult, ["rgb" + sfx, "rggi" + sfx], ["rgb" + sfx])
                dst = hs if dr == 0 else hb
                dk = "rghs" if dr == 0 else "rghb"
                if dr == 0:
                    segs = [(slice(0, CTX), None), (slice(CTX, NT), (CTX - 1, CTX))]
                    rev = False
                else:
                    segs = [(slice(0, CTX), None), (slice(CTX, NT), (0, 1))]
                    rev = True
                for (sg, init) in segs:
                    a0, a1 = sg.start, sg.stop
                    if not rev:
                        o_ap, d0, d1 = dst[:, a0:a1], av[:, a0:a1], bv[:, a0:a1]
                    else:
                        def rv(t, a0=a0, a1=a1):
                            return t[:, a1 - 1:a0 - 1:-1] if a0 > 0 else t[:, a1 - 1::-1]
                        o_ap, d0, d1 = rv(dst), rv(av), rv(bv)
                    ini = 0.0 if init is None else dst[:, init[0]:init[1]]
                    self.P.op("dve", lambda e, o_ap=o_ap, d0=d0, d1=d1, ini=ini: e.tensor_tensor_scan(
                        out=o_ap, data0=d0, data1=d1, initial=ini, op0=ALU.mult, op1=ALU.add), ["rga" + sfx, "rgb" + sfx, dk], [dk])
            self.tt("pool", hs[:], hs[:], hb[:], ALU.add, ["rghs", "rghb"], ["rghs"])
            P.dma("sp", hb[:], S["rg_g"][cc * 128:(cc + 1) * 128, :], r=[], w=["rghb"])
            self.tt("dve", yb[:], hs[:], hb[:], ALU.mult, ["rghs", "rghb"], ["rgy"])
            P.dma(STQ, S["ybuf"][CH + cc * 128:CH + (cc + 1) * 128, :], yb[:], r=["rgy"], w=["ybuf"])
        P.flush()


Kern.phase_rg = _phase_rg


def _phase_hg(self, l):
    P = self.P
    S = self.scr
    L = str(l)
    NCH = NT // 64
    self.ps_rot = [0, 1, 2, 3, 4, 5]
    with contextlib.ExitStack() as st:
        vtok = self.sb(st, "hgvtok", [64, NCH, 256], BF16)
        q = self.sb(st, "hgq", [128, NT])
        zs = self.sb(st, "hgzs", [128, NT])
        Pb = self.sb(st, "hgP", [128, NT + 1])
        Db = self.sb(st, "hgD", [128, NT])
        kGf = self.sb(st, "hgkG", [128, NT])
        qg = self.sb(st, "hgqg", [128, NT], BF16)
        kg = self.sb(st, "hgkg", [128, NT], BF16)
        qG = self.sb(st, "hgqG", [128, NT], BF16)
        osum = self.sb(st, "hgos", [128, NT])
        dec = self.sb(st, "hgdec", [128, NCH])
        lbc = self.sb(st, "hglb", [128, 4])
        Sf = self.sb(st, "hgSf", [128, 128])
        Sbs = [self.sb(st, "hgSb", [128, 128], BF16) for _ in range(2)]
        MT = [self.sb(st, "hgMT", [128, 64], BF16) for _ in range(3)]
        kGt = [self.sb(st, "hgkGt", [128, 128], BF16) for _ in range(3)]
        msk = [self.sb(st, "hgmsk", [128, 32], I32) for _ in range(2)]
        XB = self.sb(st, "hgXB", [128, NT], BF16)
        P.dma("sp", msk[0][:], self.din["c_hgmask_f"], w=["hgmsk"])
        P.dma("sp", msk[1][:], self.din["c_hgmask_b"], w=["hgmsk"])
        msk64 = [self.sb(st, "hgmsk64", [64, 64], I32) for _ in range(2)]
        P.dma("sp", msk64[0][:], self.din["c_hgmask64_f"], w=["hgmsk"])
        P.dma("sp", msk64[1][:], self.din["c_hgmask64_b"], w=["hgmsk"])
        for i in range(3):
            self.memset("pool", MT[i][:], 0.0, ["hgMT%d" % i])
        vsrc = S["hg_v"].rearrange("(c p) d -> p c d", p=64)
        with contextlib.ExitStack() as st1:
            vst = self.sb(st1, "hgvst", [64, 17, 256])
            for qd in range(4):
                P.dma("sp", vst[:], vsrc[:, qd * 17:(qd + 1) * 17, :], w=["hgvst"])
                self.cp("pool", vtok[:, qd * 17:(qd + 1) * 17, :], vst[:], ["hgvst"], ["hgvtok"])
            P.flush()
        Pv3 = Pb[:, 0:NT].rearrange("p (c t) -> p c t", t=64)
        Pn3 = Pb[:, 1:NT + 1].rearrange("p (c t) -> p c t", t=64)
        D3 = Db[:].rearrange("p (c t) -> p c t", t=64)
        bc = lambda a: a.to_broadcast([128, NCH, 64])
        ref_b = bc(Pv3[:, :, 32:33])
        p0_b = bc(Pv3[:, :, 0:1])
        p1_b = bc(Pn3[:, :, 63:64])
        for cc in range(2):
            rows = slice(cc * 128, (cc + 1) * 128)
            P.dma("sp", q[:], S["hg_q"][rows, :], w=["hgq"])
            for dr in range(2):
                DD = L + str(dr)
                if l == 0:
                    self.memset("dve", lbc[:, 0:1], 0.0, ["hglb"])
                else:
                    self.tt("dve", lbc[:, 0:1], self.pc("hg_lb1_" + DD, cc, 1), self.pc("hg_lb0_" + DD, cc, 1), ALU.subtract,
                            ["pcol"], ["hglb"])
                    self.act(lbc[:, 0:1], lbc[:, 0:1], AF.Sigmoid, ["hglb"], ["hglb"])
                self.ts("dve", lbc[:, 1:2], lbc[:, 0:1], -1.0, ALU.mult, ["hglb"], ["hglb"], s2=1.0, op1=ALU.add)
                self.ts("dve", lbc[:, 2:3], lbc[:, 1:2], -1.0, ALU.mult, ["hglb"], ["hglb"])
                P.dma("sp", zs[:], S["hg_f"][dr, rows, :], w=["hgzs"])
                self.act(zs[:], zs[:], AF.Sigmoid, ["hgzs"], ["hgzs"])
                self.ts("dve", Db[:], zs[:], lbc[:, 1:2], ALU.mult, ["hgzs", "hglb"], ["hgD"], s2=lbc[:, 0:1], op1=ALU.add)
                self.act(Db[:], Db[:], AF.Ln, ["hgD"], ["hgD"])
                self.ts("pool", zs[:], zs[:], lbc[:, 2:3], ALU.mult, ["hgzs", "hglb"], ["hgzs"], s2=lbc[:, 1:2], op1=ALU.add)
                self.memset("dve", Pb[:, 0:1], 0.0, ["hgP"])
                ones_bc = self.ones_f[:, 0:1].to_broadcast([128, NT])
                self.P.op("dve", lambda e, ones_bc=ones_bc: e.tensor_tensor_scan(
                    out=Pb[:, 1:NT + 1], data0=ones_bc, data1=Db[:], initial=0.0, op0=ALU.mult, op1=ALU.add),
                    ["hgD", "const"], ["hgP"])
                arr3 = Pn3 if dr == 0 else Pv3
                sg = 1.0 if dr == 0 else -1.0
                self.tt("dve", D3, arr3, ref_b, ALU.subtract, ["hgP"], ["hgD"])
                XB4 = XB[:].rearrange("p (c h t) -> p c h t", h=2, t=32)
                q4 = q[:].rearrange("p (c h t) -> p c h t", h=2, t=32)
                k4 = zs[:].rearrange("p (c h t) -> p c h t", h=2, t=32)
                e4 = kGf[:].rearrange("p (c h t) -> p c h t", h=2, t=32)
                hq, hk = (1, 0) if dr == 0 else (0, 1)
                self.act(kGf[:], Db[:], AF.Exp, ["hgD"], ["hgkG"], scale=sg)
                self.tt("pool", XB4[:, :, hq, :], q4[:, :, hq, :], e4[:, :, hq, :], ALU.mult, ["hgq", "hgkG"], ["hgXB"])
                self.act(kGf[:], Db[:], AF.Exp, ["hgD", "hgXB"], ["hgkG"], scale=-sg)
                self.tt("dve", XB4[:, :, hk, :], k4[:, :, hk, :], e4[:, :, hk, :], ALU.mult, ["hgzs", "hgkG"], ["hgXB"])
                Ph3 = Pb[:, 0:NT].rearrange("p (c t) -> p c t", t=32)
                a32 = (Pb[:, 1:NT + 1] if dr == 0 else Pb[:, 0:NT]).rearrange("p (c t) -> p c t", t=32)
                self.tt("dve", Db[:].rearrange("p (c t) -> p c t", t=32), a32,
                        Ph3[:, :, 16:17].to_broadcast([128, NT // 32, 32]), ALU.subtract, ["hgP", "hgXB"], ["hgD"])
                self.act(kGf[:], Db[:], AF.Exp, ["hgD", "hgXB"], ["hgkG"], scale=sg)
                self.tt("pool", qg[:], q[:], kGf[:], ALU.mult, ["hgq", "hgkG"], ["hgqg"])
                self.act(kGf[:], Db[:], AF.Exp, ["hgD", "hgqg"], ["hgkG"], scale=-sg)
                self.tt("dve", kg[:], zs[:], kGf[:], ALU.mult, ["hgzs", "hgkG"], ["hgkg"])
                self.tt("pool", D3, arr3, p0_b if dr == 0 else p1_b, ALU.subtract, ["hgP", "hgkg"], ["hgD"])
                self.act(kGf[:], Db[:], AF.Exp, ["hgD", "hgkg"], ["hgkG"], scale=sg)
                self.tt("dve", qG[:], q[:], kGf[:], ALU.mult, ["hgq", "hgkG"], ["hgqG"])
                self.tt("pool", D3, arr3, p1_b if dr == 0 else p0_b, ALU.subtract, ["hgP", "hgqG"], ["hgD"])
                self.act(kGf[:], Db[:], AF.Exp, ["hgD", "hgqG"], ["hgkG"], scale=-sg)
                self.tt("dve", kGf[:], kGf[:], zs[:], ALU.mult, ["hgzs", "hgkG"], ["hgkG"])
                self.tt("dve", dec[:], Pb[:, 64:NT + 1:64], Pb[:, 0:NT:64], ALU.subtract, ["hgP"], ["hgdec"])
                self.act(dec[:], dec[:], AF.Exp, ["hgdec"], ["hgdec"])
                self.memset("dve", Sf[:], 0.0, ["hgSf"])
                for i in range(2):
                    self.memset("pool", Sbs[i][:], 0.0, ["hgSb%d" % i])
                for i in range(3):
                    self.memset("pool", MT[i][:], 0.0, ["hgMT%d" % i])
                order = list(range(NCH)) if dr == 0 else [3, 2, 1, 0] + list(range(NCH - 1, 3, -1))
                grp_of = lambda c: -1 if c < 4 else (c - 4) // 8
                bank_of = {}
                gi = 0
                for c in order:
                    g = grp_of(c)
                    if g not in bank_of:
                        bank_of[g] = 6 + gi % 2
                        gi += 1

                def front(ci, c):
                    t0 = c * 64
                    mt, mk = MT[ci % 3], "hgMT%d" % (ci % 3)
                    kt, kk = kGt[ci % 3], "hgkGt%d" % (ci % 3)
                    psT, ptk = self.psum()
                    self.tr(psT[0:64, 0:128], kGf[:, t0:t0 + 64], ["hgkG"], [ptk])
                    self.cp("act", kt[0:64, :], psT[0:64, 0:128], [ptk], [kk])
                    psA, pak = self.psum()
                    for hh in range(2):
                        a0 = t0 + 32 * hh
                        self.mm(psA[32 * hh:32 * hh + 32, 32 * hh:32 * hh + 32], kg[:, a0:a0 + 32], qg[:, a0:a0 + 32],
                                True, True, ["hgkg", "hgqg"], [pak])
                    if dr == 0:
                        ob = (0, 32); kB = XB[:, t0:t0 + 32]; qB = XB[:, t0 + 32:t0 + 64]
                    else:
                        ob = (32, 0); kB = XB[:, t0 + 32:t0 + 64]; qB = XB[:, t0:t0 + 32]
                    self.mm(psA[ob[0]:ob[0] + 32, ob[1]:ob[1] + 32], kB, qB, True, True, ["hgXB"], [pak])
                    self.P.op("dve", lambda e, mt=mt, psA=psA, m=msk64[dr]: e.copy_predicated(
                        out=mt[0:64, :], mask=m[:, :], data=psA[0:64, 0:64]), [pak, "hgmsk"], [mk])
                    return mt, mk, kt, kk

                def back(ci, c, fr):
                    mt, mk, kt, kk = fr
                    t0 = c * 64
                    g = grp_of(c)
                    bk = bank_of[g]
                    psO, pok = self.ps[bk], "ps%d" % bk
                    gstart = 0 if g < 0 else 4 + 8 * g
                    col = (c - gstart) * 64
                    vch = vtok[0:64, c, cc * 128:(cc + 1) * 128]
                    sb_prev, sbk_prev = Sbs[(ci + 1) % 2], "hgSb%d" % ((ci + 1) % 2)
                    sb_new, sbk_new = Sbs[ci % 2], "hgSb%d" % (ci % 2)
                    psS, psk = self.psum()
                    self.mm(psS[:, 0:128], kt[0:64, :], vch, True, True, [kk, "hgvtok"], [psk])
                    self.mm(psO[:, col:col + 64], sb_prev[:], qG[:, t0:t0 + 64], True, False, [sbk_prev, "hgqG"], [pok])
                    self.mm(psO[:, col:col + 64], vch, mt[0:64, :], False, True, ["hgvtok", mk], [pok])
                    self.stt(Sf[:], Sf[:], dec[:, c:c + 1], psS[:, 0:128], ALU.mult, ALU.add, ["hgSf", "hgdec", psk], ["hgSf"])
                    self.cp("act", sb_new[:], Sf[:], ["hgSf"], [sbk_new])
                    last_of_group = (ci + 1 == len(order)) or (grp_of(order[ci + 1]) != g)
                    if last_of_group:
                        n = 256 if g < 0 else 512
                        tg = gstart * 64
                        if dr == 0:
                            self.cp("act", osum[:, tg:tg + n], psO[:, 0:n], [pok], ["hgos"])
                        else:
                            self.tt("dve", osum[:, tg:tg + n], psO[:, 0:n], osum[:, tg:tg + n], ALU.add, [pok, "hgos"], ["hgos"])

                prev = None
                for ci, c in enumerate(order):
                    fr = front(ci, c)
                    if prev is not None:
                        back(*prev)
                    prev = (ci, c, fr)
                back(*prev)
            self.act(qg[:], osum[:], AF.Square, ["hgos"], ["hgqg"])
            for i in range(9):
                t0 = i * 512
                n = min(512, NT - t0)
                ps, pk = self.psum()
                self.mm(ps[:, :n], self.ones_b[:], qg[:, t0:t0 + n], True, True, ["const", "hgqg"], [pk])
                self.act(Db[:, t0:t0 + n], ps[:, :n], AF.Ln, [pk], ["hgD"], scale=1.0 / 128, bias=self.epsc[:, 0:1])
            self.act(Db[:], Db[:], AF.Exp, ["hgD"], ["hgD"], scale=-0.5)
            self.stt(osum[:], osum[:], self.pc("hg_nw" + L, cc, 1), Db[:], ALU.mult, ALU.mult, ["hgos", "hgD", "pcol"], ["hgos"])
            P.dma("sp", zs[:], S["hg_g"][rows, :], w=["hgzs"])
            self.tt("dve", kg[:], osum[:], zs[:], ALU.mult, ["hgos", "hgzs"], ["hgkg"])
            P.dma(STQ, S["ybuf"][2 * CH + cc * 128:2 * CH + (cc + 1) * 128, :], kg[:], r=["hgkg"], w=["ybuf"])
        P.flush()
    self.ps_rot = list(range(8))


Kern.phase_hg = _phase_hg


def _phase_m2(self, l):
    P = self.P
    S = self.scr
    L = str(l)
    NH = 4
    self.ps_rot = list(range(8))
    with contextlib.ExitStack() as st0:
        ytok = self.sb(st0, "m2ytok", [128, NTILE, 256])
        with contextlib.ExitStack() as st:
            stage = self.sb(st, "m2stage", [128, NT])
            BTb = self.sb(st, "m2BT", [128, NT], BF16)
            CTb = self.sb(st, "m2CT", [128, NT], BF16)
            Btok = self.sb(st, "m2Btok", [128, NTILE, 128], BF16)
            xstok = self.sb(st, "m2xstok", [128, NTILE, 256])
            Xtok = self.sb(st, "m2Xtok", [128, NTILE, 256], BF16)
            Xdtok = self.sb(st, "m2Xdtok", [128, NTILE, 256], BF16)
            dtraw = self.sb(st, "m2dtraw", [128, NTILE, NH])
            dt = self.sb(st, "m2dt", [128, NTILE, NH])
            dtA = self.sb(st, "m2dtA", [128, NTILE, NH])
            Qt = self.sb(st, "m2Q", [128, NTILE, NH])
            Qtot = self.sb(st, "m2Qtot", [128, NTILE, NH])
            Eoff = self.sb(st, "m2Eoff", [128, NTILE, NH])
            decs = self.sb(st, "m2decs", [128, NTILE, NH])
            cdec = self.sb(st, "m2cdec", [128, NTILE, NH])
            hp = self.sb(st, "m2hp", [128, 3, NH])
            Dt = self.sb(st, "m2Dt", [128, 256])
            tri = [self.sb(st, "m2tri", [128, 128]) for _ in range(2)]
            mask = [self.sb(st, "m2mask", [128, 128]) for _ in range(2)]
            GMs = [self.sb(st, "m2GM", [128, 128]) for _ in range(3)]
            Lhs = [self.sb(st, "m2Lh", [128, NH, 128]) for _ in range(3)]
            Exs = [self.sb(st, "m2Ex", [128, NH, 128]) for _ in range(3)]
            MT = [self.sb(st, "m2MT", [128, NH, 128], BF16) for _ in range(3)]
            tmps = [self.sb(st, "m2tmp", [128, 256]) for _ in range(3)]
            tmp = tmps[0]
            stf = self.sb(st, "m2stf", [128, 256])
            stbs = [self.sb(st, "m2stb", [128, 256], BF16) for _ in range(2)]
            P.dma("sp", tri[0][:], self.din["c_tri_incl"], w=["m2tri"])
            P.dma("sp", tri[1][:], self.din["c_tri_excl"], w=["m2tri"])
            P.dma("sp", mask[0][:], self.din["c_m2mask_f"], w=["m2mask"])
            P.dma("sp", mask[1][:], self.din["c_m2mask_b"], w=["m2mask"])
            negm = [self.sb(st, "m2neg", [128, NH, 128]) for _ in range(2)]
            negb = [self.sb(st, "m2negb", [128, NH, 128], BF16) for _ in range(2)]
            trib = [self.sb(st, "m2trib", [128, 128], BF16) for _ in range(2)]
            Rrs = [self.sb(st, "m2Rr", [128, 2 * NH, 128], BF16) for _ in range(3)]
            dtAhl = self.sb(st, "m2dtAhl", [128, NTILE, 2, NH], BF16)
            dtmp = self.sb(st, "m2dtmp", [128, NTILE, NH])
            nQ = self.sb(st, "m2nQ", [128, NTILE, NH])
            for d_ in range(2):
                self.ts("dve", negm[d_][:], mask[d_][:].unsqueeze(1).to_broadcast([128, NH, 128]), -1.0, ALU.add, ["m2mask"], ["m2neg"],
                        s2=(30000.0 if d_ == 0 else -30000.0), op1=ALU.mult)
                self.cp("dve", negb[d_][:], negm[d_][:], ["m2neg"], ["m2neg"])
                self.cp("dve", trib[d_][:], tri[d_][:], ["m2tri"], ["m2tri"])
            P.dma("sp", dtraw[:].rearrange("p t h -> p (t h)"), S["m2_dt"], w=["m2dtraw"])
            b34 = lambda a: a.unsqueeze(1).to_broadcast([128, NTILE, NH])
            for dr in range(2):
                DD = L + str(dr)
                for i, nm in enumerate(("m2_dtb", "m2_alog")):
                    o = self.roff[nm + DD]
                    P.dma("sp", hp[:, i, :], self.din["prow"][:, o:o + NH], w=["m2hp"])
                o = self.roff["m2_d" + DD]
                P.dma("sp", Dt[:], self.din["prow"][:, o:o + 256], w=["m2Dt"])
                self.act(hp[:, 2, :], hp[:, 1, :], AF.Exp, ["m2hp"], ["m2hp"])
                self.ts("dve", hp[:, 2, :], hp[:, 2, :], -1.0, ALU.mult, ["m2hp"], ["m2hp"])
                self.tt("dve", dt[:], dtraw[:], b34(hp[:, 0, :]), ALU.add, ["m2dtraw", "m2hp"], ["m2dt"])
                self.act(dt[:], dt[:], AF.Exp, ["m2dt"], ["m2dt"])
                self.act(dt[:], dt[:], AF.Ln, ["m2dt"], ["m2dt"], bias=self.epsc[:, 1:2])
                self.tt("dve", dtA[:], dt[:], b34(hp[:, 2, :]), ALU.mult, ["m2dt", "m2hp"], ["m2dtA"])
                self.cp("dve", dtAhl[:, :, 0, :], dtA[:], ["m2dtA"], ["m2dtAhl"])
                self.tt("dve", dtmp[:], dtA[:], dtAhl[:, :, 0, :], ALU.subtract, ["m2dtA", "m2dtAhl"], ["m2dtmp"])
                self.cp("dve", dtAhl[:, :, 1, :], dtmp[:], ["m2dtmp"], ["m2dtAhl"])
                dtA2 = dtA[:].rearrange("p t h -> p (t h)")
                psq, pqk = self.psum()
                self.mm(psq[:, 0:NTILE * NH], tri[0][:], dtA2, True, True, ["m2tri", "m2dtA"], [pqk])
                pst, ptk = self.psum()
                self.mm(pst[:, 0:NTILE * NH], self.ones_f[:], dtA2, True, True, ["const", "m2dtA"], [ptk])
                Q2 = Qt[:].rearrange("p t h -> p (t h)")
                T2 = Qtot[:].rearrange("p t h -> p (t h)")
                self.cp("dve", T2, pst[:, 0:NTILE * NH], [ptk], ["m2Qtot"])
                if dr == 0:
                    self.cp("dve", Q2, psq[:, 0:NTILE * NH], [pqk], ["m2Q"])
                    self.act(Eoff[:], Qt[:], AF.Exp, ["m2Q"], ["m2Eoff"])
                    self.tt("dve", decs[:], Qtot[:], Qt[:], ALU.subtract, ["m2Q", "m2Qtot"], ["m2decs"])
                    self.act(decs[:], decs[:], AF.Exp, ["m2decs"], ["m2decs"])
                else:
                    self.tt("dve", Q2, psq[:, 0:NTILE * NH], dtA2, ALU.subtract, [pqk, "m2dtA"], ["m2Q"])
                    self.tt("dve", Eoff[:], Qtot[:], Qt[:], ALU.subtract, ["m2Q", "m2Qtot"], ["m2Eoff"])
                    self.act(Eoff[:], Eoff[:], AF.Exp, ["m2Eoff"], ["m2Eoff"])
                    self.act(decs[:], Qt[:], AF.Exp, ["m2Q"], ["m2decs"])
                self.act(cdec[:], Qtot[:], AF.Exp, ["m2Qtot"], ["m2cdec"])
                self.ts("dve", nQ[:], Qt[:], (-1.0 if dr == 0 else 1.0), ALU.mult, ["m2Q"], ["m2Q"])
                for ai, (r0, dstt, ck) in enumerate(((0, xstok, 0), (128, xstok, 1), (256, Btok, None), (384, None, None))):
                    P.dma("sp", stage[:], S["m2_x"][dr, r0:r0 + 128, :], w=["m2stage"])
                    if r0 == 256:
                        self.cp("pool", BTb[:], stage[:], ["m2stage"], ["m2BT"])
                    if r0 == 384:
                        self.cp("pool", CTb[:], stage[:], ["m2stage"], ["m2CT"])
                        continue
                    for g0 in range(0, NTILE, 4):
                        ng = min(4, NTILE - g0)
                        ps, pk = self.psum()
                        for i in range(ng):
                            self.tr(ps[:, i * 128:(i + 1) * 128], stage[:, (g0 + i) * 128:(g0 + i + 1) * 128], ["m2stage"], [pk])
                        src = ps[:, 0:ng * 128].rearrange("p (t c) -> p t c", c=128)
                        if dstt is xstok:
                            self.cp("act" if (g0 // 4) % 2 else "dve", xstok[:, g0:g0 + ng, ck * 128:(ck + 1) * 128], src, [pk], ["m2xstok"])
                        else:
                            self.cp("act" if (g0 // 4) % 2 else "dve", Btok[:, g0:g0 + ng, :], src, [pk], ["m2Btok"])
                xs4 = xstok[:].rearrange("p t (h q) -> p t h q", q=64)
                b64 = lambda a: a.unsqueeze(3).to_broadcast([128, NTILE, NH, 64])
                self.tt("dve", Xtok[:].rearrange("p t (h q) -> p t h q", q=64), xs4, b64(dt[:]), ALU.mult,
                        ["m2xstok", "m2dt"], ["m2Xtok"])
                self.tt("dve", decs[:], decs[:], dt[:], ALU.mult, ["m2decs", "m2dt"], ["m2decs"])
                self.tt("pool", Xdtok[:].rearrange("p t (h q) -> p t h q", q=64), xs4, b64(decs[:]), ALU.mult,
                        ["m2xstok", "m2decs"], ["m2Xdtok"])
                Db = Dt[:].unsqueeze(1).to_broadcast([128, NTILE, 256])
                if dr == 0:
                    self.tt("dve", ytok[:], xstok[:], Db, ALU.mult, ["m2xstok", "m2Dt"], ["m2ytok"])
                else:
                    self.tt("dve", xstok[:], xstok[:], Db, ALU.mult, ["m2xstok", "m2Dt"], ["m2xstok"])
                    self.tt("pool", ytok[:], ytok[:], xstok[:], ALU.add, ["m2xstok", "m2ytok"], ["m2ytok"])
                self.memset("dve", stf[:], 0.0, ["m2stf"])
                for i in range(2):
                    self.memset("pool", stbs[i][:], 0.0, ["m2stb%d" % i])
                order = list(range(NTILE)) if dr == 0 else [1, 0] + list(range(NTILE - 1, 1, -1))
                def front_a(ci, j):
                    tk = slice(j * 128, (j + 1) * 128)
                    Rr, rrk = Rrs[ci % 3], "m2Rr%d" % (ci % 3)
                    self.tt("dve", Rr[:].rearrange("p (a h) t -> p a h t", a=2),
                            trib[dr][:].unsqueeze(1).unsqueeze(1).to_broadcast([128, 2, NH, 128]),
                            dtAhl[:, j, :, :].unsqueeze(3).to_broadcast([128, 2, NH, 128]), ALU.mult, ["m2tri", "m2dtAhl"], [rrk])
                    psG, pgk = self.psum()
                    self.mm(psG[:, 0:128], BTb[:, tk], CTb[:, tk], True, True, ["m2BT", "m2CT"], [pgk])
                    psL, plk = self.psum()
                    R2 = Rr[:].rearrange("p g t -> p (g t)")
                    self.mm(psL[:, :], self.ones_b[:], R2[:, 0:512], True, False, ["const", rrk], [plk])
                    self.mm(psL[:, :], self.ones_b[:], R2[:, 512:1024], False, False, ["const", rrk], [plk])
                    self.mm(psL[:, :], self.ident_b[:], negb[dr][:].rearrange("p h t -> p (h t)"), False, True, ["const", "m2neg"], [plk])
                    return psG, pgk, psL, plk

                def front_b(ci, j, fa):
                    psG, pgk, psL, plk = fa
                    mt, mk = MT[ci % 3], "m2MT%d" % (ci % 3)
                    Ex, exk = Exs[ci % 3], "m2Ex%d" % (ci % 3)
                    for h in range(NH):
                        self.act(Ex[:, h, :], psL[:, h * 128:(h + 1) * 128], AF.Exp, [plk, "m2Q"], [exk],
                                 scale=(1.0 if dr == 0 else -1.0), bias=nQ[:, j, h:h + 1])
                    self.tt("dve", mt[:], Ex[:], psG[:, 0:128].unsqueeze(1).to_broadcast([128, NH, 128]), ALU.mult,
                            [exk, pgk], [mk])
                    return mt, mk

                def back(ci, j, fr):
                    mt, mk = fr
                    tk = slice(j * 128, (j + 1) * 128)
                    tmp, tmk = tmps[ci % 3], "m2tmp%d" % (ci % 3)
                    st_prev, sk_prev = stbs[(ci + 1) % 2], "m2stb%d" % ((ci + 1) % 2)
                    st_new, sk_new = stbs[ci % 2], "m2stb%d" % (ci % 2)
                    psS, psk = self.psum()
                    self.mm(psS[:, 0:256], Btok[:, j, :], Xdtok[:, j, :], True, True, ["m2Btok", "m2Xdtok"], [psk])
                    psO, pok = self.psum()
                    self.mm(psO[:, 0:256], CTb[:, tk], st_prev[:], True, True, ["m2CT", sk_prev], [pok])
                    for h in range(NH):
                        self.mm(psO[:, 256 + h * 64:256 + (h + 1) * 64], mt[:, h, :], Xtok[:, j, h * 64:(h + 1) * 64], True, True,
                                [mk, "m2Xtok"], [pok])
                    self.tt("pool", stf[:].rearrange("p (h q) -> p h q", q=64), stf[:].rearrange("p (h q) -> p h q", q=64),
                            cdec[:, j, :].unsqueeze(2).to_broadcast([128, NH, 64]), ALU.mult, ["m2stf", "m2cdec"], ["m2stf"])
                    self.tt("dve", stf[:], psS[:, 0:256], stf[:], ALU.add, [psk, "m2stf"], ["m2stf"])
                    self.cp("act", st_new[:], stf[:], ["m2stf"], [sk_new])
                    yk = ("m2ytok", j)
                    self.tt("dve", tmp[:].rearrange("p (h q) -> p h q", q=64), psO[:, 0:256].rearrange("p (h q) -> p h q", q=64),
                            Eoff[:, j, :].unsqueeze(2).to_broadcast([128, NH, 64]), ALU.mult, [pok, "m2Eoff"], [tmk])
                    self.tt("dve", ytok[:, j, :], psO[:, 256:512], ytok[:, j, :], ALU.add, [pok, "m2ytok", yk], [yk])
                    self.tt("pool", ytok[:, j, :], ytok[:, j, :], tmp[:], ALU.add, [tmk, yk], [yk])

                nO = len(order)
                fas = {}
                frs = {}
                for it in range(nO + 2):
                    if it < nO:
                        fas[it] = front_a(it, order[it])
                    if 0 <= it - 1 < nO:
                        frs[it - 1] = front_b(it - 1, order[it - 1], fas.pop(it - 1))
                    if 0 <= it - 2 < nO:
                        back(it - 2, order[it - 2], frs.pop(it - 2))
                self.P.op("pool", lambda e: e.memset(tmp[:, 0:1], 0.0), [("m2ytok", j) for j in range(NTILE)] + ["m2tmp0"], ["m2ytok", "m2tmp0"])
            P.flush()
        with contextlib.ExitStack() as st:
            yT = [self.sb(st, "m2yT", [128, NT]) for _ in range(2)]
            gt = self.sb(st, "m2gt", [128, NT])
            sq = [self.sb(st, "m2sq", [128, NT], BF16) for _ in range(2)]
            rs = self.sb(st, "m2rs", [128, NT])
            yb = self.sb(st, "m2yb", [128, NT], BF16)
            for cc in range(2):
                yk = "m2yT%d" % cc
                for g0 in [0] + list(range(2, NTILE, 4)):
                    ng = 2 if g0 == 0 else 4
                    ps, pk = self.psum()
                    for i in range(ng):
                        self.tr(ps[:, i * 128:(i + 1) * 128], ytok[:, g0 + i, cc * 128:(cc + 1) * 128], ["m2ytok"], [pk])
                    if g0 == 0:
                        self.cp("act", yT[cc][:, 0:CTX], ps[:, 0:256], [pk], [yk])
                    else:
                        gi = (g0 - 2) // 4
                        dst = yT[cc][:, CTX:NT].rearrange("p (r w) -> p w r", w=64)[:, 8 * gi:8 * gi + 8, :]
                        self.cp("act" if gi % 2 else "dve", dst, ps[:, 0:512].rearrange("p (w r) -> p w r", r=64), [pk], [yk])
                P.dma("sp", gt[:], S["m2_g"][cc * 128:(cc + 1) * 128, :], w=["m2gt"])
                self.tt("dve", yT[cc][:], yT[cc][:], gt[:], ALU.mult, [yk, "m2gt"], [yk])
                self.act(sq[cc][:], yT[cc][:], AF.Square, [yk], ["m2sq%d" % cc])
            for i in range(9):
                t0 = i * 512
                n = min(512, NT - t0)
                ps, pk = self.psum()
                for cc in range(2):
                    self.mm(ps[:, :n], self.ones_b[:], sq[cc][:, t0:t0 + n], cc == 0, cc == 1, ["const", "m2sq%d" % cc], [pk])
                self.act(rs[:, t0:t0 + n], ps[:, :n], AF.Ln, [pk], ["m2rs"], scale=1.0 / 256, bias=self.epsc[:, 0:1])
            self.act(rs[:], rs[:], AF.Exp, ["m2rs"], ["m2rs"], scale=-0.5)
            for cc in range(2):
                self.stt(yT[cc][:], yT[cc][:], self.pc("m2_nw" + L, cc, 1), rs[:], ALU.mult, ALU.mult,
                         ["m2yT%d" % cc, "m2rs", "pcol"], ["m2yT%d" % cc])
                self.cp("pool", yb[:], yT[cc][:], ["m2yT%d" % cc], ["m2yb"])
                P.dma(STQ, S["ybuf"][3 * CH + cc * 128:3 * CH + (cc + 1) * 128, :], yb[:], r=["m2yb"], w=["ybuf"])
            P.flush()


Kern.phase_m2 = _phase_m2


MAGIC = 12582912.0


def _phase_hy(self, l, n, tag, tok0):
    P = self.P
    S = self.scr
    L = str(l)
    NB = n // 128
    tabC = self.din["tabC_" + tag]
    tabS = self.din["tabS_" + tag]
    khat = S["khat_" + tag]
    FG = min(8, NB)
    FGK = min(4, NB)
    TBG = min(4, NB)
    TW = min(2048, n)
    CW = min(512, n)
    CQ = CW
    NQ = TW // CW
    with contextlib.ExitStack() as st0:
        HS = self.sb(st0, "hyHS", [128, NB, 512], BF16)
        HD = self.sb(st0, "hyHD", [128, NB, 512], BF16)
        invn = self.sb(st0, "hyinvn", [128, 512])
        NTB = 16 if False else 8
        tabb = [self.sb(st0, "hytab", [128, TBG * CQ], BF16) for _ in range(NTB)]
        tcnt = [0]

        def tab_load(tab, lg, fq):
            i = tcnt[0] % NTB; tcnt[0] += 1
            tk = "hytab%d" % i
            dst = tabb[i][:].rearrange("p (j c) -> p j c", c=CQ)
            P.dma("sp", dst, tab[lg, fq], w=[tk])
            return dst, tk

        with contextlib.ExitStack() as st:
            zT = self.sb(st, "hyzT", [HY_EMB, n + 1])
            h1 = self.sb(st, "hyh1", [64, n + 1])
            h2 = self.sb(st, "hyh2", [64, n + 1])
            w1 = self.sb(st, "hyw1", [HY_EMB, 64])
            w2 = self.sb(st, "hyw2", [64, 64])
            w3 = self.sb(st, "hyw3", [64, 2, 512])
            frb = self.sb(st, "hyfrb", [64, 2])
            delta = self.sb(st, "hydelta", [128, 256])
            arg = self.sb(st, "hyarg", [64, 512])
            tq = self.sb(st, "hytq", [64, 512])
            NR = 3
            decs = [[self.sb(st, "hydec", [128, 256]) for _ in range(2)] for _ in range(NR)]
            hfbs = [[self.sb(st, "hyhfb", [128, 512]) for _ in range(2)] for _ in range(NR)]
            abs_ = [[self.sb(st, "hyab", [128, 512]) for _ in range(2)] for _ in range(NR)]
            abbs = [self.sb(st, "hyabb", [128, 512], BF16) for _ in range(NR)]
            P.dma("sp", zT[:], self.din["z_" + tag], w=["hyzT"])
            P.dma("sp", w1[:], self.din["hy_w1"][l], w=["hyw"])
            P.dma("sp", w2[:], self.din["hy_w2"][l], w=["hyw"])
            P.dma("sp", w3[:], self.din["hy_w3"][l], w=["hyw"])
            o = self.roff["hy_delta"]
            P.dma("sp", delta[:], self.din["prow"][:, o:o + 256], w=["hydelta"])
            fr = self.pc("hy_fr" + L)[0:64, :]
            self.tt("dve", frb[:, 0:1], self.pc("hy_b1" + L)[0:64, :], fr, ALU.mult, ["pcol"], ["hyfrb"])
            self.tt("dve", frb[:, 1:2], self.pc("hy_b2" + L)[0:64, :], fr, ALU.mult, ["pcol"], ["hyfrb"])
            for li, (wt, kdim, src, dst, dk) in enumerate(((w1, HY_EMB, zT, h1, "hyh1"), (w2, 64, h1, h2, "hyh2"))):
                sk = "hyzT" if li == 0 else "hyh1"
                for c0 in range(0, n, 512):
                    m = min(512, n - c0)
                    ps, pk = self.psum()
                    self.mm(ps[0:64, :m], wt[0:kdim, :], src[0:kdim, c0:c0 + m], True, True, ["hyw", sk], [pk])
                    self.ts("dve", arg[:, :m], ps[0:64, :m], fr, ALU.mult, [pk, "pcol", "hyfrb"], ["hyarg"],
                            s2=frb[:, li:li + 1], op1=ALU.add)
                    self.ts("dve", tq[:, :m], arg[:, :m], 1.0 / (2 * PI), ALU.mult, ["hyarg"], ["hytq"], s2=MAGIC, op1=ALU.add)
                    self.ts("dve", tq[:, :m], tq[:, :m], -MAGIC, ALU.add, ["hytq"], ["hytq"], s2=-2 * PI, op1=ALU.mult)
                    self.tt("dve", arg[:, :m], arg[:, :m], tq[:, :m], ALU.add, ["hyarg", "hytq"], ["hyarg"])
                    self.ts("dve", arg[:, :m], arg[:, :m], 3.141592, ALU.min, ["hyarg"], ["hyarg"], s2=-3.141592, op1=ALU.max)
                    self.act(dst[:, c0:c0 + m], arg[:, :m], AF.Sin, ["hyarg"], [dk])
            self.memset("dve", h2[:, n:n + 1], 0.0, ["hyh2"])
            psN, pnk = self.ps[7], "ps7"
            self.ps_rot = [0, 1, 2, 3, 4, 5, 6]
            for lb in range(NB):
                r3 = lb % NR
                dec, hfb, ab, abb = decs[r3], hfbs[r3], abs_[r3], abbs[r3]
                sf = str(r3)
                pF, pfk = self.psum()
                pB, pbk = self.psum()
                self.mm(pF[:, :], h2[:, lb * 128:lb * 128 + 128], w3[:, 0, :], True, True, ["hyh2", "hyw"], [pfk])
                self.mm(pB[:, :], h2[:, lb * 128 + 1:lb * 128 + 129], w3[:, 1, :], True, True, ["hyh2", "hyw"], [pbk])
                self.act(dec[0][:], delta[:], AF.Exp, ["hydelta", "pcol"], ["hydec0" + sf], scale=self.pc("negt_f" + tag, lb, 1))
                self.act(dec[1][:], delta[:], AF.Exp, ["hydelta", "pcol"], ["hydec1" + sf], scale=self.pc("negt_b" + tag, lb, 1))
                v3 = lambda a: a.rearrange("p (o c) -> p o c", c=256)
                bo = lambda a: a.unsqueeze(1).to_broadcast([128, 2, 256])
                self.tt("dve", v3(hfb[0][:]), v3(pF[:, :]), bo(dec[0][:]), ALU.mult, [pfk, "hydec0" + sf], ["hyhf" + sf])
                self.tt("dve", v3(hfb[1][:]), v3(pB[:, :]), bo(dec[1][:]), ALU.mult, [pbk, "hydec1" + sf], ["hyhb" + sf])
                self.tt("pool", HS[:, lb, :], hfb[0][:], hfb[1][:], ALU.add, ["hyhf" + sf, "hyhb" + sf], [("hyHS", lb)])
                self.tt("dve", HD[:, lb, :], hfb[0][:], hfb[1][:], ALU.subtract, ["hyhf" + sf, "hyhb" + sf], [("hyHS", lb)])
                self.act(ab[0][:], hfb[0][:], AF.Abs, ["hyhf" + sf], ["hyab0" + sf])
                self.act(ab[1][:], hfb[1][:], AF.Abs, ["hyhb" + sf], ["hyab1" + sf])
                self.tt("pool", abb[:], ab[0][:], ab[1][:], ALU.add, ["hyab0" + sf, "hyab1" + sf], ["hyabb" + sf])
                self.mm(psN[:, :], self.ones_b[:], abb[:], lb == 0, lb == NB - 1, ["const", "hyabb" + sf], [pnk])
            self.ts("dve", invn[:], psN[:, :], HY_EPS_, ALU.add, [pnk], ["hyinvn"])
            self.P.op("dve", lambda e: e.reciprocal(out=invn[:], in_=invn[:]), ["hyinvn"], ["hyinvn"])
            self.ts("dve", invn[:], invn[:], 1.0 / n, ALU.mult, ["hyinvn"], ["hyinvn"])
            P.flush()
        self.ps_rot = list(range(8))
        with contextlib.ExitStack() as st:
            ka = [self.sb(st, "hyka", [128, 512]) for _ in range(2)]
            kb = [self.sb(st, "hykb", [128, 512]) for _ in range(2)]
            kr = [self.sb(st, "hykr", [128, 512]) for _ in range(2)]
            ki = [self.sb(st, "hyki", [128, 512]) for _ in range(2)]
            it = 0
            for fg in range(NB // FGK):
                for lg in range(NB // TBG):
                    Ct, ck = tab_load(tabC, lg, fg)
                    St, sk = tab_load(tabS, lg, fg)
                    for j in range(TBG):
                        lb = lg * TBG + j
                        for fb in range(FGK):
                            self.mm(self.ps[fb][:, :], Ct[:, j, fb * 128:(fb + 1) * 128], HS[:, lb, :], lb == 0, lb == NB - 1,
                                    [ck, ("hyHS", lb)], ["ps%d" % fb])
                            self.mm(self.ps[4 + fb][:, :], St[:, j, fb * 128:(fb + 1) * 128], HD[:, lb, :], lb == 0, lb == NB - 1,
                                    [sk, ("hyHS", lb)], ["ps%d" % (4 + fb)])
                for fb in range(FGK):
                    F = fg * FGK + fb
                    i2 = it % 2; it += 1
                    ch = self.pc("chalf" + tag, F, 1)
                    sh = self.pc("shalf" + tag, F, 1)
                    self.tt("dve", ka[i2][:], self.ps[fb][:, :], invn[:], ALU.mult, ["ps%d" % fb, "hyinvn"], ["hyka%d" % i2])
                    self.tt("dve", kb[i2][:], self.ps[4 + fb][:, :], invn[:], ALU.mult, ["ps%d" % (4 + fb), "hyinvn"], ["hykb%d" % i2])
                    self.act(kr[i2][:], ka[i2][:], AF.Copy, ["hyka%d" % i2, "pcol"], ["hykr%d" % i2], scale=ch)
                    self.stt(kr[i2][:], kb[i2][:], sh, kr[i2][:], ALU.mult, ALU.add, ["hykb%d" % i2, "hykr%d" % i2, "pcol"], ["hykr%d" % i2])
                    self.act(ki[i2][:], kb[i2][:], AF.Copy, ["hykb%d" % i2, "pcol"], ["hyki%d" % i2], scale=ch)
                    self.stt(ki[i2][:], ka[i2][:], sh, ki[i2][:], ALU.mult, ALU.subtract, ["hyka%d" % i2, "hyki%d" % i2, "pcol"], ["hyki%d" % i2])
                    P.dma(STQ, khat[F, 0], kr[i2][:], r=["hykr%d" % i2], w=["khat"])
                    P.dma(STQ, khat[F, 1], ki[i2][:], r=["hyki%d" % i2], w=["khat"])
            P.flush()
    with contextlib.ExitStack() as st:
        NTB = 16 if True else 8
        tabb = [self.sb(st, "hytab", [128, TBG * CQ], BF16) for _ in range(NTB)]
        tcnt = [0]

        def tab_load(tab, lg, fq):
            i = tcnt[0] % NTB; tcnt[0] += 1
            tk = "hytab%d" % i
            dst = tabb[i][:].rearrange("p (j c) -> p j c", c=CQ)
            P.dma("sp", dst, tab[lg, fq], w=[tk])
            return dst, tk
        zT = self.sb(st, "hyz", [128, 2, n])
        ztok = self.sb(st, "hyztok", [128, NB, 256], BF16)
        Yh = self.sb(st, "hyYh", [128, NB, 2, 256], BF16)
        Kt = [self.sb(st, "hyKt", [128, 2, 256]) for _ in range(3)]
        ta = [self.sb(st, "hyta", [128, 256]) for _ in range(2)]
        tb_ = [self.sb(st, "hytb", [128, 256]) for _ in range(2)]
        xs = [self.sb(st, "hyxs", [128, 512]) for _ in range(4)]
        te = [self.sb(st, "hyte", [128, 512]) for _ in range(2)]
        yo = self.sb(st, "hyyo", [128, n], BF16)
        for cc in range(2):
            P.dma("sp", zT[:, cc, :], S["hy_u"][0, cc * 128:(cc + 1) * 128, tok0:tok0 + n], w=[("hyz", cc)])
        kcnt = 0
        xcnt = 0
        for o in range(2):
            for tbp in range(0, NB, 2):
                ps, pk = self.psum()
                for i in range(2):
                    for cc in range(2):
                        self.tr(ps[:, (i * 2 + cc) * 128:(i * 2 + cc + 1) * 128], zT[:, cc, (tbp + i) * 128:(tbp + i + 1) * 128],
                                [("hyz", cc)], [pk])
                self.cp("act" if (tbp // 2) % 2 else "dve", ztok[:, tbp:tbp + 2, :], ps[:, :].rearrange("p (t c) -> p t c", c=256),
                        [pk], [("hyztok", tbp // 2)])
            TPF = (FG * 128) // CQ
            for fg in range(NB // FG):
                for lg in range(NB // TBG):
                    tl = []
                    for q in range(TPF):
                        tl.append((tab_load(tabC, lg, fg * TPF + q), tab_load(tabS, lg, fg * TPF + q)))
                    for j in range(TBG):
                        tb = lg * TBG + j
                        zk = ("hyztok", tb // 2)
                        for fb in range(FG):
                            (Ct, ck), (St, sk) = tl[(fb * 128) // CQ]
                            cc0 = (fb * 128) % CQ
                            self.mm(self.ps[fb][:, 0:256], Ct[:, j, cc0:cc0 + 128], ztok[:, tb, :], tb == 0, False,
                                    [ck, zk], ["ps%d" % fb])
                            self.mm(self.ps[fb][:, 256:512], St[:, j, cc0:cc0 + 128], ztok[:, tb, :], False, tb == NB - 1,
                                    [sk, zk], ["ps%d" % fb])
                for fb in range(FG):
                    F = fg * FG + fb
                    kt = Kt[kcnt % 3]; kk = "hyKt%d" % (kcnt % 3); kcnt += 1
                    i2 = F % 2
                    P.dma("sp", kt[:], khat[F, :, :, o * 256:(o + 1) * 256].rearrange("r p c -> p r c"), w=[kk])
                    M1 = self.ps[fb][:, 0:256]
                    M2 = self.ps[fb][:, 256:512]
                    pk = "ps%d" % fb
                    self.tt("dve", ta[i2][:], M1, kt[:, 0, :], ALU.mult, [pk, kk], ["hyta%d" % i2])
                    self.tt("dve", tb_[i2][:], M2, kt[:, 1, :], ALU.mult, [pk, kk], ["hytb%d" % i2])
                    self.tt("pool", Yh[:, F, 0, :], ta[i2][:], tb_[i2][:], ALU.add, ["hyta%d" % i2, "hytb%d" % i2], [("hyYh", F)])
                    self.tt("dve", ta[i2][:], M2, kt[:, 0, :], ALU.mult, [pk, kk, ("hyYh", F)], ["hyta%d" % i2])
                    self.tt("dve", tb_[i2][:], M1, kt[:, 1, :], ALU.mult, [pk, kk, ("hyYh", F)], ["hytb%d" % i2])
                    self.tt("pool", Yh[:, F, 1, :], ta[i2][:], tb_[i2][:], ALU.subtract, ["hyta%d" % i2, "hytb%d" % i2], [("hyYh", F)])
            for th in range(n // TW):
                w0 = th * TW
                for lg in range(NB // TBG):
                    tl = [(tab_load(tabC, lg, th * NQ + q), tab_load(tabS, lg, th * NQ + q)) for q in range(NQ)]
                    for j in range(TBG):
                        fb = lg * TBG + j
                        for cc in range(2):
                            for q in range(NQ):
                                bank = cc * NQ + q
                                (Ct, ck), (St, sk) = tl[q]
                                self.mm(self.ps[bank][:, 0:CW], Yh[:, fb, 0, cc * 128:(cc + 1) * 128], Ct[:, j, :],
                                        fb == 0, False, [ck, ("hyYh", fb)], ["ps%d" % bank])
                                self.mm(self.ps[bank][:, 0:CW], Yh[:, fb, 1, cc * 128:(cc + 1) * 128], St[:, j, :],
                                        False, fb == NB - 1, [sk, ("hyYh", fb)], ["ps%d" % bank])
                for cc in range(2):
                    for q in range(NQ):
                        bank = cc * NQ + q
                        t0 = w0 + q * CW
                        x = xs[xcnt % 4]; xk = "hyxs%d" % (xcnt % 4)
                        tt_ = te[xcnt % 2]; tk = "hyte%d" % (xcnt % 2); xcnt += 1
                        P.dma("sp", x[:, :CW], S["hy_u"][1 + o, cc * 128:(cc + 1) * 128, tok0 + t0:tok0 + t0 + CW], w=[xk])
                        self.stt(tt_[:, :CW], zT[:, cc, t0:t0 + CW], self.pc("hy_skip" + L, o * 2 + cc, 1), self.ps[bank][:, 0:CW],
                                 ALU.mult, ALU.add, [("hyz", cc), "ps%d" % bank, "pcol"], [tk])
                        self.tt("pool", zT[:, cc, t0:t0 + CW], tt_[:, :CW], x[:, :CW], ALU.mult, [tk, xk], [("hyz", cc)])
        for cc in range(2):
            for q in range(n // CW):
                t0 = q * CW
                x = xs[xcnt % 4]; xk = "hyxs%d" % (xcnt % 4); xcnt += 1
                P.dma("sp", x[:, :CW], S["hy_g"][cc * 128:(cc + 1) * 128, tok0 + t0:tok0 + t0 + CW], w=[xk])
                self.tt("dve", yo[:, t0:t0 + CW], zT[:, cc, t0:t0 + CW], x[:, :CW], ALU.mult, [("hyz", cc), xk], ["hyyo"])
            P.dma(STQ, S["ybuf"][cc * 128:(cc + 1) * 128, tok0:tok0 + n], yo[:], r=["hyyo"], w=["ybuf"])
        P.flush()


HY_EPS_ = 1e-6
Kern.phase_hy1 = _phase_hy


def _phase_hy_all(self, l, need_ctx=True):
    self.phase_hy1(l, SEQ, "L", CTX)
    if need_ctx:
        self.phase_hy1(l, CTX, "C", 0)


Kern.phase_hy = _phase_hy_all


def _phase_out(self, l, yf, xsrc, tiles, xdst=None, final_dst=None, lat_only=False):
    P = self.P
    S = self.scr
    w_out = self.din["w_out"]
    with contextlib.ExitStack() as st:
        wst = [self.sb(st, "wost", [128, 1024]) for _ in range(2)]
        wob = self.sb(st, "wob", [128, 16, 1024], BF16)
        gb = self.sb(st, "ogb", [128, 2, 1024])
        fw = self.sb(st, "ofw", [128, 1024])
        yt = [self.sb(st, "oyt", [128, 16, 512], BF16) for _ in range(2)]
        xo = [self.sb(st, "oxo", [128, 1024]) for _ in range(5)]
        tm = [self.sb(st, "otm", [128, 1024]) for _ in range(4)]
        junks = [self.sb(st, "ojunk", [128, 1024]) for _ in range(2)]
        ssqs = self.sb(st, "ossq", [128, 8])
        for kc in range(16):
            if self.mode == "ALL":
                j, hh = kc // 2, kc % 2
                br, cc = j // 2, j % 2
            else:
                hh, br, cc = kc // 8, (kc % 8) // 2, kc % 2
            r0 = br * 512 + hh * 256 + cc * 128
            P.dma("sp", wst[kc % 2][:], w_out[l, r0:r0 + 128, :], w=["wost%d" % (kc % 2)])
            self.cp("act" if kc % 2 else "dve", wob[:, kc, :], wst[kc % 2][:], ["wost%d" % (kc % 2)], ["wob"])
        for s in range(2):
            P.dma("sp", gb[:, s, :], S["gbc"][l, s], w=["ogb"])
        o = self.roff["final_w"]
        P.dma("sp", fw[:], self.din["prow"][:, o:o + 1024], w=["ofw"])
        yfv = yf.rearrange("(kc p) t -> p kc t", p=128)
        chunks = {}
        for ti in tiles:
            t0 = ti * 128
            if lat_only:
                ck = t0 // 512
            else:
                ck = -1 if t0 < CTX else (t0 - CTX) // 512
            chunks.setdefault(ck, []).append(ti)
        it = 0
        xi = 0
        for ck, tl in chunks.items():
            if lat_only:
                c0, cw = ck * 512, 512
            else:
                c0 = 0 if ck < 0 else CTX + ck * 512
                cw = CTX if ck < 0 else 512
            ybuf = yt[it % 2]; yk = "oyt%d" % (it % 2); it += 1
            P.dma("sp", ybuf[:, :, :cw], yfv[:, :, c0:c0 + cw], w=[yk])
            for ti in tl:
                off = ti * 128 - c0
                s = 1 if (ti < 2 and not lat_only) else 0
                pa, pak = self.psum()
                pb, pbk = self.psum()
                for kc in range(16):
                    self.mm(pa[:, :], ybuf[:, kc, off:off + 128], wob[:, kc, 0:512], kc == 0, kc == 15, [yk, "wob"], [pak])
                    self.mm(pb[:, :], ybuf[:, kc, off:off + 128], wob[:, kc, 512:1024], kc == 0, kc == 15, [yk, "wob"], [pbk])
                x = xo[xi % 5]; xk = "oxo%d" % (xi % 5)
                t = tm[xi % 4]; tk = "otm%d" % (xi % 4)
                junk = junks[xi % 2]; jk = "ojunk%d" % (xi % 2)
                ssq = ssqs[:, 2 * (xi % 4):2 * (xi % 4) + 2]; sqk = "ossq%d" % (xi % 4); xi += 1
                P.dma("sp", x[:], xsrc(ti), w=[xk])
                self.tt("dve", t[:, 0:512], pa[:, :], gb[:, s, 0:512], ALU.mult, [pak, "ogb"], [tk])
                self.tt("dve", t[:, 512:1024], pb[:, :], gb[:, s, 512:1024], ALU.mult, [pbk, "ogb"], [tk])
                self.tt("dve", x[:], x[:], t[:], ALU.add, [xk, tk], [xk])
                if xdst is not None:
                    for dst in xdst(ti):
                        P.dma(STQ, dst, x[:], r=[xk], w=["xres"])
                if final_dst is not None:
                    self.P.op("act", lambda e, x=x, junk=junk, ssq=ssq: e.activation(out=junk[:], in_=x[:], func=AF.Square,
                                                                                   accum_out=ssq[:, 0:1]), [xk], [jk, sqk])
                    self.act(ssq[:, 1:2], ssq[:, 0:1], AF.Ln, [sqk], [sqk], scale=1.0 / D, bias=self.epsc[:, 0:1])
                    self.act(ssq[:, 1:2], ssq[:, 1:2], AF.Exp, [sqk], [sqk], scale=-0.5)
                    self.stt(t[:], x[:], ssq[:, 1:2], fw[:], ALU.mult, ALU.mult, [xk, sqk, "ofw", tk], [tk])
                    P.dma(STQ, final_dst(ti), t[:], r=[tk], w=["final"])
        P.flush()


Kern.phase_out = _phase_out


_PROG_CACHE = {}


def _launch(mode, in_maps, poff, roff, npc, npr):
    nc = build_program(mode, poff, roff, npc, npr)
    res = run_bass_kernel_spmd(nc, in_maps, core_ids=list(range(8)))
    return res.results


MIX_KEYS = ("w_in", "rg_bd", "hy_w1", "hy_w2", "hy_w3", "c_tri_incl", "c_tri_excl", "c_hgmask_f", "c_hgmask_b", "c_hgmask64_f", "c_hgmask64_b",
            "c_m2mask_f", "c_m2mask_b", "tabC_L", "tabS_L", "tabC_C", "tabS_C", "z_L", "z_C")
BASE_KEYS = ("w_mod", "w_out", "pcol", "prow", "c_ident", "c_ones", "c_sel")


def kernel(**inputs):
    consts = const_inputs()
    preps = []
    for b in range(4):
        for h in range(2):
            d, poff, roff = host_prep(inputs, b, h)
            d.update(consts)
            preps.append(d)
    npc = preps[0]["pcol"].shape[1]
    npr = preps[0]["prow"].shape[1]
    maps = [{k: d[k] for k in BASE_KEYS + MIX_KEYS + ("x", "ctx")} for d in preps]
    res = _launch("ALL", maps, poff, roff, npc, npr)
    out = np.empty((4, SEQ, D), np.float32)
    for b in range(4):
        out[b] = res[2 * b]["out"]
    return out
```

```python
import contextlib
import math
import numpy as np
import ml_dtypes
import concourse.bass as bass
import concourse.mybir as mybir
from concourse.bass_utils import run_bass_kernel_spmd

F32 = mybir.dt.float32
BF16 = mybir.dt.bfloat16
I32 = mybir.dt.int32
AF = mybir.ActivationFunctionType
ALU = mybir.AluOpType
AX = mybir.AxisListType

D = 1024
SEQ = 4096
CTX = 256
NT = SEQ + CTX
NTILE = NT // 128
DEPTH = 2
W_BR = 512
CH = 256
IN_COLS = 7176
NCOL = 3588
EPS = 1e-6
HY_HID = 64
HY_EMB = 33
PI = math.pi

ENGS = ("pe", "dve", "act", "pool", "sp")
NSLOT = 12
STQ = "pool"


class Prog:
    def __init__(self, nc, self_sync=True):
        self.nc = nc
        self.self_sync = self_sync
        self.stack = contextlib.ExitStack()
        self.sem = {e: self.stack.enter_context(nc.semaphore("s_" + e)) for e in ENGS}
        self.cnt = {e: 0 for e in ENGS}
        self.dq = ("sp", "pool", "act")
        self.slot_sem = {q: [self.stack.enter_context(nc.semaphore(f"d_{q}{i}")) for i in range(NSLOT)]
                         for q in self.dq}
        self.slot_val = {q: [0] * NSLOT for q in self.dq}
        self.slot_next = {q: 0 for q in self.dq}
        self.known = {e: {} for e in ENGS}
        self.semobj = {}
        for e in ENGS:
            self.semobj["s_" + e] = self.sem[e]
        for q in self.dq:
            for i in range(NSLOT):
                self.semobj[f"d_{q}{i}"] = self.slot_sem[q][i]
        self.last_w = {}
        self.readers = {}
        self.rec = {e: [] for e in ENGS}
        self.n_ops = 0
        self.cc_sem = self.stack.enter_context(nc.semaphore("d_cc"))
        self.semobj["d_cc"] = self.cc_sem
        self.cc_val = 0

    def collective(self, emit, r=(), w=()):
        r = tuple(r); w = tuple(w)
        waits = self._deps("pool", r, w)
        self.cc_val += 1
        ev = ("d_cc", self.cc_val, "pool")
        self._commit(ev, r, w)
        self.rec["pool"].append((waits, emit, self.cc_sem, 1))
        self.n_ops += 1

    def _commit(self, ev, r, w):
        for k in r:
            lst = self.readers.setdefault(k, {})
            lst[ev[0]] = max(lst.get(ev[0], (0,))[0], ev[1]), ev[2]
        for k in w:
            self.last_w[k] = ev
            self.readers[k] = {}

    def op(self, eng, emit, r=(), w=()):
        r = tuple(r); w = tuple(w)
        waits = self._deps(eng, r, w)
        self.cnt[eng] += 1
        ev = ("s_" + eng, self.cnt[eng], eng)
        self._commit(ev, r, w)
        self.rec[eng].append((waits, emit, self.sem[eng], 1))
        self.n_ops += 1

    def _deps(self, eng, r, w):
        deps = {}

        def add(sn, v, src):
            if src == eng and (eng == "pe" or not self.self_sync) and not sn.startswith("d_"):
                return
            if v > deps.get(sn, 0):
                deps[sn] = v
        for k in r:
            ev = self.last_w.get(k)
            if ev is not None:
                add(*ev)
        for k in w:
            ev = self.last_w.get(k)
            if ev is not None:
                add(*ev)
            for sn, (v, src) in self.readers.get(k, {}).items():
                add(sn, v, src)
        out = []
        kn = self.known[eng]
        for sn, v in deps.items():
            if kn.get(sn, 0) < v:
                kn[sn] = v
                out.append((sn, v))
        return out

    def dma(self, q, out, in_, r=(), w=(), **kw):
        r = tuple(r); w = tuple(w)
        s = self.slot_next[q]
        self.slot_next[q] = (s + 1) % NSLOT
        sn = f"d_{q}{s}"
        waits = self._deps(q, r, w)
        prev = self.slot_val[q][s]
        if prev > 0 and self.known[q].get(sn, 0) < prev:
            self.known[q][sn] = prev
            waits.append((sn, prev))
        self.slot_val[q][s] = prev + 16
        ev = (sn, prev + 16, q)
        self._commit(ev, r, w)
        self.rec[q].append((waits, (lambda e, o=out, i=in_, k=kw: e.dma_start(out=o, in_=i, **k)),
                            self.slot_sem[q][s], 16))
        self.n_ops += 1

    def flush(self):
        final = []
        for e in ENGS:
            if self.cnt[e] > 0:
                final.append(("s_" + e, self.cnt[e]))
        for q in self.dq:
            for i in range(NSLOT):
                if self.slot_val[q][i] > 0:
                    final.append((f"d_{q}{i}", self.slot_val[q][i]))
        if self.cc_val > 0:
            final.append(("d_cc", self.cc_val))
        recs = {}
        for e in ENGS:
            ws = []
            for sn, v in final:
                if sn == "s_" + e:
                    continue
                if self.known[e].get(sn, 0) < v:
                    self.known[e][sn] = v
                    ws.append((sn, v))
            recs[e] = (self.rec[e], ws)
            self.rec[e] = []
        semobj = self.semobj

        def body(lst, ws):
            def f(engine):
                for waits, emit, sem, inc in lst:
                    for sn, v in waits:
                        engine.wait_ge(semobj[sn], v)
                    emit(engine).then_inc(sem, inc)
                for sn, v in ws:
                    engine.wait_ge(semobj[sn], v)
            return f
        with self.nc.Block() as block:
            block.tensor(body(*recs["pe"]))
            block.vector(body(*recs["dve"]))
            block.scalar(body(*recs["act"]))
            block.gpsimd(body(*recs["pool"]))
            block.sync(body(*recs["sp"]))
        self.last_w = {}
        self.readers = {}

    def close(self):
        self.stack.close()


class Pack:
    def __init__(self):
        self.cols = []
        self.off = {}
        self.n = 0

    def add(self, name, arr):
        arr = np.asarray(arr, np.float32)
        assert arr.shape[0] == 128, (name, arr.shape)
        arr = arr.reshape(128, -1)
        self.off[name] = self.n
        self.cols.append(arr)
        self.n += arr.shape[1]

    def build(self):
        return np.ascontiguousarray(np.concatenate(self.cols, axis=1))


def fm(vec):
    vec = np.asarray(vec, np.float32)
    return np.ascontiguousarray(vec.reshape(-1, 128).T)


def fm64(vec):
    out = np.zeros((128, 1), np.float32)
    out[:64, 0] = vec
    return out


def rowb(vec):
    vec = np.asarray(vec, np.float32).reshape(1, -1)
    return np.ascontiguousarray(np.broadcast_to(vec, (128, vec.shape[1])))


P_HY, P_HYG, P_RGX, P_RGG, P_HGQ, P_HGFF, P_HGFB, P_HGI, P_HGG, P_M2X, P_M2DT, P_M2G = (
    0, 1536, 2048, 2560, 3072, 3584, 4096, 4608, 5120, 5632, 6656, 6664)
C_HYV, C_HYX1, C_HYX2, C_HYG, C_RGX, C_RGG, C_HGQ, C_HGFF, C_HGFB, C_HGI, C_HGG, C_M2XS, C_M2B, C_M2C, C_M2G, C_M2DT = (
    0, 256, 512, 768, 1024, 1280, 1536, 1792, 2048, 2304, 2560, 2816, 3072, 3200, 3328, 3584)


def core_cols(h):
    r = lambda a, n: list(range(a, a + n))
    idx = []
    idx += r(P_HY + h * 256, 256) + r(P_HY + 512 + h * 256, 256) + r(P_HY + 1024 + h * 256, 256)
    idx += r(P_HYG + h * 256, 256)
    idx += r(P_RGX + h * 256, 256) + r(P_RGG + h * 256, 256)
    idx += r(P_HGQ + h * 256, 256) + r(P_HGFF + h * 256, 256) + r(P_HGFB + h * 256, 256)
    idx += r(P_HGI + h * 256, 256) + r(P_HGG + h * 256, 256)
    idx += r(P_M2X + h * 256, 256) + r(P_M2X + 512 + h * 128, 128) + r(P_M2X + 768 + h * 128, 128)
    idx += r(P_M2G + h * 256, 256) + r(P_M2DT + h * 4, 4)
    assert len(idx) == NCOL
    return np.array(idx)


_CONST_CACHE = {}


def dft_tables(n):
    key = ("dft", n)
    if key not in _CONST_CACHE:
        a = (np.arange(n, dtype=np.float64) + 0.5)
        ang = (2.0 * np.pi / (2 * n)) * np.outer(a, a)
        nb = n // 128
        tbg, cq = min(4, nb), min(512, n)

        def tile(t):
            t = t.astype(ml_dtypes.bfloat16).reshape(nb // tbg, tbg, 128, n // cq, cq)
            return np.ascontiguousarray(t.transpose(0, 3, 2, 1, 4))
        _CONST_CACHE[key] = (tile(np.cos(ang)), tile(np.sin(ang)))
    return _CONST_CACHE[key]


def hy_zfeat(n):
    f32 = np.float32
    t = np.linspace(0.0, 1.0, n, dtype=f32)[:, None]
    bands = np.linspace(1e-4, 16 - 1, 16, dtype=f32)[None]
    ang = (f32(2.0 * math.pi / n) * np.arange(n, dtype=f32)[:, None]) * bands
    z = np.concatenate([t, np.cos(ang), -np.sin(ang)], axis=-1).astype(f32)
    zt = np.zeros((HY_EMB, n + 1), f32)
    zt[:, :n] = z.T
    return zt


def const_pack():
    pk = Pack()
    for n, tag in ((SEQ, "L"), (CTX, "C")):
        nb = n // 128
        l = np.arange(n, dtype=np.float64)
        t = np.linspace(0.0, 1.0, n, dtype=np.float32).astype(np.float64)
        tf = np.zeros(n + 1); tf[:n] = t; tf[n] = 0.0
        pk.add("negt_f" + tag, fm(-tf[0:n]))
        pk.add("negt_b" + tag, fm(-tf[1:n + 1]))
        th2 = np.pi * (l + 0.5) / (2 * n)
        pk.add("chalf" + tag, fm(np.cos(th2)))
        pk.add("shalf" + tag, fm(np.sin(th2)))
    return pk


def const_mats():
    f32 = np.float32
    m = {}
    m["ident"] = np.eye(128, dtype=f32)
    m["ones"] = np.ones((128, 128), f32)
    k = np.arange(128)
    m["tri_incl"] = (k[:, None] <= k[None, :]).astype(f32)
    m["tri_excl"] = (k[:, None] < k[None, :]).astype(f32)
    s32 = np.arange(128) % 32
    t32 = np.arange(32)
    m["hgmask_f"] = (t32[None, :] >= s32[:, None]).astype(np.int32)
    m["hgmask_b"] = (t32[None, :] <= s32[:, None]).astype(np.int32)
    s64 = np.arange(64)
    m["hgmask64_f"] = (s64[None, :] >= s64[:, None]).astype(np.int32)
    m["hgmask64_b"] = (s64[None, :] <= s64[:, None]).astype(np.int32)
    m["m2mask_f"] = (k[None, :] >= k[:, None]).astype(f32)
    m["m2mask_b"] = (k[None, :] <= k[:, None]).astype(f32)
    sel = np.zeros((2, 2, 128), f32)
    sel[0, 0, :] = 1.0
    sel[1, 1, :] = 1.0
    m["sel"] = sel
    return m


def host_prep(inp, b, h):
    g = lambda k: np.asarray(inp[k], np.float32)
    d = {}
    d["x"] = np.ascontiguousarray(g("x")[b])
    d["ctx"] = np.ascontiguousarray(g("ctx")[b])
    d["w_mod"] = g("w_mod")
    d["w_out"] = g("w_out")
    cols = core_cols(h)
    d["w_in"] = np.ascontiguousarray(g("w_in")[:, :, cols])
    sl = slice(h * 256, (h + 1) * 256)
    pk = Pack()
    cv = np.stack([fm(g("c")[b]), fm(g("c_ctx"))], axis=-1)
    pk.add("cvec", cv)
    for l in range(DEPTH):
        L = str(l)
        pk.add("b_mod" + L, fm(g("b_mod")[l]))
        pk.add("norm_w" + L, fm(g("norm_w")[l]))
        cw = g("hy_conv_w")[l]
        cb = g("hy_conv_b")[l]
        hyc = np.concatenate([np.arange(o + h * 256, o + h * 256 + 256) for o in (0, 512, 1024)])
        pk.add("hy_cw" + L, np.stack([fm(cw[j, hyc]) for j in range(3)], axis=-1))
        pk.add("hy_cb" + L, fm(cb[hyc]))
        pk.add("hy_b1" + L, fm64(g("hy_b1")[l]))
        pk.add("hy_b2" + L, fm64(g("hy_b2")[l]))
        pk.add("hy_fr" + L, fm64(g("hy_freq")[l]))
        pk.add("hy_skip" + L, np.stack([fm(g("hy_skip")[l, o, sl]) for o in range(2)], axis=1))
        for dr in range(2):
            DD = L + str(dr)
            pk.add("rg_cw" + DD, np.stack([fm(g("rg_conv_w")[l, dr, j, sl]) for j in range(4)], axis=-1))
            pk.add("rg_cb" + DD, fm(g("rg_conv_b")[l, dr, sl]))
            pk.add("rg_ba" + DD, fm(g("rg_ba")[l, dr, sl]))
            pk.add("rg_bx" + DD, fm(g("rg_bx")[l, dr, sl]))
            pk.add("rg_lam" + DD, fm(g("rg_lam")[l, dr, sl]))
            m2c = np.concatenate([np.arange(h * 256, h * 256 + 256), np.arange(512 + h * 128, 512 + h * 128 + 128),
                                  np.arange(768 + h * 128, 768 + h * 128 + 128)])
            pk.add("m2_cw" + DD, np.stack([fm(g("m2_conv_w")[l, dr, j, m2c]) for j in range(4)], axis=-1))
            pk.add("m2_cb" + DD, fm(g("m2_conv_b")[l, dr, m2c]))
            for ll in range(DEPTH):
                pk.add("hg_lb%d_%s" % (ll, DD), fm(g("hg_lb")[ll, dr, sl]))
        pk.add("hg_nw" + L, fm(g("hg_norm_w")[l, sl]))
        pk.add("m2_nw" + L, fm(g("m2_norm_w")[l, sl]))
    cp = const_pack()
    for name in cp.off:
        pass
    base = pk.n
    pk.cols += cp.cols
    for name, o in cp.off.items():
        pk.off[name] = base + o
    pk.n += cp.n
    d["pcol"] = pk.build()
    pr = Pack()
    for l in range(DEPTH):
        L = str(l)
        pr.add("bmod_g" + L, rowb(g("b_mod")[l, 2048:3072]))
        for dr in range(2):
            DD = L + str(dr)
            hs = slice(h * 4, h * 4 + 4)
            pr.add("m2_dtb" + DD, rowb(g("m2_dt_bias")[l, dr, hs]))
            pr.add("m2_alog" + DD, rowb(g("m2_a_log")[l, dr, hs]))
            pr.add("m2_d" + DD, rowb(np.repeat(g("m2_d")[l, dr, hs], 64)))
    pr.add("final_w", rowb(g("final_norm_w")))
    deltas = np.abs(np.linspace(math.log(1e-2) / 1.5, math.log(1e-2) / 0.3, W_BR, dtype=np.float32))
    pr.add("hy_delta", rowb(deltas[sl]))
    d["prow"] = pr.build()
    bd = np.zeros((DEPTH, 2, 2, 2, 128, 128), np.float32)
    for l in range(DEPTH):
        for dr in range(2):
            for ai, nm in enumerate(("rg_wa", "rg_wx")):
                wgt = g(nm)[l, dr]
                for cc in range(2):
                    for hh in range(2):
                        head = h * 4 + cc * 2 + hh
                        bd[l, dr, ai, cc, hh * 64:(hh + 1) * 64, hh * 64:(hh + 1) * 64] = wgt[head]
    d["rg_bd"] = bd
    d["hy_w1"] = g("hy_w1")
    d["hy_w2"] = g("hy_w2")
    w3 = g("hy_w3").reshape(DEPTH, HY_HID, 2, 2, W_BR)[:, :, :, :, sl]
    d["hy_w3"] = np.ascontiguousarray(w3.transpose(0, 1, 3, 2, 4)).reshape(DEPTH, HY_HID, 2, 512)
    return d, pk.off, pr.off


PADW = 3
RB_CTX = PADW
RB_LAT = PADW + CTX + PADW
RB_W = RB_LAT + SEQ + PADW
TOKCH = [(0, CTX)] + [(CTX + 512 * i, 512) for i in range(8)]


def rb_off(t0):
    return RB_CTX + t0 if t0 < CTX else RB_LAT + (t0 - CTX)


class Kern:
    def __init__(self, mode, poff, roff, dbg=()):
        self.mode = mode
        self.poff = poff
        self.roff = roff
        self.dbg = set(dbg)
        nc = self.nc = bass.Bass("TRN2", target_bir_lowering=False)
        self.P = Prog(nc)
        self.din = {}
        self.scr = {}
        self.gst = contextlib.ExitStack()
        self.ps_i = 0
        self.uid = 0

    def inp(self, name, shape, dt=F32):
        ap = self.nc.dram_tensor(name, list(shape), dt, kind="ExternalInput").ap()
        self.din[name] = ap
        return ap

    def out(self, name, shape, dt=F32):
        ap = self.nc.dram_tensor(name, list(shape), dt, kind="ExternalOutput").ap()
        self.din[name] = ap
        return ap

    def scratch(self, name, shape, dt=F32):
        kind = "ExternalOutput" if name in self.dbg else "Internal"
        ap = self.nc.dram_tensor(name, list(shape), dt, kind=kind).ap()
        self.scr[name] = ap
        return ap

    def sb(self, st, name, shape, dt=F32):
        self.uid += 1
        return st.enter_context(self.nc.sbuf_tensor(f"{name}_{self.uid}", list(shape), dt))

    def psum(self):
        rot = getattr(self, "ps_rot", None) or list(range(8))
        self.ps_i = (self.ps_i + 1) % len(rot)
        i = rot[self.ps_i]
        return self.ps[i], f"ps{i}"

    def pc(self, name, j=0, n=1):
        o = self.poff[name] + j
        return self.pcol[:, o:o + n]

    def mm(self, out, lhsT, rhs, start, stop, r, w):
        self.P.op("pe", lambda e: e.matmul(out, lhsT=lhsT, rhs=rhs, start=start, stop=stop), r, w)

    def tr(self, out, in_, r, w):
        idn = self.ident
        self.P.op("pe", lambda e: e.transpose(out, in_, idn[:]), list(r) + ["const"], w)

    def act(self, out, in_, func, r, w, **kw):
        self.P.op("act", lambda e: e.activation(out=out, in_=in_, func=func, **kw), r, w)

    def ts(self, eng, out, in0, s1, op0, r, w, s2=None, op1=None):
        if op1 is None:
            self.P.op(eng, lambda e: e.tensor_scalar(out=out, in0=in0, scalar1=s1, scalar2=None, op0=op0), r, w)
        else:
            self.P.op(eng, lambda e: e.tensor_scalar(out=out, in0=in0, scalar1=s1, scalar2=s2, op0=op0, op1=op1), r, w)

    def tt(self, eng, out, in0, in1, op, r, w):
        self.P.op(eng, lambda e: e.tensor_tensor(out=out, in0=in0, in1=in1, op=op), r, w)

    def stt(self, out, in0, scalar, in1, op0, op1, r, w):
        self.P.op("dve", lambda e: e.scalar_tensor_tensor(out=out, in0=in0, scalar=scalar, in1=in1, op0=op0, op1=op1), r, w)

    def cp(self, eng, out, in_, r, w):
        if eng == "act":
            self.act(out, in_, AF.Copy, r, w)
        else:
            self.P.op(eng, lambda e: e.tensor_copy(out=out, in_=in_), r, w)

    def memset(self, eng, ap, val, w):
        self.P.op(eng, lambda e: e.memset(ap, val), (), w)

    def setup(self):
        nc, P = self.nc, self.P
        g = self.gst
        self.ps = [g.enter_context(nc.psum_tensor(f"psb{i}", [128, 512], F32)) for i in range(8)]
        npc = max(self.poff.values()) + 64
        self.npc = self.din["pcol"].shape[1]
        self.pcol = self.sb(g, "pcol", [128, self.npc])
        self.ident = self.sb(g, "ident", [128, 128])
        self.ones_f = self.sb(g, "ones_f", [128, 128])
        self.ones_b = self.sb(g, "ones_b", [128, 128], BF16)
        self.ident_b = self.sb(g, "ident_b", [128, 128], BF16)
        self.modT = self.sb(g, "modT", [128, DEPTH, 24, 2])
        self.Amod = self.sb(g, "Amod", [128, DEPTH, 8, 2])
        P.dma("sp", self.pcol[:], self.din["pcol"], w=["pcol"])
        P.dma("sp", self.ident[:], self.din["c_ident"], w=["const"])
        P.dma("sp", self.ones_f[:], self.din["c_ones"], w=["const"])
        self.epsc = self.sb(g, "epsc", [128, 2])
        self.memset("dve", self.epsc[:, 0:1], EPS, ["epsc"])
        self.memset("dve", self.epsc[:, 1:2], 1.0, ["epsc"])
        self.cp("dve", self.ones_b[:], self.ones_f[:], ["const"], ["const"])
        self.cp("dve", self.ident_b[:], self.ident[:], ["const"], ["const"])

    def phase_mod(self):
        P = self.P
        w_mod = self.din["w_mod"]
        with contextlib.ExitStack() as st:
            wst = [self.sb(st, "wm", [128, 8, 512]) for _ in range(2)]
            cond = self.sb(st, "cond", [128, 8, 2])
            grow = self.sb(st, "grow", [2, 1024])
            brow = self.sb(st, "brow", [2, 1024])
            sel = self.sb(st, "sel", [2, 2, 128])
            gb = [self.sb(st, "gb", [128, 512]) for _ in range(2)]
            P.dma("sp", sel[:], self.din["c_sel"].rearrange("s k m -> k s m"), w=["sel"])
            self.act(cond[:].rearrange("p k s -> p (k s)"), self.pc("cvec", 0, 16), AF.Silu, ["pcol"], ["cond"])
            it = 0
            for l in range(DEPTH):
                o = self.roff["bmod_g%d" % l]
                P.dma("sp", brow[:], self.din["prow"][0:2, o:o + 1024], w=["brow"])
                for jb in range(6):
                    buf = wst[it % 2]; bk = "wm%d" % (it % 2); it += 1
                    P.dma("sp", buf[:], w_mod[l, :, jb * 512:(jb + 1) * 512].rearrange("(k p) c -> p k c", p=128), w=[bk])
                    ps, pk = self.psum()
                    for jj in range(4):
                        for k in range(8):
                            self.mm(ps[:, jj * 2:jj * 2 + 2], buf[:, k, jj * 128:(jj + 1) * 128], cond[:, k, :],
                                    k == 0, k == 7, [bk, "cond"], [pk])
                    bm = self.pc("b_mod%d" % l, jb * 4, 4)
                    self.tt("dve", self.modT[:, l, jb * 4:(jb + 1) * 4, :], ps[:, 0:8].rearrange("p (j s) -> p j s", s=2),
                            bm.unsqueeze(2).to_broadcast([128, 4, 2]), ALU.add, [pk, "pcol"], ["modT"])
                    if jb >= 4:
                        ps2, pk2 = self.psum()
                        for k in range(8):
                            self.mm(ps2[0:2, :], cond[:, k, :], buf[:, k, :], k == 0, k == 7, [bk, "cond"], [pk2])
                        c0 = (jb - 4) * 512
                        self.tt("dve", grow[:, c0:c0 + 512], ps2[0:2, :], brow[:, c0:c0 + 512], ALU.add,
                                [pk2, "brow"], ["grow"])
                nw = self.pc("norm_w%d" % l, 0, 8)
                self.ts("dve", self.Amod[:, l, :, :], self.modT[:, l, 8:16, :], 1.0, ALU.add, ["modT"], ["Amod"])
                self.tt("dve", self.Amod[:, l, :, :], self.Amod[:, l, :, :], nw.unsqueeze(2).to_broadcast([128, 8, 2]),
                        ALU.mult, ["Amod", "pcol"], ["Amod"])
                for s in range(2):
                    for hf in range(2):
                        ps3, pk3 = self.psum()
                        self.mm(ps3[:, :], sel[:, s, :], grow[:, hf * 512:(hf + 1) * 512], True, True, ["sel", "grow"], [pk3])
                        gt = gb[(s * 2 + hf) % 2]; gk = "gb%d" % ((s * 2 + hf) % 2)
                        self.cp("act", gt[:], ps3[:, :], [pk3], [gk])
                        P.dma(STQ, self.scr["gbc"][l, s, :, hf * 512:(hf + 1) * 512], gt[:], r=[gk], w=["gbc"])
            P.flush()

    def phase_norm(self, l, st, xlat, xctx):
        P = self.P
        hlT = self.hlT = self.sb(st, "hlT", [128, 8, NT], BF16)
        with contextlib.ExitStack() as s2:
            xt = [self.sb(s2, "xt", [128, 1024]) for _ in range(6)]
            junks = [self.sb(s2, "junk", [128, 1024]) for _ in range(2)]
            ssq = self.sb(s2, "ssq", [128, 4])
            rstd = self.sb(s2, "rstd", [128, 4])
            groups = [[0, 1]] + [list(range(2 + 4 * i, 6 + 4 * i)) for i in range(8)]
            xi = 0
            for gi, grp in enumerate(groups):
                s = 1 if gi == 0 else 0
                ng = len(grp)
                tl = []
                for i, ti in enumerate(grp):
                    t = xt[xi % 6]; tk = "xt%d" % (xi % 6); xi += 1
                    src = xctx[ti * 128:(ti + 1) * 128, :] if ti < 2 else xlat[(ti - 2) * 128:(ti - 1) * 128, :]
                    P.dma("sp", t[:], src, w=[tk])
                    jn = junks[xi % 2]
                    self.P.op("act", lambda e, t=t, i=i, jn=jn: e.activation(out=jn[:], in_=t[:], func=AF.Square,
                                                                           accum_out=ssq[:, i:i + 1]), [tk], ["junk%d" % (xi % 2), ("ssq", i)])
                    tl.append((t, tk))
                self.act(rstd[:, :ng], ssq[:, :ng], AF.Ln, [("ssq", i) for i in range(ng)], ["rstd"], scale=1.0 / D, bias=self.epsc[:, 0:1])
                self.act(rstd[:, :ng], rstd[:, :ng], AF.Exp, ["rstd"], ["rstd"], scale=-0.5)
                for i, (t, tk) in enumerate(tl):
                    if i % 2 == 0:
                        self.ts("dve", t[:], t[:], rstd[:, i:i + 1], ALU.mult, [tk, "rstd"], [tk])
                    else:
                        self.act(t[:], t[:], AF.Copy, [tk, "rstd"], [tk], scale=rstd[:, i:i + 1])
                tok0 = grp[0] * 128
                for k in range(8):
                    ps, pk = self.psum()
                    for i, (t, tk) in enumerate(tl):
                        self.tr(ps[:, i * 128:(i + 1) * 128], t[:, k * 128:(k + 1) * 128], [tk], [pk])
                    hk = ("hlT", tok0 // 512 if tok0 >= CTX else -1)
                    a_col = self.Amod[:, l, k, s:s + 1]
                    sh_col = self.modT[:, l, k, s:s + 1]
                    if k % 2 == 0:
                        self.ts("dve", hlT[:, k, tok0:tok0 + ng * 128], ps[:, :ng * 128], a_col, ALU.mult,
                                [pk, "Amod", "modT"], [hk], s2=sh_col, op1=ALU.add)
                    else:
                        self.act(hlT[:, k, tok0:tok0 + ng * 128], ps[:, :ng * 128], AF.Identity,
                                 [pk, "Amod", "modT"], [hk], scale=a_col, bias=sh_col)
            P.flush()

    def hl_key(self, t0):
        return ("hlT", (t0 - CTX) // 512 if t0 >= CTX else -1)

    def conv(self, obuf, ok, rbuf, rk, wname, wj, ntap, bname, bj, offs):
        lo, hi = PADW, RB_W - PADW
        o = obuf[:, lo:hi]
        self.ts("dve", o, rbuf[:, lo + offs[0]:hi + offs[0]], self.pc(wname, wj, 1), ALU.mult,
                [rk, "pcol"], [ok], s2=self.pc(bname, bj, 1), op1=ALU.add)
        for j in range(1, ntap):
            self.stt(o, rbuf[:, lo + offs[j]:hi + offs[j]], self.pc(wname, wj + j, 1), o, ALU.mult, ALU.add,
                     [rk, ok, "pcol"], [ok])

    def rb_store(self, q, dst_row_ap, buf, bk, wkey):
        self.P.dma(q, dst_row_ap[:, 0:CTX], buf[:, RB_CTX:RB_CTX + CTX], r=[bk], w=[wkey])
        self.P.dma(q, dst_row_ap[:, CTX:NT], buf[:, RB_LAT:RB_LAT + SEQ], r=[bk], w=[wkey])

    def phase_inproj(self, l):
        P = self.P
        hlT = self.hlT
        w_in = self.din["w_in"]
        S = self.scr
        L = str(l)
        with contextlib.ExitStack() as st:
            wst = [self.sb(st, "wst", [128, 8, 256]) for _ in range(2)]
            wbf = [self.sb(st, "wbf", [128, 8, 256], BF16) for _ in range(2)]
            rbs = [self.sb(st, "rb", [128, RB_W]) for _ in range(3)]
            obs = [self.sb(st, "ob", [128, RB_W]) for _ in range(3)]
            tmv = [self.sb(st, "tmv", [128, 256]) for _ in range(2)]
            dtt = self.sb(st, "dtt", [128, NTILE * 4])
            for i in range(3):
                self.memset("pool", rbs[i][:], 0.0, ["rb%d" % i])
            cnt = {"w": 0, "rb": 0, "ob": 0, "ev": 0, "tm": 0}

            def load_w(c0, n):
                i = cnt["w"] % 2; cnt["w"] += 1
                P.dma("sp", wst[i][:, :, :n], w_in[l, :, c0:c0 + n].rearrange("(k p) c -> p k c", p=128), w=["wst%d" % i])
                self.cp("act", wbf[i][:, :, :n], wst[i][:, :, :n], ["wst%d" % i], ["wbf%d" % i])
                return wbf[i], "wbf%d" % i

            def rhs_nat(k, t0, n):
                return hlT[:, k, t0:t0 + n], [self.hl_key(t0)]

            def rhs_cm(k, t0, n):
                if t0 < CTX:
                    return rhs_nat(k, t0, n)
                i = (t0 - CTX) // 512
                v = hlT[:, k, CTX:NT].rearrange("p (r w) -> p w r", w=64)[:, 8 * i:8 * i + 8, :]
                return v, [("hlT", j) for j in range(8)]

            def fm_block(wb, wk, cb, rhsf, evac):
                for (t0, n) in TOKCH:
                    ps, pk = self.psum()
                    if rhsf is rhs_cm and t0 >= CTX:
                        i = (t0 - CTX) // 512
                        hk = [("hlT", j) for j in range(8)]
                        for wl in range(8):
                            w = 8 * i + wl
                            for k in range(8):
                                self.mm(ps[:, wl * 64:(wl + 1) * 64], wb[:, k, cb * 128:(cb + 1) * 128],
                                        hlT[:, k, CTX + w:NT:64], k == 0, k == 7, [wk] + hk, [pk])
                    else:
                        for k in range(8):
                            rhs, rk = rhs_nat(k, t0, n)
                            self.mm(ps[:, :n], wb[:, k, cb * 128:(cb + 1) * 128], rhs, k == 0, k == 7, [wk] + rk, [pk])
                    evac(ps, pk, t0, n)

            def evac_to(buf, bk, func, scale=None):
                def f(ps, pk, t0, n):
                    o = buf[:, rb_off(t0):rb_off(t0) + n]
                    cnt["ev"] += 1
                    if func is None and scale is None and cnt["ev"] % 2 == 0:
                        self.cp("dve", o, ps[:, :n], [pk], [bk])
                    else:
                        kw = {} if scale is None else {"scale": scale}
                        self.act(o, ps[:, :n], AF.Copy if func is None else func, [pk], [bk], **kw)
                return f

            def next_rb():
                i = cnt["rb"] % 3; cnt["rb"] += 1
                return rbs[i], "rb%d" % i

            def next_ob():
                i = cnt["ob"] % 3; cnt["ob"] += 1
                return obs[i], "ob%d" % i

            def do_hy():
                for j in range(3):
                    wb, wk = load_w(C_HYV + j * 256, 256)
                    for cb in range(2):
                        rb, rk = next_rb()
                        fm_block(wb, wk, cb, rhs_nat, evac_to(rb, rk, None))
                        ob, ok = next_ob()
                        self.conv(ob, ok, rb, rk, "hy_cw" + L, (j * 2 + cb) * 3, 3, "hy_cb" + L, j * 2 + cb, (-1, 0, 1))
                        self.rb_store(STQ, S["hy_u"][j, cb * 128:(cb + 1) * 128, :], ob, ok, "hy_u")

            def do_gate(c0, nm):
                wb, wk = load_w(c0, 256)
                for cb in range(2):
                    rb, rk = next_rb()
                    fm_block(wb, wk, cb, rhs_nat, evac_to(rb, rk, AF.Silu))
                    self.rb_store(STQ, S[nm][cb * 128:(cb + 1) * 128, :], rb, rk, nm)

            def do_rg():
                wb, wk = load_w(C_RGX, 256)
                for cb in range(2):
                    rb, rk = next_rb()
                    fm_block(wb, wk, cb, rhs_nat, evac_to(rb, rk, None))
                    for dr in range(2):
                        ob, ok = next_ob()
                        offs = (-3, -2, -1, 0) if dr == 0 else (3, 2, 1, 0)
                        self.conv(ob, ok, rb, rk, "rg_cw" + L + str(dr), cb * 4, 4, "rg_cb" + L + str(dr), cb, offs)
                        self.rb_store(STQ, S["rg_x"][dr, cb * 128:(cb + 1) * 128, :], ob, ok, "rg_x")

            def do_plain(c0, dst, scale):
                wb, wk = load_w(c0, 256)
                for cb in range(2):
                    rb, rk = next_rb()
                    fm_block(wb, wk, cb, rhs_nat, evac_to(rb, rk, None, scale))
                    self.rb_store(STQ, dst[cb * 128:(cb + 1) * 128, :], rb, rk, "hg_qf")

            def do_m2(c0, n, ch0):
                wb, wk = load_w(c0, n)
                for cb in range(2):
                    rb, rk = next_rb()
                    fm_block(wb, wk, cb, rhs_cm, evac_to(rb, rk, None))
                    for dr in range(2):
                        ob, ok = next_ob()
                        offs = (-3, -2, -1, 0) if dr == 0 else (3, 2, 1, 0)
                        ch = ch0 + cb
                        self.conv(ob, ok, rb, rk, "m2_cw" + L + str(dr), ch * 4, 4, "m2_cb" + L + str(dr), ch, offs)
                        self.act(ob[:, PADW:RB_W - PADW], ob[:, PADW:RB_W - PADW], AF.Silu, [ok], [ok])
                        self.rb_store(STQ, S["m2_x"][dr, ch * 128:(ch + 1) * 128, :], ob, ok, "m2_x")

            do_m2(C_M2XS, 256, 0)
            do_gate(C_HYG, "hy_g")
            do_gate(C_RGG, "rg_g")
            do_rg()
            do_gate(C_HGG, "hg_g")
            do_gate(C_M2G, "m2_g")
            do_m2(C_M2B, 256, 2)
            do_plain(C_HGQ, S["hg_q"], 128.0 ** -0.5)
            do_plain(C_HGFF, S["hg_f"][0], None)
            do_hy()
            do_plain(C_HGFB, S["hg_f"][1], None)
            wb, wk = load_w(C_HGI, 256)
            for ti in range(NTILE):
                ps, pk = self.psum()
                for k in range(8):
                    self.mm(ps[:, 0:256], hlT[:, k, ti * 128:(ti + 1) * 128], wb[:, k, :], k == 0, k == 7,
                            [wk, self.hl_key(ti * 128)], [pk])
                i = cnt["tm"] % 2; cnt["tm"] += 1
                self.cp("dve" if ti % 2 else "act", tmv[i][:], ps[:, 0:256], [pk], ["tmv%d" % i])
                P.dma(STQ, S["hg_v"][ti * 128:(ti + 1) * 128, :], tmv[i][:], r=["tmv%d" % i], w=["hg_v"])
            wb, wk = load_w(C_M2DT, 4)
            ps, pk = self.psum()
            for ti in range(NTILE):
                if ti < 2:
                    for k in range(8):
                        self.mm(ps[:, ti * 4:ti * 4 + 4], hlT[:, k, ti * 128:(ti + 1) * 128], wb[:, k, 0:4], k == 0, k == 7,
                                [wk, self.hl_key(ti * 128)], [pk])
                else:
                    hk = [("hlT", q) for q in range(8)]
                    for wl in range(2):
                        w = 2 * (ti - 2) + wl
                        for k in range(8):
                            self.mm(ps[wl * 64:(wl + 1) * 64, ti * 4:ti * 4 + 4], hlT[:, k, CTX + w:NT:64], wb[:, k, 0:4],
                                    k == 0, k == 7, [wk] + hk, [pk])
            self.cp("dve", dtt[:], ps[:, 0:NTILE * 4], [pk], ["dtt"])
            P.dma(STQ, S["m2_dt"], dtt[:], r=["dtt"], w=["m2_dt"])
            P.flush()

    def declare(self, npc, npr):
        mode = self.mode
        mix = mode in ("A", "B", "test", "ALL")
        self.inp("w_mod", [DEPTH, D, 3 * D]); self.inp("w_out", [DEPTH, 2 * D, D])
        self.inp("pcol", [128, npc]); self.inp("prow", [128, npr])
        self.inp("c_ident", [128, 128]); self.inp("c_ones", [128, 128]); self.inp("c_sel", [2, 2, 128])
        if mode != "C":
            self.inp("x", [SEQ, D]); self.inp("ctx", [CTX, D])
        if mix:
            self.inp("w_in", [DEPTH, D, NCOL])
            self.inp("rg_bd", [DEPTH, 2, 2, 2, 128, 128])
            self.inp("hy_w1", [DEPTH, HY_EMB, HY_HID]); self.inp("hy_w2", [DEPTH, HY_HID, HY_HID])
            self.inp("hy_w3", [DEPTH, HY_HID, 2, 512])
            self.inp("c_tri_incl", [128, 128]); self.inp("c_tri_excl", [128, 128])
            self.inp("c_hgmask_f", [128, 32], I32); self.inp("c_hgmask_b", [128, 32], I32)
            self.inp("c_hgmask64_f", [64, 64], I32); self.inp("c_hgmask64_b", [64, 64], I32)
            self.inp("c_m2mask_f", [128, 128]); self.inp("c_m2mask_b", [128, 128])
            self.inp("tabC_L", [8, 8, 128, 4, 512], BF16); self.inp("tabS_L", [8, 8, 128, 4, 512], BF16)
            self.inp("tabC_C", [1, 1, 128, 2, 256], BF16); self.inp("tabS_C", [1, 1, 128, 2, 256], BF16)
            self.inp("z_L", [HY_EMB, SEQ + 1]); self.inp("z_C", [HY_EMB, CTX + 1])
        sc = self.scratch
        sc("gbc", [DEPTH, 2, 128, D])
        if mix:
            sc("hy_u", [3, CH, NT]); sc("hy_g", [CH, NT]); sc("rg_g", [CH, NT]); sc("hg_g", [CH, NT]); sc("m2_g", [CH, NT])
            sc("rg_x", [2, CH, NT]); sc("hg_q", [CH, NT]); sc("hg_f", [2, CH, NT]); sc("hg_v", [NT, CH])
            sc("m2_x", [2, 512, NT]); sc("m2_dt", [128, NTILE * 4])
            sc("khat_L", [SEQ // 128, 2, 128, 512]); sc("khat_C", [CTX // 128, 2, 128, 512])
            if mode in ("A", "B"):
                self.scr["ybuf"] = self.out("y_out", [4 * CH, NT], BF16)
            else:
                sc("ybuf", [4 * CH, NT], BF16)
        if mode == "ALL":
            sc("yf", [2 * D, NT], BF16)
            sc("xres", [NT, D])
            self.out("out", [SEQ, D])
        if mode == "B":
            self.inp("yf", [2 * D, NT], BF16)
            self.scr["xres"] = self.out("xres_out", [NT, D])
        if mode == "C":
            self.inp("yf", [2 * D, SEQ // 2], BF16)
            self.inp("xres", [SEQ // 2, D])
            self.out("out", [SEQ // 2, D])

    def mixers(self, l, xlat, xctx, need_ctx, early=None):
        with contextlib.ExitStack() as st:
            self.phase_norm(l, st, xlat, xctx)
            self.phase_inproj(l)
        self.phase_rg(l)
        self.phase_hg(l)
        self.phase_m2(l)
        if early is not None:
            early()
        self.phase_hy(l, need_ctx)

    def finish(self):
        self.gst.close()
        self.P.close()
        return self.nc


def build_program(mode, poff, roff, npc, npr):
    kn = Kern(mode, poff, roff)
    kn.declare(npc, npr)
    kn.setup()
    kn.phase_mod()
    din = kn.din
    if mode == "A":
        kn.mixers(0, din["x"], din["ctx"], True)
    elif mode == "B":
        xres = kn.scr["xres"]
        kn.phase_out(0, din["yf"], lambda ti: (din["ctx"][ti * 128:(ti + 1) * 128, :] if ti < 2 else
                                                din["x"][(ti - 2) * 128:(ti - 1) * 128, :]),
                     list(range(NTILE)), xdst=lambda ti: [xres[ti * 128:(ti + 1) * 128, :]])
        kn.mixers(1, xres[CTX:NT, :], xres[0:CTX, :], False)
    elif mode == "ALL":
        xres = kn.scr["xres"]
        yf = kn.scr["yf"]
        ybuf = kn.scr["ybuf"]
        groups = [[0, 1], [2, 3], [4, 5], [6, 7]]

        def gather(js):
            for j in js:
                kn.P.collective(lambda e, j=j: e.collective_compute(
                    "AllGather", ALU.bypass, replica_groups=groups,
                    ins=[ybuf[j * 128:(j + 1) * 128, :].opt()], outs=[yf[j * 256:(j + 1) * 256, :].opt()]),
                    r=["ybuf"], w=[("yf", j)])
            if 0 in js:
                kn.P.flush()
        kn.mixers(0, din["x"], din["ctx"], True)
        gather(range(0, 8))
        kn.phase_out(0, yf, lambda ti: (din["ctx"][ti * 128:(ti + 1) * 128, :] if ti < 2 else
                                        din["x"][(ti - 2) * 128:(ti - 1) * 128, :]),
                     list(range(NTILE)), xdst=lambda ti: [xres[ti * 128:(ti + 1) * 128, :]])
        kn.mixers(1, xres[CTX:NT, :], xres[0:CTX, :], False)
        gather(range(0, 8))
        kn.phase_out(1, yf, lambda ti: xres[ti * 128:(ti + 1) * 128, :], list(range(2, NTILE)),
                     final_dst=lambda ti: din["out"][(ti - 2) * 128:(ti - 1) * 128, :])
    elif mode == "C":
        kn.phase_out(1, din["yf"], lambda ti: din["xres"][ti * 128:(ti + 1) * 128, :], list(range(SEQ // 256)),
                     final_dst=lambda ti: din["out"][ti * 128:(ti + 1) * 128, :], lat_only=True)
    return kn.finish()


def const_inputs():
    key = "cin"
    if key not in _CONST_CACHE:
        m = const_mats()
        d = {"c_" + k: v for k, v in m.items()}
        d["tabC_L"], d["tabS_L"] = dft_tables(SEQ)
        d["tabC_C"], d["tabS_C"] = dft_tables(CTX)
        d["z_L"] = hy_zfeat(SEQ)
        d["z_C"] = hy_zfeat(CTX)
        _CONST_CACHE[key] = d
    return _CONST_CACHE[key]


def _phase_rg(self, l):
    P = self.P
    S = self.scr
    L = str(l)
    CHK = [(i * 512, min(512, NT - i * 512)) for i in range(9)]
    with contextlib.ExitStack() as st:
        wtmps = [self.sb(st, "rgw", [128, 128]) for _ in range(2)]
        wbds = [self.sb(st, "rgwb", [128, 4, 128], BF16) for _ in range(2)]
        xcs = [self.sb(st, "rgxc", [128, NT]) for _ in range(2)]
        xcbs = [self.sb(st, "rgxcb", [128, NT], BF16) for _ in range(2)]
        avs = [self.sb(st, "rga", [128, NT]) for _ in range(2)]
        bvs = [self.sb(st, "rgb", [128, NT]) for _ in range(2)]
        gis = [self.sb(st, "rggi", [128, NT]) for _ in range(2)]
        prms = [self.sb(st, "rgprm", [128, 2]) for _ in range(2)]
        hs = self.sb(st, "rghs", [128, NT])
        hb = self.sb(st, "rghb", [128, NT])
        yb = self.sb(st, "rgy", [128, NT], BF16)
        for cc in range(2):
            for dr in range(2):
                DD = L + str(dr)
                bi = (cc * 2 + dr) % 2
                wtmp, wbd, xc, xcb, av, bv, gi, prm = wtmps[bi], wbds[bi], xcs[bi], xcbs[bi], avs[bi], bvs[bi], gis[bi], prms[bi]
                sfx = str(bi)
                self.act(prm[:, 0:1], self.pc("rg_lam" + DD, cc, 1), AF.Exp, ["pcol"], ["rgprm" + sfx], scale=-1.0)
                self.act(prm[:, 0:1], prm[:, 0:1], AF.Ln, ["rgprm" + sfx], ["rgprm" + sfx], bias=self.epsc[:, 1:2])
                self.ts("dve", prm[:, 1:2], prm[:, 0:1], -8.0, ALU.mult, ["rgprm" + sfx], ["rgprm" + sfx])
                for ai in range(2):
                    P.dma("sp", wtmp[:], self.din["rg_bd"][l, dr, ai, cc], w=["rgw" + sfx])
                    self.cp("dve", wbd[:, ai, :], wtmp[:], ["rgw" + sfx], ["rgwb" + sfx])
                P.dma("sp", xc[:], S["rg_x"][dr, cc * 128:(cc + 1) * 128, :], w=["rgxc" + sfx])
                self.cp("dve", xcb[:], xc[:], ["rgxc" + sfx], ["rgxcb" + sfx])
                for (t0, n) in CHK:
                    pa, pak = self.psum()
                    px, pxk = self.psum()
                    self.mm(pa[:, :n], wbd[:, 0, :], xcb[:, t0:t0 + n], True, True, ["rgwb" + sfx, "rgxcb" + sfx], [pak])
                    self.mm(px[:, :n], wbd[:, 1, :], xcb[:, t0:t0 + n], True, True, ["rgwb" + sfx, "rgxcb" + sfx], [pxk])
                    self.act(av[:, t0:t0 + n], pa[:, :n], AF.Sigmoid, [pak, "pcol"], ["rga" + sfx], bias=self.pc("rg_ba" + DD, cc, 1))
                    self.act(gi[:, t0:t0 + n], px[:, :n], AF.Sigmoid, [pxk, "pcol"], ["rggi" + sfx], bias=self.pc("rg_bx" + DD, cc, 1))
                self.act(av[:], av[:], AF.Exp, ["rga" + sfx, "rgprm" + sfx], ["rga" + sfx], scale=prm[:, 1:2])
                self.tt("pool", bv[:], av[:], av[:], ALU.mult, ["rga" + sfx], ["rgb" + sfx])
                self.act(bv[:], bv[:], AF.Sqrt, ["rgb" + sfx], ["rgb" + sfx], scale=-1.0, bias=self.epsc[:, 1:2])
                self.tt("dve", gi[:], gi[:], xc[:], ALU.mult, ["rggi" + sfx, "rgxc" + sfx], ["rggi" + sfx])
                self.tt("dve", bv[:], bv[:], gi[:], ALU.mult, ["rgb" + sfx, "rggi" + sfx], ["rgb" + sfx])
                dst = hs if dr == 0 else hb
                dk = "rghs" if dr == 0 else "rghb"
                if dr == 0:
                    segs = [(slice(0, CTX), None), (slice(CTX, NT), (CTX - 1, CTX))]
                    rev = False
                else:
                    segs = [(slice(0, CTX), None), (slice(CTX, NT), (0, 1))]
                    rev = True
                for (sg, init) in segs:
                    a0, a1 = sg.start, sg.stop
                    if not rev:
                        o_ap, d0, d1 = dst[:, a0:a1], av[:, a0:a1], bv[:, a0:a1]
                    else:
                        def rv(t, a0=a0, a1=a1):
                            return t[:, a1 - 1:a0 - 1:-1] if a0 > 0 else t[:, a1 - 1::-1]
                        o_ap, d0, d1 = rv(dst), rv(av), rv(bv)
                    ini = 0.0 if init is None else dst[:, init[0]:init[1]]
                    self.P.op("dve", lambda e, o_ap=o_ap, d0=d0, d1=d1, ini=ini: e.tensor_tensor_scan(
                        out=o_ap, data0=d0, data1=d1, initial=ini, op0=ALU.mult, op1=ALU.add), ["rga" + sfx, "rgb" + sfx, dk], [dk])
            self.tt("pool", hs[:], hs[:], hb[:], ALU.add, ["rghs", "rghb"], ["rghs"])
            P.dma("sp", hb[:], S["rg_g"][cc * 128:(cc + 1) * 128, :], r=[], w=["rghb"])
            self.tt("dve", yb[:], hs[:], hb[:], ALU.mult, ["rghs", "rghb"], ["rgy"])
            P.dma(STQ, S["ybuf"][CH + cc * 128:CH + (cc + 1) * 128, :], yb[:], r=["rgy"], w=["ybuf"])
        P.flush()


Kern.phase_rg = _phase_rg


def _phase_hg(self, l):
    P = self.P
    S = self.scr
    L = str(l)
    NCH = NT // 64
    self.ps_rot = [0, 1, 2, 3, 4, 5]
    with contextlib.ExitStack() as st:
        vtok = self.sb(st, "hgvtok", [64, NCH, 256], BF16)
        q = self.sb(st, "hgq", [128, NT])
        zs = self.sb(st, "hgzs", [128, NT])
        Pb = self.sb(st, "hgP", [128, NT + 1])
        Db = self.sb(st, "hgD", [128, NT])
        kGf = self.sb(st, "hgkG", [128, NT])
        Tx = [self.sb(st, "hgTx", [128, NT]) for _ in range(1)]
        qg = self.sb(st, "hgqg", [128, NT], BF16)
        kg = self.sb(st, "hgkg", [128, NT], BF16)
        qG = self.sb(st, "hgqG", [128, NT], BF16)
        osum = self.sb(st, "hgos", [128, NT])
        dec = self.sb(st, "hgdec", [128, NCH])
        lbc = self.sb(st, "hglb", [128, 4])
        Sf = self.sb(st, "hgSf", [128, 128])
        Sbs = [self.sb(st, "hgSb", [128, 128], BF16) for _ in range(2)]
        MT = [self.sb(st, "hgMT", [128, 64], BF16) for _ in range(3)]
        kGt = [self.sb(st, "hgkGt", [128, 128], BF16) for _ in range(3)]
        msk = [self.sb(st, "hgmsk", [128, 32], I32) for _ in range(2)]
        XB = self.sb(st, "hgXB", [128, NT], BF16)
        P.dma("sp", msk[0][:], self.din["c_hgmask_f"], w=["hgmsk"])
        P.dma("sp", msk[1][:], self.din["c_hgmask_b"], w=["hgmsk"])
        msk64 = [self.sb(st, "hgmsk64", [64, 64], I32) for _ in range(2)]
        P.dma("sp", msk64[0][:], self.din["c_hgmask64_f"], w=["hgmsk"])
        P.dma("sp", msk64[1][:], self.din["c_hgmask64_b"], w=["hgmsk"])
        for i in range(3):
            self.memset("pool", MT[i][:], 0.0, ["hgMT%d" % i])
        vsrc = S["hg_v"].rearrange("(c p) d -> p c d", p=64)
        with contextlib.ExitStack() as st1:
            vsts = [self.sb(st1, "hgvst", [64, 4, 256]) for _ in range(2)]
            for qd in range(17):
                vst = vsts[qd % 2]
                P.dma("sp", vst[:], vsrc[:, qd * 4:(qd + 1) * 4, :], w=["hgvst%d" % (qd % 2)])
                self.cp("pool" if qd % 2 else "act", vtok[:, qd * 4:(qd + 1) * 4, :], vst[:], ["hgvst%d" % (qd % 2)], ["hgvtok"])
            P.flush()
        Pv3 = Pb[:, 0:NT].rearrange("p (c t) -> p c t", t=64)
        Pn3 = Pb[:, 1:NT + 1].rearrange("p (c t) -> p c t", t=64)
        D3 = Db[:].rearrange("p (c t) -> p c t", t=64)
        bc = lambda a: a.to_broadcast([128, NCH, 64])
        ref_b = bc(Pv3[:, :, 32:33])
        p0_b = bc(Pv3[:, :, 0:1])
        p1_b = bc(Pn3[:, :, 63:64])
        for cc in range(2):
            rows = slice(cc * 128, (cc + 1) * 128)
            P.dma("sp", q[:], S["hg_q"][rows, :], w=["hgq"])
            for dr in range(2):
                DD = L + str(dr)
                if l == 0:
                    self.memset("dve", lbc[:, 0:1], 0.0, ["hglb"])
                else:
                    self.tt("dve", lbc[:, 0:1], self.pc("hg_lb1_" + DD, cc, 1), self.pc("hg_lb0_" + DD, cc, 1), ALU.subtract,
                            ["pcol"], ["hglb"])
                    self.act(lbc[:, 0:1], lbc[:, 0:1], AF.Sigmoid, ["hglb"], ["hglb"])
                self.ts("dve", lbc[:, 1:2], lbc[:, 0:1], -1.0, ALU.mult, ["hglb"], ["hglb"], s2=1.0, op1=ALU.add)
                self.ts("dve", lbc[:, 2:3], lbc[:, 1:2], -1.0, ALU.mult, ["hglb"], ["hglb"])
                P.dma("sp", zs[:], S["hg_f"][dr, rows, :], w=["hgzs"])
                self.act(zs[:], zs[:], AF.Sigmoid, ["hgzs"], ["hgzs"])
                self.ts("dve", Db[:], zs[:], lbc[:, 1:2], ALU.mult, ["hgzs", "hglb"], ["hgD"], s2=lbc[:, 0:1], op1=ALU.add)
                self.act(Db[:], Db[:], AF.Ln, ["hgD"], ["hgD"])
                self.ts("pool", zs[:], zs[:], lbc[:, 2:3], ALU.mult, ["hgzs", "hglb"], ["hgzs"], s2=lbc[:, 1:2], op1=ALU.add)
                self.memset("dve", Pb[:, 0:1], 0.0, ["hgP"])
                ones_bc = self.ones_f[:, 0:1].to_broadcast([128, NT])
                self.P.op("dve", lambda e, ones_bc=ones_bc: e.tensor_tensor_scan(
                    out=Pb[:, 1:NT + 1], data0=ones_bc, data1=Db[:], initial=0.0, op0=ALU.mult, op1=ALU.add),
                    ["hgD", "const"], ["hgP"])
                arr3 = Pn3 if dr == 0 else Pv3
                sg = 1.0 if dr == 0 else -1.0
                T = [Db, kGf, Tx[0]]
                TK = ["hgD", "hgkG", "hgT2"]
                c64 = lambda a: a.rearrange("p (c t) -> p c t", t=64)
                c32 = lambda a: a.rearrange("p (c t) -> p c t", t=32)
                h4 = lambda a: a.rearrange("p (c h t) -> p c h t", h=2, t=32)
                XB4, q4, k4 = h4(XB[:]), h4(q[:]), h4(zs[:])
                hq, hk = (1, 0) if dr == 0 else (0, 1)
                self.tt("dve", c64(T[0][:]), arr3, ref_b, ALU.subtract, ["hgP"], [TK[0]])
                self.act(T[2][:], T[0][:], AF.Exp, [TK[0]], [TK[2]], scale=sg)
                self.act(T[1][:], T[0][:], AF.Exp, [TK[0]], [TK[1]], scale=-sg)
                self.tt("dve", XB4[:, :, hq, :], q4[:, :, hq, :], h4(T[2][:])[:, :, hq, :], ALU.mult, ["hgq", TK[2]], ["hgXB"])
                Ph3 = c32(Pb[:, 0:NT])
                a32 = c32(Pb[:, 1:NT + 1] if dr == 0 else Pb[:, 0:NT])
                self.tt("dve", c32(T[0][:]), a32, Ph3[:, :, 16:17].to_broadcast([128, NT // 32, 32]), ALU.subtract,
                        ["hgP"], [TK[0]])
                self.act(T[2][:], T[0][:], AF.Exp, [TK[0]], [TK[2]], scale=sg)
                self.tt("dve", XB4[:, :, hk, :], k4[:, :, hk, :], h4(T[1][:])[:, :, hk, :], ALU.mult, ["hgzs", TK[1]], ["hgXB"])
                self.act(T[1][:], T[0][:], AF.Exp, [TK[0]], [TK[1]], scale=-sg)
                self.tt("dve", c64(T[0][:]), arr3, p0_b if dr == 0 else p1_b, ALU.subtract, ["hgP"], [TK[0]])
                self.tt("dve", qg[:], q[:], T[2][:], ALU.mult, ["hgq", TK[2]], ["hgqg"])
                self.act(T[0][:], T[0][:], AF.Exp, [TK[0]], [TK[0]], scale=sg)
                self.tt("dve", c64(T[2][:]), arr3, p1_b if dr == 0 else p0_b, ALU.subtract, ["hgP"], [TK[2]])
                self.tt("dve", kg[:], zs[:], T[1][:], ALU.mult, ["hgzs", TK[1]], ["hgkg"])
                self.act(T[1][:], T[2][:], AF.Exp, [TK[2]], [TK[1]], scale=-sg)
                self.tt("dve", qG[:], q[:], T[0][:], ALU.mult, ["hgq", TK[0]], ["hgqG"])
                self.tt("dve", kGf[:], kGf[:], zs[:], ALU.mult, ["hgzs", TK[1]], [TK[1]])
                self.tt("dve", dec[:], Pb[:, 64:NT + 1:64], Pb[:, 0:NT:64], ALU.subtract, ["hgP"], ["hgdec"])
                self.act(dec[:], dec[:], AF.Exp, ["hgdec"], ["hgdec"])
                self.memset("dve", Sf[:], 0.0, ["hgSf"])
                for i in range(2):
                    self.memset("pool", Sbs[i][:], 0.0, ["hgSb%d" % i])
                for i in range(3):
                    self.memset("pool", MT[i][:], 0.0, ["hgMT%d" % i])
                order = list(range(NCH)) if dr == 0 else [3, 2, 1, 0] + list(range(NCH - 1, 3, -1))
                grp_of = lambda c: -1 if c < 4 else (c - 4) // 8
                bank_of = {}
                gi = 0
                for c in order:
                    g = grp_of(c)
                    if g not in bank_of:
                        bank_of[g] = 6 + gi % 2
                        gi += 1

                def front(ci, c):
                    t0 = c * 64
                    mt, mk = MT[ci % 3], "hgMT%d" % (ci % 3)
                    kt, kk = kGt[ci % 3], "hgkGt%d" % (ci % 3)
                    psT, ptk = self.psum()
                    self.tr(psT[0:64, 0:128], kGf[:, t0:t0 + 64], ["hgkG"], [ptk])
                    self.cp("act", kt[0:64, :], psT[0:64, 0:128], [ptk], [kk])
                    psA, pak = self.psum()
                    for hh in range(2):
                        a0 = t0 + 32 * hh
                        self.mm(psA[32 * hh:32 * hh + 32, 32 * hh:32 * hh + 32], kg[:, a0:a0 + 32], qg[:, a0:a0 + 32],
                                True, True, ["hgkg", "hgqg"], [pak])
                    if dr == 0:
                        ob = (0, 32); kB = XB[:, t0:t0 + 32]; qB = XB[:, t0 + 32:t0 + 64]
                    else:
                        ob = (32, 0); kB = XB[:, t0 + 32:t0 + 64]; qB = XB[:, t0:t0 + 32]
                    self.mm(psA[ob[0]:ob[0] + 32, ob[1]:ob[1] + 32], kB, qB, True, True, ["hgXB"], [pak])
                    self.P.op("dve", lambda e, mt=mt, psA=psA, m=msk64[dr]: e.copy_predicated(
                        out=mt[0:64, :], mask=m[:, :], data=psA[0:64, 0:64]), [pak, "hgmsk"], [mk])
                    return mt, mk, kt, kk

                def back(ci, c, fr):
                    mt, mk, kt, kk = fr
                    t0 = c * 64
                    g = grp_of(c)
                    bk = bank_of[g]
                    psO, pok = self.ps[bk], "ps%d" % bk
                    gstart = 0 if g < 0 else 4 + 8 * g
                    col = (c - gstart) * 64
                    vch = vtok[0:64, c, cc * 128:(cc + 1) * 128]
                    sb_prev, sbk_prev = Sbs[(ci + 1) % 2], "hgSb%d" % ((ci + 1) % 2)
                    sb_new, sbk_new = Sbs[ci % 2], "hgSb%d" % (ci % 2)
                    psS, psk = self.psum()
                    self.mm(psS[:, 0:128], kt[0:64, :], vch, True, True, [kk, "hgvtok"], [psk])
                    self.mm(psO[:, col:col + 64], sb_prev[:], qG[:, t0:t0 + 64], True, False, [sbk_prev, "hgqG"], [pok])
                    self.mm(psO[:, col:col + 64], vch, mt[0:64, :], False, True, ["hgvtok", mk], [pok])
                    self.stt(Sf[:], Sf[:], dec[:, c:c + 1], psS[:, 0:128], ALU.mult, ALU.add, ["hgSf", "hgdec", psk], ["hgSf"])
                    self.cp("act", sb_new[:], Sf[:], ["hgSf"], [sbk_new])
                    last_of_group = (ci + 1 == len(order)) or (grp_of(order[ci + 1]) != g)
                    if last_of_group:
                        n = 256 if g < 0 else 512
                        tg = gstart * 64
                        if dr == 0:
                            self.cp("act", osum[:, tg:tg + n], psO[:, 0:n], [pok], ["hgos"])
                        else:
                            self.tt("dve", osum[:, tg:tg + n], psO[:, 0:n], osum[:, tg:tg + n], ALU.add, [pok, "hgos"], ["hgos"])

                prev = None
                for ci, c in enumerate(order):
                    fr = front(ci, c)
                    if prev is not None:
                        back(*prev)
                    prev = (ci, c, fr)
                back(*prev)
            self.act(qg[:], osum[:], AF.Square, ["hgos"], ["hgqg"])
            for i in range(9):
                t0 = i * 512
                n = min(512, NT - t0)
                ps, pk = self.psum()
                self.mm(ps[:, :n], self.ones_b[:], qg[:, t0:t0 + n], True, True, ["const", "hgqg"], [pk])
                self.act(Db[:, t0:t0 + n], ps[:, :n], AF.Ln, [pk], ["hgD"], scale=1.0 / 128, bias=self.epsc[:, 0:1])
            self.act(Db[:], Db[:], AF.Exp, ["hgD"], ["hgD"], scale=-0.5)
            self.stt(osum[:], osum[:], self.pc("hg_nw" + L, cc, 1), Db[:], ALU.mult, ALU.mult, ["hgos", "hgD", "pcol"], ["hgos"])
            P.dma("sp", zs[:], S["hg_g"][rows, :], w=["hgzs"])
            self.tt("dve", kg[:], osum[:], zs[:], ALU.mult, ["hgos", "hgzs"], ["hgkg"])
            P.dma(STQ, S["ybuf"][2 * CH + cc * 128:2 * CH + (cc + 1) * 128, :], kg[:], r=["hgkg"], w=["ybuf"])
        P.flush()
    self.ps_rot = list(range(8))


Kern.phase_hg = _phase_hg


def _phase_m2(self, l):
    P = self.P
    S = self.scr
    L = str(l)
    NH = 4
    self.ps_rot = list(range(8))
    with contextlib.ExitStack() as st0:
        ytok = self.sb(st0, "m2ytok", [128, NTILE, 256])
        with contextlib.ExitStack() as st:
            stage = self.sb(st, "m2stage", [128, NT])
            BTb = self.sb(st, "m2BT", [128, NT], BF16)
            CTb = self.sb(st, "m2CT", [128, NT], BF16)
            Btok = self.sb(st, "m2Btok", [128, NTILE, 128], BF16)
            xstok = self.sb(st, "m2xstok", [128, NTILE, 256])
            Xtok = self.sb(st, "m2Xtok", [128, NTILE, 256], BF16)
            Xdtok = self.sb(st, "m2Xdtok", [128, NTILE, 256], BF16)
            dtraw = self.sb(st, "m2dtraw", [128, NTILE, NH])
            dt = self.sb(st, "m2dt", [128, NTILE, NH])
            dtA = self.sb(st, "m2dtA", [128, NTILE, NH])
            Qt = self.sb(st, "m2Q", [128, NTILE, NH])
            Qtot = self.sb(st, "m2Qtot", [128, NTILE, NH])
            Eoff = self.sb(st, "m2Eoff", [128, NTILE, NH])
            decs = self.sb(st, "m2decs", [128, NTILE, NH])
            cdec = self.sb(st, "m2cdec", [128, NTILE, NH])
            hp = self.sb(st, "m2hp", [128, 3, NH])
            Dt = self.sb(st, "m2Dt", [128, 256])
            tri = [self.sb(st, "m2tri", [128, 128]) for _ in range(2)]
            mask = [self.sb(st, "m2mask", [128, 128]) for _ in range(2)]
            GMs = [self.sb(st, "m2GM", [128, 128]) for _ in range(3)]
            Lhs = [self.sb(st, "m2Lh", [128, NH, 128]) for _ in range(3)]
            Exs = [self.sb(st, "m2Ex", [128, NH, 128]) for _ in range(3)]
            MT = [self.sb(st, "m2MT", [128, NH, 128], BF16) for _ in range(3)]
            tmps = [self.sb(st, "m2tmp", [128, 256]) for _ in range(3)]
            tmp = tmps[0]
            stf = self.sb(st, "m2stf", [128, 256])
            stbs = [self.sb(st, "m2stb", [128, 256], BF16) for _ in range(2)]
            P.dma("sp", tri[0][:], self.din["c_tri_incl"], w=["m2tri"])
            P.dma("sp", tri[1][:], self.din["c_tri_excl"], w=["m2tri"])
            P.dma("sp", mask[0][:], self.din["c_m2mask_f"], w=["m2mask"])
            P.dma("sp", mask[1][:], self.din["c_m2mask_b"], w=["m2mask"])
            negm = [self.sb(st, "m2neg", [128, NH, 128]) for _ in range(2)]
            negb = [self.sb(st, "m2negb", [128, NH, 128], BF16) for _ in range(2)]
            trib = [self.sb(st, "m2trib", [128, 128], BF16) for _ in range(2)]
            Rrs = [self.sb(st, "m2Rr", [128, 2 * NH, 128], BF16) for _ in range(3)]
            dtAhl = self.sb(st, "m2dtAhl", [128, NTILE, 2, NH], BF16)
            dtmp = self.sb(st, "m2dtmp", [128, NTILE, NH])
            nQ = self.sb(st, "m2nQ", [128, NTILE, NH])
            for d_ in range(2):
                self.ts("dve", negm[d_][:], mask[d_][:].unsqueeze(1).to_broadcast([128, NH, 128]), -1.0, ALU.add, ["m2mask"], ["m2neg"],
                        s2=(30000.0 if d_ == 0 else -30000.0), op1=ALU.mult)
                self.cp("dve", negb[d_][:], negm[d_][:], ["m2neg"], ["m2neg"])
                self.cp("dve", trib[d_][:], tri[d_][:], ["m2tri"], ["m2tri"])
            P.dma("sp", dtraw[:].rearrange("p t h -> p (t h)"), S["m2_dt"], w=["m2dtraw"])
            b34 = lambda a: a.unsqueeze(1).to_broadcast([128, NTILE, NH])
            for dr in range(2):
                DD = L + str(dr)
                for i, nm in enumerate(("m2_dtb", "m2_alog")):
                    o = self.roff[nm + DD]
                    P.dma("sp", hp[:, i, :], self.din["prow"][:, o:o + NH], w=["m2hp"])
                o = self.roff["m2_d" + DD]
                P.dma("sp", Dt[:], self.din["prow"][:, o:o + 256], w=["m2Dt"])
                self.act(hp[:, 2, :], hp[:, 1, :], AF.Exp, ["m2hp"], ["m2hp"])
                self.ts("dve", hp[:, 2, :], hp[:, 2, :], -1.0, ALU.mult, ["m2hp"], ["m2hp"])
                self.tt("dve", dt[:], dtraw[:], b34(hp[:, 0, :]), ALU.add, ["m2dtraw", "m2hp"], ["m2dt"])
                self.act(dt[:], dt[:], AF.Exp, ["m2dt"], ["m2dt"])
                self.act(dt[:], dt[:], AF.Ln, ["m2dt"], ["m2dt"], bias=self.epsc[:, 1:2])
                self.tt("dve", dtA[:], dt[:], b34(hp[:, 2, :]), ALU.mult, ["m2dt", "m2hp"], ["m2dtA"])
                self.cp("dve", dtAhl[:, :, 0, :], dtA[:], ["m2dtA"], ["m2dtAhl"])
                self.tt("dve", dtmp[:], dtA[:], dtAhl[:, :, 0, :], ALU.subtract, ["m2dtA", "m2dtAhl"], ["m2dtmp"])
                self.cp("dve", dtAhl[:, :, 1, :], dtmp[:], ["m2dtmp"], ["m2dtAhl"])
                dtA2 = dtA[:].rearrange("p t h -> p (t h)")
                psq, pqk = self.psum()
                self.mm(psq[:, 0:NTILE * NH], tri[0][:], dtA2, True, True, ["m2tri", "m2dtA"], [pqk])
                pst, ptk = self.psum()
                self.mm(pst[:, 0:NTILE * NH], self.ones_f[:], dtA2, True, True, ["const", "m2dtA"], [ptk])
                Q2 = Qt[:].rearrange("p t h -> p (t h)")
                T2 = Qtot[:].rearrange("p t h -> p (t h)")
                self.cp("dve", T2, pst[:, 0:NTILE * NH], [ptk], ["m2Qtot"])
                if dr == 0:
                    self.cp("dve", Q2, psq[:, 0:NTILE * NH], [pqk], ["m2Q"])
                    self.act(Eoff[:], Qt[:], AF.Exp, ["m2Q"], ["m2Eoff"])
                    self.tt("dve", decs[:], Qtot[:], Qt[:], ALU.subtract, ["m2Q", "m2Qtot"], ["m2decs"])
                    self.act(decs[:], decs[:], AF.Exp, ["m2decs"], ["m2decs"])
                else:
                    self.tt("dve", Q2, psq[:, 0:NTILE * NH], dtA2, ALU.subtract, [pqk, "m2dtA"], ["m2Q"])
                    self.tt("dve", Eoff[:], Qtot[:], Qt[:], ALU.subtract, ["m2Q", "m2Qtot"], ["m2Eoff"])
                    self.act(Eoff[:], Eoff[:], AF.Exp, ["m2Eoff"], ["m2Eoff"])
                    self.act(decs[:], Qt[:], AF.Exp, ["m2Q"], ["m2decs"])
                self.act(cdec[:], Qtot[:], AF.Exp, ["m2Qtot"], ["m2cdec"])
                self.ts("dve", nQ[:], Qt[:], (-1.0 if dr == 0 else 1.0), ALU.mult, ["m2Q"], ["m2Q"])
                for ai, (r0, dstt, ck) in enumerate(((0, xstok, 0), (128, xstok, 1), (256, Btok, None), (384, None, None))):
                    P.dma("sp", stage[:], S["m2_x"][dr, r0:r0 + 128, :], w=["m2stage"])
                    if r0 == 256:
                        self.cp("pool", BTb[:], stage[:], ["m2stage"], ["m2BT"])
                    if r0 == 384:
                        self.cp("pool", CTb[:], stage[:], ["m2stage"], ["m2CT"])
                        continue
                    for g0 in range(0, NTILE, 4):
                        ng = min(4, NTILE - g0)
                        ps, pk = self.psum()
                        for i in range(ng):
                            self.tr(ps[:, i * 128:(i + 1) * 128], stage[:, (g0 + i) * 128:(g0 + i + 1) * 128], ["m2stage"], [pk])
                        src = ps[:, 0:ng * 128].rearrange("p (t c) -> p t c", c=128)
                        if dstt is xstok:
                            self.cp("act" if (g0 // 4) % 2 else "dve", xstok[:, g0:g0 + ng, ck * 128:(ck + 1) * 128], src, [pk], ["m2xstok"])
                        else:
                            self.cp("act" if (g0 // 4) % 2 else "dve", Btok[:, g0:g0 + ng, :], src, [pk], ["m2Btok"])
                xs4 = xstok[:].rearrange("p t (h q) -> p t h q", q=64)
                b64 = lambda a: a.unsqueeze(3).to_broadcast([128, NTILE, NH, 64])
                self.tt("dve", Xtok[:].rearrange("p t (h q) -> p t h q", q=64), xs4, b64(dt[:]), ALU.mult,
                        ["m2xstok", "m2dt"], ["m2Xtok"])
                self.tt("dve", decs[:], decs[:], dt[:], ALU.mult, ["m2decs", "m2dt"], ["m2decs"])
                self.tt("pool", Xdtok[:].rearrange("p t (h q) -> p t h q", q=64), xs4, b64(decs[:]), ALU.mult,
                        ["m2xstok", "m2decs"], ["m2Xdtok"])
                Db = Dt[:].unsqueeze(1).to_broadcast([128, NTILE, 256])
                if dr == 0:
                    self.tt("dve", ytok[:], xstok[:], Db, ALU.mult, ["m2xstok", "m2Dt"], ["m2ytok"])
                else:
                    self.tt("dve", xstok[:], xstok[:], Db, ALU.mult, ["m2xstok", "m2Dt"], ["m2xstok"])
                    self.tt("pool", ytok[:], ytok[:], xstok[:], ALU.add, ["m2xstok", "m2ytok"], ["m2ytok"])
                self.memset("dve", stf[:], 0.0, ["m2stf"])
                for i in range(2):
                    self.memset("pool", stbs[i][:], 0.0, ["m2stb%d" % i])
                order = list(range(NTILE)) if dr == 0 else [1, 0] + list(range(NTILE - 1, 1, -1))
                def front_a(ci, j):
                    tk = slice(j * 128, (j + 1) * 128)
                    Rr, rrk = Rrs[ci % 3], "m2Rr%d" % (ci % 3)
                    self.tt("dve", Rr[:].rearrange("p (a h) t -> p a h t", a=2),
                            trib[dr][:].unsqueeze(1).unsqueeze(1).to_broadcast([128, 2, NH, 128]),
                            dtAhl[:, j, :, :].unsqueeze(3).to_broadcast([128, 2, NH, 128]), ALU.mult, ["m2tri", "m2dtAhl"], [rrk])
                    psG, pgk = self.psum()
                    self.mm(psG[:, 0:128], BTb[:, tk], CTb[:, tk], True, True, ["m2BT", "m2CT"], [pgk])
                    psL, plk = self.psum()
                    R2 = Rr[:].rearrange("p g t -> p (g t)")
                    self.mm(psL[:, :], self.ones_b[:], R2[:, 0:512], True, False, ["const", rrk], [plk])
                    self.mm(psL[:, :], self.ones_b[:], R2[:, 512:1024], False, False, ["const", rrk], [plk])
                    self.mm(psL[:, :], self.ident_b[:], negb[dr][:].rearrange("p h t -> p (h t)"), False, True, ["const", "m2neg"], [plk])
                    return psG, pgk, psL, plk

                def front_b(ci, j, fa):
                    psG, pgk, psL, plk = fa
                    mt, mk = MT[ci % 3], "m2MT%d" % (ci % 3)
                    Ex, exk = Exs[ci % 3], "m2Ex%d" % (ci % 3)
                    for h in range(NH):
                        self.act(Ex[:, h, :], psL[:, h * 128:(h + 1) * 128], AF.Exp, [plk, "m2Q"], [exk],
                                 scale=(1.0 if dr == 0 else -1.0), bias=nQ[:, j, h:h + 1])
                    self.tt("dve", mt[:], Ex[:], psG[:, 0:128].unsqueeze(1).to_broadcast([128, NH, 128]), ALU.mult,
                            [exk, pgk], [mk])
                    return mt, mk

                def back(ci, j, fr):
                    mt, mk = fr
                    tk = slice(j * 128, (j + 1) * 128)
                    tmp, tmk = tmps[ci % 3], "m2tmp%d" % (ci % 3)
                    st_prev, sk_prev = stbs[(ci + 1) % 2], "m2stb%d" % ((ci + 1) % 2)
                    st_new, sk_new = stbs[ci % 2], "m2stb%d" % (ci % 2)
                    psS, psk = self.psum()
                    self.mm(psS[:, 0:256], Btok[:, j, :], Xdtok[:, j, :], True, True, ["m2Btok", "m2Xdtok"], [psk])
                    psO, pok = self.psum()
                    self.mm(psO[:, 0:256], CTb[:, tk], st_prev[:], True, True, ["m2CT", sk_prev], [pok])
                    for h in range(NH):
                        self.mm(psO[:, 256 + h * 64:256 + (h + 1) * 64], mt[:, h, :], Xtok[:, j, h * 64:(h + 1) * 64], True, True,
                                [mk, "m2Xtok"], [pok])
                    self.tt("pool", stf[:].rearrange("p (h q) -> p h q", q=64), stf[:].rearrange("p (h q) -> p h q", q=64),
                            cdec[:, j, :].unsqueeze(2).to_broadcast([128, NH, 64]), ALU.mult, ["m2stf", "m2cdec"], ["m2stf"])
                    self.tt("dve", stf[:], psS[:, 0:256], stf[:], ALU.add, [psk, "m2stf"], ["m2stf"])
                    self.cp("act", st_new[:], stf[:], ["m2stf"], [sk_new])
                    yk = ("m2ytok", j)
                    self.tt("dve", tmp[:].rearrange("p (h q) -> p h q", q=64), psO[:, 0:256].rearrange("p (h q) -> p h q", q=64),
                            Eoff[:, j, :].unsqueeze(2).to_broadcast([128, NH, 64]), ALU.mult, [pok, "m2Eoff"], [tmk])
                    self.tt("dve", ytok[:, j, :], psO[:, 256:512], ytok[:, j, :], ALU.add, [pok, "m2ytok", yk], [yk])
                    self.tt("pool", ytok[:, j, :], ytok[:, j, :], tmp[:], ALU.add, [tmk, yk], [yk])

                nO = len(order)
                fas = {}
                frs = {}
                for it in range(nO + 2):
                    if it < nO:
                        fas[it] = front_a(it, order[it])
                    if 0 <= it - 1 < nO:
                        frs[it - 1] = front_b(it - 1, order[it - 1], fas.pop(it - 1))
                    if 0 <= it - 2 < nO:
                        back(it - 2, order[it - 2], frs.pop(it - 2))
                self.P.op("pool", lambda e: e.memset(tmp[:, 0:1], 0.0), [("m2ytok", j) for j in range(NTILE)] + ["m2tmp0"], ["m2ytok", "m2tmp0"])
            P.flush()
        with contextlib.ExitStack() as st:
            yT = [self.sb(st, "m2yT", [128, NT]) for _ in range(2)]
            gt = self.sb(st, "m2gt", [128, NT])
            sq = [self.sb(st, "m2sq", [128, NT], BF16) for _ in range(2)]
            rs = self.sb(st, "m2rs", [128, NT])
            yb = self.sb(st, "m2yb", [128, NT], BF16)
            for cc in range(2):
                yk = "m2yT%d" % cc
                for g0 in [0] + list(range(2, NTILE, 4)):
                    ng = 2 if g0 == 0 else 4
                    ps, pk = self.psum()
                    for i in range(ng):
                        self.tr(ps[:, i * 128:(i + 1) * 128], ytok[:, g0 + i, cc * 128:(cc + 1) * 128], ["m2ytok"], [pk])
                    if g0 == 0:
                        self.cp("act", yT[cc][:, 0:CTX], ps[:, 0:256], [pk], [yk])
                    else:
                        gi = (g0 - 2) // 4
                        dst = yT[cc][:, CTX:NT].rearrange("p (r w) -> p w r", w=64)[:, 8 * gi:8 * gi + 8, :]
                        self.cp("act" if gi % 2 else "dve", dst, ps[:, 0:512].rearrange("p (w r) -> p w r", r=64), [pk], [yk])
                P.dma("sp", gt[:], S["m2_g"][cc * 128:(cc + 1) * 128, :], w=["m2gt"])
                self.tt("dve", yT[cc][:], yT[cc][:], gt[:], ALU.mult, [yk, "m2gt"], [yk])
                self.act(sq[cc][:], yT[cc][:], AF.Square, [yk], ["m2sq%d" % cc])
            for i in range(9):
                t0 = i * 512
                n = min(512, NT - t0)
                ps, pk = self.psum()
                for cc in range(2):
                    self.mm(ps[:, :n], self.ones_b[:], sq[cc][:, t0:t0 + n], cc == 0, cc == 1, ["const", "m2sq%d" % cc], [pk])
                self.act(rs[:, t0:t0 + n], ps[:, :n], AF.Ln, [pk], ["m2rs"], scale=1.0 / 256, bias=self.epsc[:, 0:1])
            self.act(rs[:], rs[:], AF.Exp, ["m2rs"], ["m2rs"], scale=-0.5)
            for cc in range(2):
                self.stt(yT[cc][:], yT[cc][:], self.pc("m2_nw" + L, cc, 1), rs[:], ALU.mult, ALU.mult,
                         ["m2yT%d" % cc, "m2rs", "pcol"], ["m2yT%d" % cc])
                self.cp("pool", yb[:], yT[cc][:], ["m2yT%d" % cc], ["m2yb"])
                P.dma(STQ, S["ybuf"][3 * CH + cc * 128:3 * CH + (cc + 1) * 128, :], yb[:], r=["m2yb"], w=["ybuf"])
            P.flush()


Kern.phase_m2 = _phase_m2


MAGIC = 12582912.0


def _phase_hy(self, l, n, tag, tok0):
    P = self.P
    S = self.scr
    L = str(l)
    NB = n // 128
    tabC = self.din["tabC_" + tag]
    tabS = self.din["tabS_" + tag]
    khat = S["khat_" + tag]
    FG = min(8, NB)
    FGK = min(4, NB)
    TBG = min(4, NB)
    TW = min(2048, n)
    CW = min(512, n)
    CQ = CW
    NQ = TW // CW
    with contextlib.ExitStack() as st0:
        HS = self.sb(st0, "hyHS", [128, NB, 512], BF16)
        HD = self.sb(st0, "hyHD", [128, NB, 512], BF16)
        invn = self.sb(st0, "hyinvn", [128, 512])
        NTB = 16 if False else 8
        tabb = [self.sb(st0, "hytab", [128, TBG * CQ], BF16) for _ in range(NTB)]
        tcnt = [0]

        def tab_load(tab, lg, fq):
            i = tcnt[0] % NTB; tcnt[0] += 1
            tk = "hytab%d" % i
            dst = tabb[i][:].rearrange("p (j c) -> p j c", c=CQ)
            P.dma("sp", dst, tab[lg, fq], w=[tk])
            return dst, tk

        with contextlib.ExitStack() as st:
            zT = self.sb(st, "hyzT", [HY_EMB, n + 1])
            h1 = self.sb(st, "hyh1", [64, n + 1])
            h2 = self.sb(st, "hyh2", [64, n + 1])
            w1 = self.sb(st, "hyw1", [HY_EMB, 64])
            w2 = self.sb(st, "hyw2", [64, 64])
            w3 = self.sb(st, "hyw3", [64, 2, 512])
            frb = self.sb(st, "hyfrb", [64, 2])
            delta = self.sb(st, "hydelta", [128, 256])
            arg = self.sb(st, "hyarg", [64, 512])
            tq = self.sb(st, "hytq", [64, 512])
            NR = 3
            decs = [[self.sb(st, "hydec", [128, 256]) for _ in range(2)] for _ in range(NR)]
            hfbs = [[self.sb(st, "hyhfb", [128, 512]) for _ in range(2)] for _ in range(NR)]
            abs_ = [[self.sb(st, "hyab", [128, 512]) for _ in range(2)] for _ in range(NR)]
            abbs = [self.sb(st, "hyabb", [128, 512], BF16) for _ in range(NR)]
            P.dma("sp", zT[:], self.din["z_" + tag], w=["hyzT"])
            P.dma("sp", w1[:], self.din["hy_w1"][l], w=["hyw"])
            P.dma("sp", w2[:], self.din["hy_w2"][l], w=["hyw"])
            P.dma("sp", w3[:], self.din["hy_w3"][l], w=["hyw"])
            o = self.roff["hy_delta"]
            P.dma("sp", delta[:], self.din["prow"][:, o:o + 256], w=["hydelta"])
            fr = self.pc("hy_fr" + L)[0:64, :]
            self.tt("dve", frb[:, 0:1], self.pc("hy_b1" + L)[0:64, :], fr, ALU.mult, ["pcol"], ["hyfrb"])
            self.tt("dve", frb[:, 1:2], self.pc("hy_b2" + L)[0:64, :], fr, ALU.mult, ["pcol"], ["hyfrb"])
            for li, (wt, kdim, src, dst, dk) in enumerate(((w1, HY_EMB, zT, h1, "hyh1"), (w2, 64, h1, h2, "hyh2"))):
                sk = "hyzT" if li == 0 else "hyh1"
                for c0 in range(0, n, 512):
                    m = min(512, n - c0)
                    ps, pk = self.psum()
                    self.mm(ps[0:64, :m], wt[0:kdim, :], src[0:kdim, c0:c0 + m], True, True, ["hyw", sk], [pk])
                    self.ts("dve", arg[:, :m], ps[0:64, :m], fr, ALU.mult, [pk, "pcol", "hyfrb"], ["hyarg"],
                            s2=frb[:, li:li + 1], op1=ALU.add)
                    self.ts("dve", tq[:, :m], arg[:, :m], 1.0 / (2 * PI), ALU.mult, ["hyarg"], ["hytq"], s2=MAGIC, op1=ALU.add)
                    self.ts("dve", tq[:, :m], tq[:, :m], -MAGIC, ALU.add, ["hytq"], ["hytq"], s2=-2 * PI, op1=ALU.mult)
                    self.tt("dve", arg[:, :m], arg[:, :m], tq[:, :m], ALU.add, ["hyarg", "hytq"], ["hyarg"])
                    self.ts("dve", arg[:, :m], arg[:, :m], 3.141592, ALU.min, ["hyarg"], ["hyarg"], s2=-3.141592, op1=ALU.max)
                    self.act(dst[:, c0:c0 + m], arg[:, :m], AF.Sin, ["hyarg"], [dk])
            self.memset("dve", h2[:, n:n + 1], 0.0, ["hyh2"])
            psN, pnk = self.ps[7], "ps7"
            self.ps_rot = [0, 1, 2, 3, 4, 5, 6]
            for lb in range(NB):
                r3 = lb % NR
                dec, hfb, ab, abb = decs[r3], hfbs[r3], abs_[r3], abbs[r3]
                sf = str(r3)
                pF, pfk = self.psum()
                pB, pbk = self.psum()
                self.mm(pF[:, :], h2[:, lb * 128:lb * 128 + 128], w3[:, 0, :], True, True, ["hyh2", "hyw"], [pfk])
                self.mm(pB[:, :], h2[:, lb * 128 + 1:lb * 128 + 129], w3[:, 1, :], True, True, ["hyh2", "hyw"], [pbk])
                self.act(dec[0][:], delta[:], AF.Exp, ["hydelta", "pcol"], ["hydec0" + sf], scale=self.pc("negt_f" + tag, lb, 1))
                self.act(dec[1][:], delta[:], AF.Exp, ["hydelta", "pcol"], ["hydec1" + sf], scale=self.pc("negt_b" + tag, lb, 1))
                v3 = lambda a: a.rearrange("p (o c) -> p o c", c=256)
                bo = lambda a: a.unsqueeze(1).to_broadcast([128, 2, 256])
                self.tt("dve", v3(hfb[0][:]), v3(pF[:, :]), bo(dec[0][:]), ALU.mult, [pfk, "hydec0" + sf], ["hyhf" + sf])
                self.tt("dve", v3(hfb[1][:]), v3(pB[:, :]), bo(dec[1][:]), ALU.mult, [pbk, "hydec1" + sf], ["hyhb" + sf])
                self.tt("pool", HS[:, lb, :], hfb[0][:], hfb[1][:], ALU.add, ["hyhf" + sf, "hyhb" + sf], [("hyHS", lb)])
                self.tt("dve", HD[:, lb, :], hfb[0][:], hfb[1][:], ALU.subtract, ["hyhf" + sf, "hyhb" + sf], [("hyHS", lb)])
                self.act(ab[0][:], hfb[0][:], AF.Abs, ["hyhf" + sf], ["hyab0" + sf])
                self.act(ab[1][:], hfb[1][:], AF.Abs, ["hyhb" + sf], ["hyab1" + sf])
                self.tt("pool", abb[:], ab[0][:], ab[1][:], ALU.add, ["hyab0" + sf, "hyab1" + sf], ["hyabb" + sf])
                self.mm(psN[:, :], self.ones_b[:], abb[:], lb == 0, lb == NB - 1, ["const", "hyabb" + sf], [pnk])
            self.ts("dve", invn[:], psN[:, :], HY_EPS_, ALU.add, [pnk], ["hyinvn"])
            self.P.op("dve", lambda e: e.reciprocal(out=invn[:], in_=invn[:]), ["hyinvn"], ["hyinvn"])
            self.ts("dve", invn[:], invn[:], 1.0 / n, ALU.mult, ["hyinvn"], ["hyinvn"])
            P.flush()
        self.ps_rot = list(range(8))
        with contextlib.ExitStack() as st:
            ka = [self.sb(st, "hyka", [128, 512]) for _ in range(2)]
            kb = [self.sb(st, "hykb", [128, 512]) for _ in range(2)]
            kr = [self.sb(st, "hykr", [128, 512]) for _ in range(2)]
            ki = [self.sb(st, "hyki", [128, 512]) for _ in range(2)]
            it = 0
            for fg in range(NB // FGK):
                for lg in range(NB // TBG):
                    Ct, ck = tab_load(tabC, lg, fg)
                    St, sk = tab_load(tabS, lg, fg)
                    for j in range(TBG):
                        lb = lg * TBG + j
                        for fb in range(FGK):
                            self.mm(self.ps[fb][:, :], Ct[:, j, fb * 128:(fb + 1) * 128], HS[:, lb, :], lb == 0, lb == NB - 1,
                                    [ck, ("hyHS", lb)], ["ps%d" % fb])
                            self.mm(self.ps[4 + fb][:, :], St[:, j, fb * 128:(fb + 1) * 128], HD[:, lb, :], lb == 0, lb == NB - 1,
                                    [sk, ("hyHS", lb)], ["ps%d" % (4 + fb)])
                for fb in range(FGK):
                    F = fg * FGK + fb
                    i2 = it % 2; it += 1
                    ch = self.pc("chalf" + tag, F, 1)
                    sh = self.pc("shalf" + tag, F, 1)
                    self.tt("dve", ka[i2][:], self.ps[fb][:, :], invn[:], ALU.mult, ["ps%d" % fb, "hyinvn"], ["hyka%d" % i2])
                    self.tt("dve", kb[i2][:], self.ps[4 + fb][:, :], invn[:], ALU.mult, ["ps%d" % (4 + fb), "hyinvn"], ["hykb%d" % i2])
                    self.act(kr[i2][:], ka[i2][:], AF.Copy, ["hyka%d" % i2, "pcol"], ["hykr%d" % i2], scale=ch)
                    self.stt(kr[i2][:], kb[i2][:], sh, kr[i2][:], ALU.mult, ALU.add, ["hykb%d" % i2, "hykr%d" % i2, "pcol"], ["hykr%d" % i2])
                    self.act(ki[i2][:], kb[i2][:], AF.Copy, ["hykb%d" % i2, "pcol"], ["hyki%d" % i2], scale=ch)
                    self.stt(ki[i2][:], ka[i2][:], sh, ki[i2][:], ALU.mult, ALU.subtract, ["hyka%d" % i2, "hyki%d" % i2, "pcol"], ["hyki%d" % i2])
                    P.dma(STQ, khat[F, 0], kr[i2][:], r=["hykr%d" % i2], w=["khat"])
                    P.dma(STQ, khat[F, 1], ki[i2][:], r=["hyki%d" % i2], w=["khat"])
            P.flush()
    with contextlib.ExitStack() as st:
        NTB = 16 if True else 8
        tabb = [self.sb(st, "hytab", [128, TBG * CQ], BF16) for _ in range(NTB)]
        tcnt = [0]

        def tab_load(tab, lg, fq):
            i = tcnt[0] % NTB; tcnt[0] += 1
            tk = "hytab%d" % i
            dst = tabb[i][:].rearrange("p (j c) -> p j c", c=CQ)
            P.dma("sp", dst, tab[lg, fq], w=[tk])
            return dst, tk
        zT = self.sb(st, "hyz", [128, 2, n])
        ztok = self.sb(st, "hyztok", [128, NB, 256], BF16)
        Yh = self.sb(st, "hyYh", [128, NB, 2, 256], BF16)
        Kt = [self.sb(st, "hyKt", [128, 2, 256]) for _ in range(3)]
        ta = [self.sb(st, "hyta", [128, 256]) for _ in range(2)]
        tb_ = [self.sb(st, "hytb", [128, 256]) for _ in range(2)]
        xs = [self.sb(st, "hyxs", [128, 512]) for _ in range(4)]
        te = [self.sb(st, "hyte", [128, 512]) for _ in range(2)]
        yo = self.sb(st, "hyyo", [128, n], BF16)
        for cc in range(2):
            P.dma("sp", zT[:, cc, :], S["hy_u"][0, cc * 128:(cc + 1) * 128, tok0:tok0 + n], w=[("hyz", cc)])
        kcnt = 0
        xcnt = 0
        for o in range(2):
            for tbp in range(0, NB, 2):
                ps, pk = self.psum()
                for i in range(2):
                    for cc in range(2):
                        self.tr(ps[:, (i * 2 + cc) * 128:(i * 2 + cc + 1) * 128], zT[:, cc, (tbp + i) * 128:(tbp + i + 1) * 128],
                                [("hyz", cc)], [pk])
                self.cp("act" if (tbp // 2) % 2 else "dve", ztok[:, tbp:tbp + 2, :], ps[:, :].rearrange("p (t c) -> p t c", c=256),
                        [pk], [("hyztok", tbp // 2)])
            TPF = (FG * 128) // CQ
            for fg in range(NB // FG):
                for lg in range(NB // TBG):
                    tl = []
                    for q in range(TPF):
                        tl.append((tab_load(tabC, lg, fg * TPF + q), tab_load(tabS, lg, fg * TPF + q)))
                    for j in range(TBG):
                        tb = lg * TBG + j
                        zk = ("hyztok", tb // 2)
                        for fb in range(FG):
                            (Ct, ck), (St, sk) = tl[(fb * 128) // CQ]
                            cc0 = (fb * 128) % CQ
                            self.mm(self.ps[fb][:, 0:256], Ct[:, j, cc0:cc0 + 128], ztok[:, tb, :], tb == 0, False,
                                    [ck, zk], ["ps%d" % fb])
                            self.mm(self.ps[fb][:, 256:512], St[:, j, cc0:cc0 + 128], ztok[:, tb, :], False, tb == NB - 1,
                                    [sk, zk], ["ps%d" % fb])
                for fb in range(FG):
                    F = fg * FG + fb
                    kt = Kt[kcnt % 3]; kk = "hyKt%d" % (kcnt % 3); kcnt += 1
                    i2 = F % 2
                    P.dma("sp", kt[:], khat[F, :, :, o * 256:(o + 1) * 256].rearrange("r p c -> p r c"), w=[kk])
                    M1 = self.ps[fb][:, 0:256]
                    M2 = self.ps[fb][:, 256:512]
                    pk = "ps%d" % fb
                    self.tt("dve", ta[i2][:], M1, kt[:, 0, :], ALU.mult, [pk, kk], ["hyta%d" % i2])
                    self.tt("dve", tb_[i2][:], M2, kt[:, 1, :], ALU.mult, [pk, kk], ["hytb%d" % i2])
                    self.tt("pool", Yh[:, F, 0, :], ta[i2][:], tb_[i2][:], ALU.add, ["hyta%d" % i2, "hytb%d" % i2], [("hyYh", F)])
                    self.tt("dve", ta[i2][:], M2, kt[:, 0, :], ALU.mult, [pk, kk, ("hyYh", F)], ["hyta%d" % i2])
                    self.tt("dve", tb_[i2][:], M1, kt[:, 1, :], ALU.mult, [pk, kk, ("hyYh", F)], ["hytb%d" % i2])
                    self.tt("pool", Yh[:, F, 1, :], ta[i2][:], tb_[i2][:], ALU.subtract, ["hyta%d" % i2, "hytb%d" % i2], [("hyYh", F)])
            for th in range(n // TW):
                w0 = th * TW
                for lg in range(NB // TBG):
                    tl = [(tab_load(tabC, lg, th * NQ + q), tab_load(tabS, lg, th * NQ + q)) for q in range(NQ)]
                    for j in range(TBG):
                        fb = lg * TBG + j
                        for cc in range(2):
                            for q in range(NQ):
                                bank = cc * NQ + q
                                (Ct, ck), (St, sk) = tl[q]
                                self.mm(self.ps[bank][:, 0:CW], Yh[:, fb, 0, cc * 128:(cc + 1) * 128], Ct[:, j, :],
                                        fb == 0, False, [ck, ("hyYh", fb)], ["ps%d" % bank])
                                self.mm(self.ps[bank][:, 0:CW], Yh[:, fb, 1, cc * 128:(cc + 1) * 128], St[:, j, :],
                                        False, fb == NB - 1, [sk, ("hyYh", fb)], ["ps%d" % bank])
                for cc in range(2):
                    for q in range(NQ):
                        bank = cc * NQ + q
                        t0 = w0 + q * CW
                        x = xs[xcnt % 4]; xk = "hyxs%d" % (xcnt % 4)
                        tt_ = te[xcnt % 2]; tk = "hyte%d" % (xcnt % 2); xcnt += 1
                        P.dma("sp", x[:, :CW], S["hy_u"][1 + o, cc * 128:(cc + 1) * 128, tok0 + t0:tok0 + t0 + CW], w=[xk])
                        self.stt(tt_[:, :CW], zT[:, cc, t0:t0 + CW], self.pc("hy_skip" + L, o * 2 + cc, 1), self.ps[bank][:, 0:CW],
                                 ALU.mult, ALU.add, [("hyz", cc), "ps%d" % bank, "pcol"], [tk])
                        self.tt("pool", zT[:, cc, t0:t0 + CW], tt_[:, :CW], x[:, :CW], ALU.mult, [tk, xk], [("hyz", cc)])
        for cc in range(2):
            for q in range(n // CW):
                t0 = q * CW
                x = xs[xcnt % 4]; xk = "hyxs%d" % (xcnt % 4); xcnt += 1
                P.dma("sp", x[:, :CW], S["hy_g"][cc * 128:(cc + 1) * 128, tok0 + t0:tok0 + t0 + CW], w=[xk])
                self.tt("dve", yo[:, t0:t0 + CW], zT[:, cc, t0:t0 + CW], x[:, :CW], ALU.mult, [("hyz", cc), xk], ["hyyo"])
            P.dma(STQ, S["ybuf"][cc * 128:(cc + 1) * 128, tok0:tok0 + n], yo[:], r=["hyyo"], w=["ybuf"])
        P.flush()


HY_EPS_ = 1e-6
Kern.phase_hy1 = _phase_hy


def _phase_hy_all(self, l, need_ctx=True):
    self.phase_hy1(l, SEQ, "L", CTX)
    if need_ctx:
        self.phase_hy1(l, CTX, "C", 0)


Kern.phase_hy = _phase_hy_all


def _phase_out(self, l, yf, xsrc, tiles, xdst=None, final_dst=None, lat_only=False):
    P = self.P
    S = self.scr
    w_out = self.din["w_out"]
    with contextlib.ExitStack() as st:
        wst = [self.sb(st, "wost", [128, 1024]) for _ in range(2)]
        wob = self.sb(st, "wob", [128, 16, 1024], BF16)
        gb = self.sb(st, "ogb", [128, 2, 1024])
        fw = self.sb(st, "ofw", [128, 1024])
        yt = [self.sb(st, "oyt", [128, 16, 512], BF16) for _ in range(2)]
        xo = [self.sb(st, "oxo", [128, 1024]) for _ in range(5)]
        tm = [self.sb(st, "otm", [128, 1024]) for _ in range(4)]
        junks = [self.sb(st, "ojunk", [128, 1024]) for _ in range(2)]
        ssqs = self.sb(st, "ossq", [128, 8])
        for kc in range(16):
            if self.mode == "ALL":
                j, hh = kc // 2, kc % 2
                br, cc = j // 2, j % 2
            else:
                hh, br, cc = kc // 8, (kc % 8) // 2, kc % 2
            r0 = br * 512 + hh * 256 + cc * 128
            P.dma("sp", wst[kc % 2][:], w_out[l, r0:r0 + 128, :], w=["wost%d" % (kc % 2)])
            self.cp("act" if kc % 2 else "dve", wob[:, kc, :], wst[kc % 2][:], ["wost%d" % (kc % 2)], ["wob"])
        for s in range(2):
            P.dma("sp", gb[:, s, :], S["gbc"][l, s], w=["ogb"])
        o = self.roff["final_w"]
        P.dma("sp", fw[:], self.din["prow"][:, o:o + 1024], w=["ofw"])
        yfv = yf.rearrange("(kc p) t -> p kc t", p=128)
        chunks = {}
        for ti in tiles:
            t0 = ti * 128
            if lat_only:
                ck = t0 // 512
            else:
                ck = -1 if t0 < CTX else (t0 - CTX) // 512
            chunks.setdefault(ck, []).append(ti)
        it = 0
        xi = 0
        for ck, tl in chunks.items():
            if lat_only:
                c0, cw = ck * 512, 512
            else:
                c0 = 0 if ck < 0 else CTX + ck * 512
                cw = CTX if ck < 0 else 512
            ybuf = yt[it % 2]; yk = "oyt%d" % (it % 2); it += 1
            P.dma("sp", ybuf[:, :, :cw], yfv[:, :, c0:c0 + cw], w=[yk])
            for ti in tl:
                off = ti * 128 - c0
                s = 1 if (ti < 2 and not lat_only) else 0
                pa, pak = self.psum()
                pb, pbk = self.psum()
                for kc in range(16):
                    self.mm(pa[:, :], ybuf[:, kc, off:off + 128], wob[:, kc, 0:512], kc == 0, kc == 15, [yk, "wob"], [pak])
                    self.mm(pb[:, :], ybuf[:, kc, off:off + 128], wob[:, kc, 512:1024], kc == 0, kc == 15, [yk, "wob"], [pbk])
                x = xo[xi % 5]; xk = "oxo%d" % (xi % 5)
                t = tm[xi % 4]; tk = "otm%d" % (xi % 4)
                junk = junks[xi % 2]; jk = "ojunk%d" % (xi % 2)
                ssq = ssqs[:, 2 * (xi % 4):2 * (xi % 4) + 2]; sqk = "ossq%d" % (xi % 4); xi += 1
                P.dma("sp", x[:], xsrc(ti), w=[xk])
                self.tt("dve", t[:, 0:512], pa[:, :], gb[:, s, 0:512], ALU.mult, [pak, "ogb"], [tk])
                self.tt("dve", t[:, 512:1024], pb[:, :], gb[:, s, 512:1024], ALU.mult, [pbk, "ogb"], [tk])
                self.tt("dve", x[:], x[:], t[:], ALU.add, [xk, tk], [xk])
                if xdst is not None:
                    for dst in xdst(ti):
                        P.dma(STQ, dst, x[:], r=[xk], w=["xres"])
                if final_dst is not None:
                    self.P.op("act", lambda e, x=x, junk=junk, ssq=ssq: e.activation(out=junk[:], in_=x[:], func=AF.Square,
                                                                                   accum_out=ssq[:, 0:1]), [xk], [jk, sqk])
                    self.act(ssq[:, 1:2], ssq[:, 0:1], AF.Ln, [sqk], [sqk], scale=1.0 / D, bias=self.epsc[:, 0:1])
                    self.act(ssq[:, 1:2], ssq[:, 1:2], AF.Exp, [sqk], [sqk], scale=-0.5)
                    self.stt(t[:], x[:], ssq[:, 1:2], fw[:], ALU.mult, ALU.mult, [xk, sqk, "ofw", tk], [tk])
                    P.dma(STQ, final_dst(ti), t[:], r=[tk], w=["final"])
        P.flush()


Kern.phase_out = _phase_out


_PROG_CACHE = {}


def _launch(mode, in_maps, poff, roff, npc, npr):
    nc = build_program(mode, poff, roff, npc, npr)
    res = run_bass_kernel_spmd(nc, in_maps, core_ids=list(range(8)))
    return res.results


MIX_KEYS = ("w_in", "rg_bd", "hy_w1", "hy_w2", "hy_w3", "c_tri_incl", "c_tri_excl", "c_hgmask_f", "c_hgmask_b", "c_hgmask64_f", "c_hgmask64_b",
            "c_m2mask_f", "c_m2mask_b", "tabC_L", "tabS_L", "tabC_C", "tabS_C", "z_L", "z_C")
BASE_KEYS = ("w_mod", "w_out", "pcol", "prow", "c_ident", "c_ones", "c_sel")


def kernel(**inputs):
    consts = const_inputs()
    preps = []
    for b in range(4):
        for h in range(2):
            d, poff, roff = host_prep(inputs, b, h)
            d.update(consts)
            preps.append(d)
    npc = preps[0]["pcol"].shape[1]
    npr = preps[0]["prow"].shape[1]
    maps = [{k: d[k] for k in BASE_KEYS + MIX_KEYS + ("x", "ctx")} for d in preps]
    res = _launch("ALL", maps, poff, roff, npc, npr)
    out = np.empty((4, SEQ, D), np.float32)
    for b in range(4):
        out[b] = res[2 * b]["out"]
    return out
```
